# Optimizing a Trainium2 kernel written in Bass

```python
import math
import jax, jax.numpy as jnp
from jax import lax
import numpy as np

D_MODEL = 1024
BATCH = 1
SEQ = 16384
DEPTH = 1

MIX_WIDTH = D_MODEL
ATTN_WIDTH = MIX_WIDTH // 2
CONV_WIDTH = MIX_WIDTH - ATTN_WIDTH
DIFF_HEAD_DIM = 64
V_HEAD_DIM = 2 * DIFF_HEAD_DIM
N_ATTN_HEADS = ATTN_WIDTH // V_HEAD_DIM
CONV_KERNEL = 31
FFN_HIDDEN = ((-(-8 * D_MODEL // 3) + 255) // 256) * 256
PLE_DIM = 256
Q_BLOCK = 128
IN_COLS = 3 * ATTN_WIDTH + 2 * CONV_WIDTH
EPS = 1e-6

kernel_name = "hybrid_diffattn_conformer_conv_block"


def alibi_slopes(n_heads):
    return np.array([2.0 ** (-8.0 * (h + 1) / n_heads) for h in range(n_heads)], dtype=np.float32)


def rmsnorm(x, gain):
    x32 = x.astype(jnp.float32)
    y = x32 * lax.rsqrt(jnp.mean(x32 * x32, axis=-1, keepdims=True) + EPS)
    return (y * gain.astype(jnp.float32)).astype(x.dtype)


def layernorm(x, gain, bias):
    x32 = x.astype(jnp.float32)
    mu = jnp.mean(x32, axis=-1, keepdims=True)
    xc = x32 - mu
    y = xc * lax.rsqrt(jnp.mean(xc * xc, axis=-1, keepdims=True) + EPS)
    return (y * gain.astype(jnp.float32) + bias.astype(jnp.float32)).astype(x.dtype)


def diff_attention(q, k, v, lam):
    B, S = q.shape[0], q.shape[1]
    nblk = S // Q_BLOCK
    scale = DIFF_HEAD_DIM ** -0.5
    slopes = jnp.asarray(alibi_slopes(N_ATTN_HEADS))
    key_pos = jnp.arange(S, dtype=jnp.int32)
    q_blocks = jnp.moveaxis(q.reshape(B, nblk, Q_BLOCK, N_ATTN_HEADS, 2, DIFF_HEAD_DIM), 1, 0)
    starts = jnp.arange(nblk, dtype=jnp.int32) * Q_BLOCK

    def block(args):
        q_blk, start = args
        s = jnp.einsum('bqhmd,bkhmd->bhmqk', q_blk, k).astype(jnp.float32) * scale
        q_pos = start + jnp.arange(Q_BLOCK, dtype=jnp.int32)
        dist = jnp.abs(q_pos[:, None] - key_pos[None, :]).astype(jnp.float32)
        s = s - slopes[:, None, None, None] * dist
        a = jax.nn.softmax(s, axis=-1)
        w = a[:, :, 0] - lam * a[:, :, 1]
        return jnp.einsum('bhqk,bkhe->bqhe', w.astype(v.dtype), v)

    o = lax.map(block, (q_blocks, starts))
    return jnp.moveaxis(o, 0, 1).reshape(B, S, N_ATTN_HEADS, V_HEAD_DIM)


def conformer_conv(a, g, conv_w, conv_b, ln_g, ln_b):
    u = a * jax.nn.sigmoid(g)
    rhs = conv_w[:, None, :].astype(u.dtype)
    y = lax.conv_general_dilated(
        u, rhs, window_strides=(1,),
        padding=[(CONV_KERNEL // 2, CONV_KERNEL // 2)],
        dimension_numbers=('NWC', 'WIO', 'NWC'),
        feature_group_count=CONV_WIDTH) + conv_b.astype(u.dtype)
    y = layernorm(y, ln_g, ln_b)
    return jax.nn.silu(y)


def setup_inputs(seed: int = 0) -> dict:
    key = jax.random.key(seed)
    ks = jax.random.split(key, 24)
    f32 = jnp.float32

    def nrm(k, shape, scale):
        return jax.random.normal(k, shape, f32) * scale

    def gain(k, shape):
        return 1.0 + 0.05 * jax.random.normal(k, shape, f32)

    L = DEPTH
    return {
        "x": nrm(ks[0], (BATCH, SEQ, D_MODEL), 1.0),
        "p": nrm(ks[1], (DEPTH, BATCH, SEQ, PLE_DIM), 1.0),
        "attn_norm": gain(ks[2], (L, D_MODEL)),
        "w_in": nrm(ks[3], (L, D_MODEL, IN_COLS), D_MODEL ** -0.5),
        "q_norm": gain(ks[4], (L, DIFF_HEAD_DIM)),
        "k_norm": gain(ks[5], (L, DIFF_HEAD_DIM)),
        "lambda_q1": nrm(ks[6], (L, DIFF_HEAD_DIM), 0.1),
        "lambda_k1": nrm(ks[7], (L, DIFF_HEAD_DIM), 0.1),
        "lambda_q2": nrm(ks[8], (L, DIFF_HEAD_DIM), 0.1),
        "lambda_k2": nrm(ks[9], (L, DIFF_HEAD_DIM), 0.1),
        "head_norm": gain(ks[10], (L, V_HEAD_DIM)),
        "conv_w": nrm(ks[11], (L, CONV_KERNEL, CONV_WIDTH), CONV_KERNEL ** -0.5),
        "conv_b": nrm(ks[12], (L, CONV_WIDTH), 0.02),
        "conv_ln_g": gain(ks[13], (L, CONV_WIDTH)),
        "conv_ln_b": nrm(ks[14], (L, CONV_WIDTH), 0.02),
        "w_out": nrm(ks[15], (L, MIX_WIDTH, D_MODEL), MIX_WIDTH ** -0.5),
        "ffn_norm": gain(ks[16], (L, D_MODEL)),
        "w_gate": nrm(ks[17], (L, D_MODEL, FFN_HIDDEN), D_MODEL ** -0.5),
        "w_up": nrm(ks[18], (L, D_MODEL, FFN_HIDDEN), D_MODEL ** -0.5),
        "w_down": nrm(ks[19], (L, FFN_HIDDEN, D_MODEL), FFN_HIDDEN ** -0.5),
        "ple_norm": gain(ks[20], (L, D_MODEL)),
        "w_ple_gate": nrm(ks[21], (L, D_MODEL, D_MODEL), D_MODEL ** -0.5),
        "w_ple_proj": nrm(ks[22], (L, PLE_DIM, D_MODEL), PLE_DIM ** -0.5),
    }


def reference(x, p, attn_norm, w_in, q_norm, k_norm, lambda_q1, lambda_k1, lambda_q2, lambda_k2,
              head_norm, conv_w, conv_b, conv_ln_g, conv_ln_b, w_out, ffn_norm, w_gate, w_up,
              w_down, ple_norm, w_ple_gate, w_ple_proj):
    B, S = x.shape[0], x.shape[1]
    splits = [ATTN_WIDTH, 2 * ATTN_WIDTH, 3 * ATTN_WIDTH, 3 * ATTN_WIDTH + CONV_WIDTH]
    for i in range(DEPTH):
        h = rmsnorm(x, attn_norm[i])
        proj = h @ w_in[i]
        q, k, v, conv_a, conv_g = jnp.split(proj, splits, axis=-1)

        q = rmsnorm(q.reshape(B, S, N_ATTN_HEADS, 2, DIFF_HEAD_DIM), q_norm[i])
        k = rmsnorm(k.reshape(B, S, N_ATTN_HEADS, 2, DIFF_HEAD_DIM), k_norm[i])
        v = v.reshape(B, S, N_ATTN_HEADS, V_HEAD_DIM)
        lam_init = 0.8 - 0.6 * math.exp(-0.3 * i)
        lam = (jnp.exp(jnp.sum(lambda_q1[i].astype(jnp.float32) * lambda_k1[i].astype(jnp.float32)))
               - jnp.exp(jnp.sum(lambda_q2[i].astype(jnp.float32) * lambda_k2[i].astype(jnp.float32)))
               + lam_init)
        o = diff_attention(q, k, v, lam)
        o = rmsnorm(o, head_norm[i]) * (1.0 - lam_init)
        attn_out = o.reshape(B, S, ATTN_WIDTH)

        conv_out = conformer_conv(conv_a, conv_g, conv_w[i], conv_b[i], conv_ln_g[i], conv_ln_b[i])

        mix = jnp.concatenate([attn_out, conv_out], axis=-1)
        x = x + mix @ w_out[i]

        h = rmsnorm(x, ffn_norm[i])
        x = x + (jax.nn.silu(h @ w_gate[i]) * (h @ w_up[i])) @ w_down[i]

        gate = jax.nn.sigmoid(rmsnorm(x, ple_norm[i]) @ w_ple_gate[i])
        x = x + gate * (p[i] @ w_ple_proj[i])
    return x
```

```python
import numpy as np
import ml_dtypes
from contextlib import ExitStack

import concourse.bass as bass
import concourse.mybir as mybir
from concourse.bass_utils import run_bass_kernel_spmd

F32 = mybir.dt.float32
BF16 = mybir.dt.bfloat16
AF = mybir.ActivationFunctionType
ALU = mybir.AluOpType
AX = mybir.AxisListType

NCORES = 8
D = 1024
NCH = D // 128
NH = 4
DH = 64
FF = 2816
NF = FF // 128
PLE = 256
CK = 31
EPS = 1e-6
SLOPES = [2.0 ** (-8.0 * (h + 1) / NH) for h in range(NH)]
LAM_INIT = 0.8 - 0.6 * 1.0
CHK = 16
NSLOT = 3
NPT = 4
NWS = 14
RSD = 768
WIN_BLOCKS = [5, 18, None, None]


class H:
    __slots__ = ("key", "sem", "val")

    def __init__(self, key, sem, val):
        self.key, self.sem, self.val = key, sem, val


class Eng:
    def __init__(self, name, sem):
        self.name, self.sem = name, sem
        self.count = 0
        self.waited = {}
        self.ops = []

    def _waits(self, deps):
        for d in deps:
            if d is None:
                continue
            if isinstance(d, (list, tuple)):
                self._waits(d)
                continue
            if self.waited.get(d.key, 0) < d.val:
                self.waited[d.key] = d.val
                self.ops.append(("wait", d.sem, d.val))

    def op(self, fn, deps=(), signal=True):
        self._waits(deps)
        if signal:
            self.count += 1
            self.ops.append(("ins", fn, self.sem, 1))
            return H(self.name, self.sem, self.count)
        self.ops.append(("ins", fn, None, 0))
        return None

    def dma(self, out, in_, dsem, deps=(), **kw):
        self._waits(deps)
        dsem.total += 16
        self.ops.append(("ins", (lambda e, o=out, i=in_, k=kw: e.dma_start(out=o, in_=i, **k)), dsem.sem, 16))
        return H(dsem.name, dsem.sem, dsem.total)

    def wait(self, deps):
        self._waits(deps)

    def replay(self, e):
        for rec in self.ops:
            if rec[0] == "wait":
                e.wait_ge(rec[1], rec[2])
            else:
                ins = rec[1](e)
                if rec[2] is not None:
                    ins.then_inc(rec[2], rec[3])


class DSem:
    def __init__(self, name, sem):
        self.name, self.sem, self.total = name, sem, 0


def build_program(NG):
    NT = NG * 512
    S = NCORES * NT
    NGRP = S // 512
    NKB = S // 128
    NCHK = NKB // CHK
    UW = NT + 32

    nc = bass.Bass("TRN2", target_bir_lowering=False)

    def din(name, shape, dt=F32):
        return nc.dram_tensor(name, list(shape), dt, kind="ExternalInput").ap()

    xr = din("xr", [S, D])
    xhalo = din("xhalo", [32, D])
    p_own = din("p_own", [NT, PLE])
    w_in = din("w_in", [D, 2560])
    w_out = din("w_out", [D, D])
    w_gate = din("w_gate", [D, FF])
    w_up = din("w_up", [D, FF])
    w_down = din("w_down", [FF, D])
    w_pg = din("w_ple_gate", [D, D])
    w_pp = din("w_ple_proj", [PLE, D])
    attn_norm = din("attn_norm", [128, NCH])
    ffn_norm = din("ffn_norm", [128, NCH])
    ple_norm = din("ple_norm", [128, NCH])
    q_norm = din("q_norm", [128, 1])
    k_norm = din("k_norm", [128, 1])
    lq1 = din("lambda_q1", [DH]); lk1 = din("lambda_k1", [DH])
    lq2 = din("lambda_q2", [DH]); lk2 = din("lambda_k2", [DH])
    head_norm = din("head_norm", [128, 1])
    conv_w = din("conv_w", [CK, 512])
    conv_b = din("conv_b", [128, 4])
    ln_g = din("conv_ln_g", [128, 4])
    ln_b = din("conv_ln_b", [128, 4])
    c_identb = din("c_identb", [128, 128], BF16)
    c_identf = din("c_identf", [128, 128])
    c_bones = din("c_bones", [128, 128], BF16)
    c_kaug = din("c_kaug", [NH, NG, 4, S], BF16)
    c_qaug = din("c_qaug", [NH, 4, 512], BF16)
    c_bdiag = din("c_bdiag", [NH, 128, 896])

    out_own = nc.dram_tensor("out_own", [NT, D], F32, kind="ExternalOutput").ap()

    kT1 = nc.dram_tensor("kT1", [NH, 64, S], BF16).ap()
    kT2 = nc.dram_tensor("kT2", [NH, 64, S], BF16).ap()
    vdr = nc.dram_tensor("vdr", [NH, 128, NKB, 128], BF16).ap()
    NPIECE = NCH + 3 * NF + NCH + 2
    wsc = nc.dram_tensor("wsc", [NPIECE, 128, 1024], BF16).ap()

    es = ExitStack()
    with es:
        def sb(name, shape, dt=F32, stack=es):
            return stack.enter_context(nc.sbuf_tensor(name, list(shape), dt))

        def ps(name, shape, dt=F32, stack=es):
            return stack.enter_context(nc.psum_tensor(name, list(shape), dt))

        def newsem(name):
            return es.enter_context(nc.semaphore(name))

        PE = Eng("pe", newsem("s_pe"))
        ACT = Eng("act", newsem("s_act"))
        DVE = Eng("dve", newsem("s_dve"))
        POOL = Eng("pool", newsem("s_pool"))
        SP = Eng("sp", newsem("s_sp"))
        _dsn = [0]
        ENGS = (PE, ACT, DVE, POOL, SP)

        def flush():
            with nc.allow_non_contiguous_dma("small param loads"):
                with nc.Block() as block:
                    @block.tensor
                    def _(e):
                        PE.replay(e)

                    @block.scalar
                    def _(e):
                        ACT.replay(e)

                    @block.vector
                    def _(e):
                        DVE.replay(e)

                    @block.gpsimd
                    def _(e):
                        POOL.replay(e)

                    @block.sync
                    def _(e):
                        SP.replay(e)
            for E_ in ENGS:
                E_.ops = []

        def dsem(name):
            _dsn[0] += 1
            return DSem(name, newsem(name))

        def rsqrt_act(out_ap, in_ap, deps):
            h1_ = ACT.op(lambda e: e.activation(out_ap, in_ap, AF.Ln), deps)
            return ACT.op(lambda e: e.activation(out_ap, out_ap, AF.Exp, scale=-0.5), [h1_])

        identb = sb("identb", [128, 128], BF16)
        identf = sb("identf", [128, 128])
        bones = sb("bones", [128, 128], BF16)
        onesb = sb("onesb", [128, 128], BF16)
        onesf = sb("onesf", [128, 128])
        negh = sb("negh", [128, 512])
        gA = sb("gA", [128, NCH]); gFn = sb("gFn", [128, NCH]); gPn = sb("gPn", [128, NCH])
        gq2 = sb("gq2", [128, 1]); gk2 = sb("gk2", [128, 1])
        hn8 = sb("hn8", [128, 1])
        cw_sb = sb("cw_sb", [CK, 512])
        cwT = sb("cwT", [128, 4, 32])
        cb4 = sb("cb4", [128, 4]); lg4 = sb("lg4", [128, 4]); lb4 = sb("lb4", [128, 4])
        lamt = sb("lamt", [128, 4, DH])
        lamw = sb("lamw", [128, 8])
        neglam = sb("neglam", [128, 1])
        conv_o = sb("conv_o", [128, 4, NT], BF16)
        attn_o = sb("attn_o", [128, NH, NT], BF16)

        ds_c = dsem("d_const")
        with nc.allow_non_contiguous_dma("small param loads"):
            hc = []
            hc.append(SP.dma(identb[:], c_identb, ds_c))
            hc.append(SP.dma(identf[:], c_identf, ds_c))
            hc.append(SP.dma(bones[:], c_bones, ds_c))
            hc.append(SP.dma(gA[:], attn_norm, ds_c))
            hc.append(SP.dma(gFn[:], ffn_norm, ds_c))
            hc.append(SP.dma(gPn[:], ple_norm, ds_c))
            hc.append(SP.dma(gq2[:], q_norm, ds_c))
            hc.append(SP.dma(gk2[:], k_norm, ds_c))
            hc.append(SP.dma(hn8[:], head_norm, ds_c))
            hc.append(SP.dma(cw_sb[:], conv_w, ds_c))
            hc.append(SP.dma(cb4[:], conv_b, ds_c))
            hc.append(SP.dma(lg4[:], ln_g, ds_c))
            hc.append(SP.dma(lb4[:], ln_b, ds_c))
            for i, v in enumerate((lq1, lk1, lq2, lk2)):
                hc.append(SP.dma(lamt[:, i, :], v.rearrange("(o d) -> o d", o=1).partition_broadcast(128), ds_c))
        h_const = hc[-1]

        h_m1 = POOL.op(lambda e: e.memset(onesb[:], 1.0))
        h_m2 = POOL.op(lambda e: e.memset(onesf[:], 1.0))
        h_m3 = POOL.op(lambda e: e.memset(negh[:], -0.5))
        h_init = h_m3

        h = DVE.op(lambda e: e.tensor_tensor(lamt[:, 0, :], lamt[:, 0, :], lamt[:, 1, :], op=ALU.mult), [h_const])
        h = DVE.op(lambda e: e.tensor_tensor(lamt[:, 2, :], lamt[:, 2, :], lamt[:, 3, :], op=ALU.mult), [h])
        h = DVE.op(lambda e: e.reduce_sum(lamw[:, 0:1], lamt[:, 0, :], axis=AX.X), [h])
        h = DVE.op(lambda e: e.reduce_sum(lamw[:, 1:2], lamt[:, 2, :], axis=AX.X), [h])
        h = ACT.op(lambda e: e.activation(lamw[:, 2:4], lamw[:, 0:2], AF.Exp), [h])
        h = DVE.op(lambda e: e.tensor_tensor(lamw[:, 4:5], lamw[:, 3:4], lamw[:, 2:3], op=ALU.subtract), [h])
        h = DVE.op(lambda e: e.tensor_scalar(neglam[:], lamw[:, 4:5], -LAM_INIT, None, op0=ALU.add), [h])
        h_hn = DVE.op(lambda e: e.tensor_scalar(hn8[:], hn8[:], 1.0 - LAM_INIT, None, op0=ALU.mult), [h_const])
        h_lam = h

        esAB = ExitStack()
        with esAB:
            Q1 = sb("Q1", [128, NH, NT], BF16, esAB)
            Q2 = sb("Q2", [128, NH, NT], BF16, esAB)
            uT = sb("uT", [128, 4, UW], BF16, esAB)

            ds_qa = dsem("d_qaug")
            hq = []
            for hh in range(NH):
                for g in range(NG):
                    hq.append(SP.dma(Q1[64:68, hh, g * 512:(g + 1) * 512], c_qaug[hh], ds_qa))
                    hq.append(SP.dma(Q2[64:68, hh, g * 512:(g + 1) * 512], c_qaug[hh], ds_qa))
            h_qaug = hq[-1]

            esA = ExitStack()
            with esA:
                w_in_sb = sb("w_in_sb", [128, NCH, 2560], BF16, esA)
                NXT = 4
                xt = [sb(f"xt{i}", [128, D], F32, esA) for i in range(NXT)]
                xb = [sb(f"xb{i}", [128, D], BF16, esA) for i in range(2)]
                junk = sb("junk", [128, D], BF16, esA)
                xT = [sb(f"xT{i}", [128, NCH, 512], BF16, esA) for i in range(2)]
                sq = [sb(f"sq{i}", [128, 512], BF16, esA) for i in range(2)]
                kst = [sb(f"kst{i}", [128, NH, 512], BF16, esA) for i in range(1)]
                vst = [sb(f"vst{i}", [128, NH, 4, 128], BF16, esA) for i in range(1)]
                Rrow = [sb(f"Rrow{i}", [128, 512], F32, esA) for i in range(2)]
                Erow = [sb(f"Erow{i}", [128, 512], F32, esA) for i in range(2)]
                t1 = [sb(f"t1_{i}", [128, 512], F32, esA) for i in range(2)]
                rk = [sb(f"rk{i}", [128, 512], F32, esA) for i in range(2)]
                stq = [sb(f"stq{i}", [128, 3, 4], F32, esA) for i in range(2)]
                q2st = sb("q2st", [128, NH, 512], BF16, esA)
                ds_q2 = dsem("d_q2")
                st_q2free = [None]
                dgR = [sb(f"dgR{i}", [128, 128], F32, esA) for i in range(2)]
                dgE = [sb(f"dgE{i}", [128, 128], F32, esA) for i in range(4)]
                cva = [sb(f"cva{i}", [128, 512], F32, esA) for i in range(1)]
                cvb = [sb(f"cvb{i}", [128, 512], F32, esA) for i in range(1)]

                ptr = [ps(f"ptr{i}", [128, D], BF16, esA) for i in range(2)]
                REp = ps("REp", [128, 512], F32, esA)
                kps = [ps(f"kps{i}", [128, 512], F32, esA) for i in range(2)]
                ssp = ps("ssp", [128, 512], F32, esA)
                vps = [ps(f"vps{i}", [128, 512], F32, esA) for i in range(2)]

                wblocks = [(768, 1024), (1280, 1536), (512, 768), (1024, 1280), (0, 512), (1536, 2048), (2048, 2560)]
                h_wblk = []
                for bi_, (c0_, c1_) in enumerate(wblocks):
                    dsw = dsem(f"d_win{bi_}")
                    h_wblk.append(POOL.dma(w_in_sb[:, :, c0_:c1_], w_in[:, c0_:c1_].rearrange("(c p) n -> p c n", p=128), dsw))

                def hw_cols(a, b_):
                    return [h_wblk[i] for i, (c0_, c1_) in enumerate(wblocks) if c0_ < b_ and a < c1_]
                h_win = None

                ds_cv = dsem("d_cv")
                h_cv = None
                _pl = [("wo", c) for c in range(NCH)]
                for f in range(NF):
                    _pl.append(("wg", f)); _pl.append(("wu", f))
                _pl += [("wd", f) for f in range(NF)] + [("pg", c) for c in range(NCH)] + [("pp", c) for c in range(2)]
                for pi_, (kind, idx) in enumerate(_pl):
                    if kind == "wo":
                        src, dst = w_out[idx * 128:(idx + 1) * 128, :], wsc[pi_]
                    elif kind in ("wg", "wu"):
                        wsrc = w_gate if kind == "wg" else w_up
                        src = wsrc[:, idx * 128:(idx + 1) * 128].rearrange("(c p) n -> p c n", p=128)
                        dst = wsc[pi_].rearrange("p (c n) -> p c n", c=NCH)
                    elif kind == "wd":
                        src, dst = w_down[idx * 128:(idx + 1) * 128, :], wsc[pi_]
                    elif kind == "pg":
                        src, dst = w_pg[idx * 128:(idx + 1) * 128, :], wsc[pi_]
                    else:
                        src, dst = w_pp[idx * 128:(idx + 1) * 128, :], wsc[pi_]
                    h_cv = POOL.dma(dst, src, ds_cv)

                hcw = None
                for cc in range(4):
                    hm = PE.op(lambda e, cc=cc: e.matmul(ssp[:, 0:CK], lhsT=cw_sb[0:CK, cc * 128:(cc + 1) * 128],
                                                         rhs=identf[0:CK, 0:CK], start=True, stop=True),
                               [h_const, hcw])
                    hcw = DVE.op(lambda e, cc=cc: e.tensor_copy(cwT[:, cc, 0:CK], ssp[:, 0:CK]), [hm])
                h_cwT = hcw

                ds_x = [dsem(f"d_x{i}") for i in range(NXT)]
                ds_kst = [dsem(f"d_kst{i}") for i in range(1)]
                ds_vst = [dsem(f"d_vst{i}") for i in range(1)]

                state = dict(xt_free=[None] * NXT, xb_free=[None] * 2, ptr_free=[None, None], xT_free=[None] * 2,
                             st_free=[None] * 2, dg_free=[None] * 4, dgR_free=[None] * 2, Rrow_free=None, RE_free=None, rows_free=[None] * 2,
                             kps_free=[None] * 2, ssp_free=h_cwT, sq_free=[None] * 2, t1_free=[None] * 2,
                             rk_free=[None] * 2, kst_free=[None] * 2, vps_free=[None] * 2, vst_free=[None] * 2,
                             cva_free=[None] * 2, cvb_free=[None] * 2, tile_no=0, grp_no=0, kq_no=0, v_no=0,
                             cv_no=0)
                st_ = state

                def _nheads(gi_):
                    n = 0
                    for hh in range(NH):
                        wb = WIN_BLOCKS[hh]
                        if wb is None or 4 * NG + 2 * wb >= NKB:
                            n += 1
                        elif any((j <= 4 * NG - 1 + wb) or (j >= NKB - wb) for j in range(4 * gi_, 4 * gi_ + 4)):
                            n += 1
                    return n
                gorder = sorted(range(NG, NGRP), key=lambda gi_: (_nheads(gi_), gi_)) + list(range(NG))
                tile_srcs = []
                for gi_ in gorder:
                    for i_ in range(4):
                        tile_srcs.append((xr[gi_ * 512 + i_ * 128: gi_ * 512 + (i_ + 1) * 128, :], 128))
                tile_srcs.append((xhalo[:, :], 32))
                ld_h = {}
                st_["loads"] = 0

                def issue_load():
                    tn = st_["loads"]
                    if tn >= len(tile_srcs):
                        return
                    src, rows_ = tile_srcs[tn]
                    xs = tn % NXT
                    ld_h[tn] = SP.dma(xt[xs][0:rows_, :], src, ds_x[xs], deps=[st_["xt_free"][xs]])
                    st_["loads"] += 1

                for _ in range(NXT):
                    issue_load()

                def fe_begin(ntile, rows, need_R):
                    gb = st_["grp_no"] % 2
                    st_["grp_no"] += 1
                    return dict(gb=gb, ntile=ntile, rows=rows, need_R=need_R, h_sqs=[], casts={}, tns=[], h_xT=None)

                def fe_cast(cx, i):
                    rows = cx["rows"]
                    tn = cx["tns"][i]
                    xs = tn % NXT; bs = tn % 2
                    h_cast = DVE.op(lambda e, xs=xs, bs=bs: e.tensor_copy(xb[bs][0:rows, :], xt[xs][0:rows, :]),
                                    [ld_h[tn], st_["xb_free"][bs]])
                    cx["casts"][i] = h_cast

                def fe_tile(cx, i):
                    gb, rows, ntile = cx["gb"], cx["rows"], cx["ntile"]
                    if i == 0:
                        for k in range(ntile):
                            cx["tns"].append(st_["tile_no"]); st_["tile_no"] += 1
                    tn = cx["tns"][i]
                    xs = tn % NXT; bs = tn % 2; pb_ = tn % 2
                    h_ld = ld_h[tn]
                    h_sq = ACT.op(lambda e, xs=xs, i=i: e.activation(junk[0:rows, :], xt[xs][0:rows, :], AF.Square,
                                                                   accum_out=stq[gb][0:rows, 0, i:i + 1]),
                                  [h_ld, st_["st_free"][gb] if i == 0 else None])
                    cx["h_sqs"].append(h_sq)
                    if i == 0:
                        fe_cast(cx, 0)
                    h_cast = cx["casts"][i]
                    st_["xt_free"][xs] = [h_sq, h_cast]
                    issue_load()
                    hp = None
                    for c in range(NCH):
                        hp = PE.op(lambda e, c=c, bs=bs, pb_=pb_: e.transpose(ptr[pb_][:, c * 128:c * 128 + rows],
                                                                             xb[bs][0:rows, c * 128:(c + 1) * 128],
                                                                             identb[0:rows, 0:rows]),
                                   [h_cast, st_["ptr_free"][pb_], h_const] if c == 0 else [], signal=(c == NCH - 1))
                    st_["xb_free"][bs] = hp
                    if i + 1 < ntile:
                        fe_cast(cx, i + 1)
                    h_ev = DVE.op(lambda e, i=i, pb_=pb_: e.tensor_tensor(
                        xT[gb][:, :, i * rows:(i + 1) * rows],
                        ptr[pb_][:, :].rearrange("p (c t) -> p c t", c=NCH)[:, :, 0:rows],
                        gA[:, :].unsqueeze(2).to_broadcast([128, NCH, rows]), op=ALU.mult),
                        [hp, st_["xT_free"][gb] if i == 0 else None])
                    st_["ptr_free"][pb_] = h_ev
                    cx["h_xT"] = h_ev

                def fe_stats_a(cx):
                    gb, rows, ntile = cx["gb"], cx["rows"], cx["ntile"]
                    h1 = DVE.op(lambda e: e.tensor_scalar(stq[gb][0:rows, 1, 0:ntile], stq[gb][0:rows, 0, 0:ntile], 1.0 / D, EPS,
                                                          op0=ALU.mult, op1=ALU.add), cx["h_sqs"])
                    h2 = rsqrt_act(stq[gb][0:rows, 2, 0:ntile], stq[gb][0:rows, 1, 0:ntile], [h1])
                    cx["h1"], cx["h2"] = h1, h2
                    cx["h3s"] = []
                    for i in range(ntile):
                        h3 = DVE.op(lambda e, i=i: e.tensor_scalar(dgE[i][0:rows, 0:rows], identf[0:rows, 0:rows],
                                                                   stq[gb][0:rows, 1, i:i + 1], EPS, op0=ALU.mult, op1=ALU.mult),
                                    [h1, st_["dg_free"][i], h_const])
                        cx["h3s"].append(h3)

                def fe_stats_b(cx):
                    gb, rows, ntile, need_R = cx["gb"], cx["rows"], cx["ntile"], cx["need_R"]
                    N = ntile * rows
                    h1, h2 = cx["h1"], cx["h2"]
                    rst_handles = [(i, h2) for i in range(ntile)]
                    hE = None
                    for i in range(ntile):
                        hE = PE.op(lambda e, i=i: e.matmul(REp[:, i * rows:(i + 1) * rows],
                                                           lhsT=onesf[0:rows, :], rhs=dgE[i][0:rows, 0:rows],
                                                           start=True, stop=True),
                                   [cx["h3s"][i], st_["RE_free"] if i == 0 else None, h_init])
                        st_["dg_free"][i] = hE
                    hEr = ACT.op(lambda e, gb=gb: e.activation(Erow[gb][:, 0:N], REp[:, 0:N], AF.Copy), [hE, st_["rows_free"][gb], st_["rows_last_t1"][gb]])
                    st_["RE_free"] = hEr
                    hRr = None
                    if need_R:
                        hR = None
                        for i in range(ntile):
                            ds_ = i % 2
                            h4 = DVE.op(lambda e, i=i, ds_=ds_: e.tensor_scalar(dgR[ds_][0:rows, 0:rows], identf[0:rows, 0:rows],
                                                                               stq[gb][0:rows, 2, i:i + 1], None, op0=ALU.mult),
                                        [h2, st_["dgR_free"][ds_]])
                            hR = PE.op(lambda e, i=i, ds_=ds_: e.matmul(REp[:, i * rows:(i + 1) * rows],
                                                                       lhsT=onesf[0:rows, :], rhs=dgR[ds_][0:rows, 0:rows],
                                                                       start=True, stop=True), [h4, hEr if i == 0 else None])
                            st_["dgR_free"][ds_] = hR
                        hRr = ACT.op(lambda e, gb=gb: e.activation(Rrow[gb][:, 0:N], REp[:, 0:N], AF.Copy), [hR])
                        st_["RE_free"] = hRr
                    st_["st_free"][gb] = [hE, h2]
                    return gb, cx["h_xT"], hEr, hRr, rst_handles

                def fe_stats(cx):
                    fe_stats_a(cx)
                    return fe_stats_b(cx)

                def pn_stage1(gb, N, col0, h_xT):
                    kn = st_["kq_no"]; st_["kq_no"] += 1
                    b = kn % 2
                    hm = None
                    for c in range(NCH):
                        hm = PE.op(lambda e, c=c, b=b: e.matmul(kps[b][:, 0:N], lhsT=w_in_sb[:, c, col0:col0 + 128],
                                                                rhs=xT[gb][:, c, 0:N], start=(c == 0), stop=(c == NCH - 1)),
                                   [h_xT, hw_cols(col0, col0 + 128), st_["kps_free"][b]] if c == 0 else [], signal=(c == NCH - 1))
                    hs = ACT.op(lambda e, b=b: e.activation(sq[b][:, 0:N], kps[b][:, 0:N], AF.Square), [hm, st_["sq_free"][b]])
                    return dict(b=b, hm=hm, hs=hs, N=N, gb=gb)

                def pn_stage2ss(cx):
                    b, hs, N = cx["b"], cx["hs"], cx["N"]
                    hss = PE.op(lambda e, b=b: e.matmul(ssp[:, 0:N], lhsT=bones[:, :], rhs=sq[b][:, 0:N], start=True, stop=True),
                                [hs, st_["ssp_free"], h_const])
                    st_["sq_free"][b] = hss
                    cx["hss"] = hss

                def pn_stage2n(cx, hEr):
                    b, N, gb = cx["b"], cx["N"], cx["gb"]
                    ht = DVE.op(lambda e, b=b: e.scalar_tensor_tensor(t1[b][:, 0:N], ssp[:, 0:N], 1.0 / DH, Erow[gb][:, 0:N],
                                                                      op0=ALU.mult, op1=ALU.add),
                                [cx["hss"], hEr, st_["t1_free"][b]])
                    st_["ssp_free"] = ht
                    st_["rows_last_t1"][gb] = ht
                    hr = rsqrt_act(rk[b][:, 0:N], t1[b][:, 0:N], [ht, st_["rk_free"][b]])
                    st_["t1_free"][b] = hr
                    cx["hr"] = hr

                def pn_stage2a(cx, hEr):
                    pn_stage2ss(cx)
                    pn_stage2n(cx, hEr)

                def pn_stage2b(cx, writer):
                    b = cx["b"]
                    hw = writer(b, cx["hr"], cx["hm"])
                    st_["rk_free"][b] = hw
                    st_["kps_free"][b] = hw
                    return hw

                def pn_stage2(cx, hEr, writer):
                    pn_stage2a(cx, hEr)
                    return pn_stage2b(cx, writer)

                def proj_norm(gb, N, col0, h_xT, hEr, gvec, writer):
                    return pn_stage2(pn_stage1(gb, N, col0, h_xT), hEr, writer)

                st_["pending"] = []
                st_["rows_last_t1"] = [None, None]
                st_["pending_a"] = []
                st_["deferred_finish"] = []

                def run_pending_a():
                    while st_["pending_a"]:
                        st_["pending_a"].pop(0)()

                def run_pending():
                    run_pending_a()
                    while st_["pending"]:
                        st_["pending"].pop(0)()
                    while st_["deferred_finish"]:
                        st_["deferred_finish"].pop(0)()

                def heads_needed(gi):
                    res = []
                    for hh in range(NH):
                        wb = WIN_BLOCKS[hh]
                        if wb is None or 4 * NG + 2 * wb >= NKB:
                            res.append(hh)
                            continue
                        need = any((j <= 4 * NG - 1 + wb) or (j >= NKB - wb) for j in range(4 * gi, 4 * gi + 4))
                        if need:
                            res.append(hh)
                    return res

                def kv_group(gi, gb, h_xT, hEr, rst_handles):
                    sb_ = 0
                    heads = heads_needed(gi)
                    h_lo = min(heads)
                    assert heads == list(range(h_lo, NH))
                    nvh = NH - h_lo
                    hws = []
                    hvs = []
                    cxs = {}
                    first_k = [True]

                    def do_s1(hh):
                        cxs[hh] = pn_stage1(gb, 512, 512 + hh * 128, h_xT)

                    def do_s2a(hh):
                        pn_stage2ss(cxs[hh])
                        st_["pending_a"].append(lambda hh=hh: pn_stage2n(cxs[hh], hEr))

                        def wr(hh=hh):
                            def writer(b, hr, hm, hh=hh):
                                dep = st_["kst_free"][sb_] if first_k[0] else None
                                first_k[0] = False
                                return DVE.op(lambda e: e.scalar_tensor_tensor(kst[sb_][:, hh, :], kps[b][:, :], gk2[:, 0:1], rk[b][:, :],
                                                                               op0=ALU.mult, op1=ALU.mult), [hr, hm, dep])
                            hws.append(pn_stage2b(cxs[hh], writer))
                        st_["pending"].append(wr)

                    def do_v(i):
                        vn = st_["v_no"]; st_["v_no"] += 1
                        b = vn % 2
                        hm = None
                        for c in range(NCH):
                            hm = PE.op(lambda e, c=c, b=b, i=i: e.matmul(vps[b][:, 0:nvh * 128], lhsT=xT[gb][:, c, i * 128:(i + 1) * 128],
                                                                         rhs=w_in_sb[:, c, 1024 + h_lo * 128:1536], start=(c == 0), stop=(c == NCH - 1)),
                                       [h_xT, hw_cols(1024 + h_lo * 128, 1536), st_["vps_free"][b]] if c == 0 else [], signal=(c == NCH - 1))
                        st_["last_v_mm"] = hm
                        ss_, h2 = rst_handles[i]
                        hv = ACT.op(lambda e, b=b, i=i, ss_=ss_: e.activation(vst[sb_][:, h_lo:NH, i, :],
                                                                            vps[b][:, 0:nvh * 128].rearrange("p (h e) -> p h e", h=nvh),
                                                                            AF.Copy, scale=stq[gb][:, 2, ss_:ss_ + 1]),
                                    [hm, h2, st_["vst_free"][sb_] if len(hvs) == 0 else None])
                        st_["vps_free"][b] = hv
                        st_["st_free"][gb] = [st_["st_free"][gb], hv]
                        hvs.append(hv)

                    hq = {0: [], 1: [], 2: [], 3: []}
                    if len(heads) == 4:
                        for i_, hh in enumerate(heads):
                            hq[i_].append(hh)
                    elif len(heads) == 3:
                        hq[0].append(heads[0]); hq[1].append(heads[1]); hq[2].append(heads[2])
                    else:
                        hq[0].append(heads[0]); hq[2].append(heads[1])

                    def quarter(i):
                        run_pending()
                        for hh in hq[i]:
                            do_s1(hh)
                        do_v(i)
                        for hh in hq[i]:
                            do_s2a(hh)
                        if i == 3:
                            st_["xT_free"][gb] = [cxs[heads[-1]]["hm"], st_["last_v_mm"]]

                    def finish():
                        hk1 = SP.dma(kT1[h_lo:NH, :, gi * 512:(gi + 1) * 512].rearrange("h d t -> d h t"), kst[sb_][0:64, h_lo:NH, :], ds_kst[sb_], deps=hws)
                        hk2 = SP.dma(kT2[h_lo:NH, :, gi * 512:(gi + 1) * 512].rearrange("h d t -> d h t"), kst[sb_][64:128, h_lo:NH, :], ds_kst[sb_])
                        st_["kst_free"][sb_] = hk2
                        hvd = SP.dma(vdr[h_lo:NH, :, gi * 4:(gi + 1) * 4, :].rearrange("h p k e -> p h k e"), vst[sb_][:, h_lo:NH, :, :], ds_vst[sb_], deps=hvs)
                        st_["vst_free"][sb_] = hvd
                        return hk2, hvd
                    return quarter, finish

                def q_group(g, gb, h_xT, hEr):
                    hws = []
                    cxq = {}

                    def s1(hh):
                        cxq[hh] = pn_stage1(gb, 512, hh * 128, h_xT)

                    def s2a(hh):
                        pn_stage2a(cxq[hh], hEr)

                    def s2b(hh):
                        def writer(b, hr, hm, hh=hh):
                            DVE.op(lambda e: e.scalar_tensor_tensor(Q1[0:64, hh, g * 512:(g + 1) * 512], kps[b][0:64, :], gq2[0:64, 0:1],
                                                                    rk[b][0:64, :], op0=ALU.mult, op1=ALU.mult), [hr, hm], signal=False)
                            return DVE.op(lambda e: e.scalar_tensor_tensor(q2st[64:128, hh, :], kps[b][64:128, :],
                                                                           gq2[64:128, 0:1], rk[b][64:128, :], op0=ALU.mult, op1=ALU.mult),
                                          [st_q2free[0] if hh == 0 else None])
                        hws.append(pn_stage2b(cxq[hh], writer))
                    s1(0); s1(1); s2a(0); s2a(1); s2b(0); s1(2); s2b(1); s1(3); s2a(2); s2a(3); s2b(2); s2b(3)
                    hmv = SP.dma(Q2[0:64, :, g * 512:(g + 1) * 512], q2st[64:128, :, :], ds_q2, deps=[hws[-1]])
                    st_q2free[0] = hmv
                    return [hws[-1], hmv]

                def conv_in(gb, N, ucol0, h_xT, hRr):
                    hl = None
                    for cc in range(4):
                        cn = st_["cv_no"]; st_["cv_no"] += 1
                        b = cn % 2
                        ha = hg = None
                        for c in range(NCH):
                            ha = PE.op(lambda e, c=c, b=b, cc=cc: e.matmul(kps[b][:, 0:N], lhsT=w_in_sb[:, c, 1536 + cc * 128:1536 + (cc + 1) * 128],
                                                                           rhs=xT[gb][:, c, 0:N], start=(c == 0), stop=(c == NCH - 1)),
                                       [h_xT, hw_cols(1536, 2560), st_["kps_free"][b]] if c == 0 else [], signal=(c == NCH - 1))
                        for c in range(NCH):
                            hg = PE.op(lambda e, c=c, b=b, cc=cc: e.matmul(vps[b][:, 0:N], lhsT=w_in_sb[:, c, 2048 + cc * 128:2048 + (cc + 1) * 128],
                                                                           rhs=xT[gb][:, c, 0:N], start=(c == 0), stop=(c == NCH - 1)),
                                       [st_["vps_free"][b]] if c == 0 else [], signal=(c == NCH - 1))
                        h1 = DVE.op(lambda e, b=b: e.tensor_tensor(t1[b][:, 0:N], vps[b][:, 0:N], Rrow[gb][:, 0:N], op=ALU.mult),
                                    [hg, hRr, st_["t1_free"][b]])
                        st_["vps_free"][b] = h1
                        h2 = ACT.op(lambda e, b=b: e.activation(rk[b][:, 0:N], t1[b][:, 0:N], AF.Tanh, scale=0.5), [h1, st_["rk_free"][b]])
                        h4 = DVE.op(lambda e, b=b: e.scalar_tensor_tensor(t1[b][:, 0:N], rk[b][:, 0:N], 1.0, Rrow[gb][:, 0:N], op0=ALU.add, op1=ALU.mult), [h2])
                        h5 = DVE.op(lambda e, b=b, cc=cc: e.scalar_tensor_tensor(uT[:, cc, ucol0:ucol0 + N], kps[b][:, 0:N], 0.5, t1[b][:, 0:N],
                                                                                 op0=ALU.mult, op1=ALU.mult), [h4, ha])
                        st_["t1_free"][b] = h5
                        st_["rk_free"][b] = h5
                        st_["kps_free"][b] = h5
                        hl = h5
                    return hl

                h_kv_all = []
                h_q_last = None
                h_u_last = None
                cx0 = fe_begin(4, 128, gorder[0] < NG)
                for i in range(4):
                    fe_tile(cx0, i)
                pre = fe_stats(cx0)
                for oi, gi in enumerate(gorder):
                    own = gi < NG
                    if oi + 1 < NGRP:
                        ncx = fe_begin(4, 128, gorder[oi + 1] < NG)
                    else:
                        ncx = fe_begin(1, 32, True)
                    gb, h_xT, hEr, hRr, rsth = pre
                    quarter, finish = kv_group(gi, gb, h_xT, hEr, rsth)
                    for i in range(4):
                        run_pending_a()
                        if i < ncx["ntile"]:
                            fe_tile(ncx, i)
                            if i == ncx["ntile"] - 1:
                                fe_stats_a(ncx)
                        quarter(i)
                    def fin_(finish=finish):
                        hk, hv = finish()
                        h_kv_all.extend([hk, hv])
                    st_["deferred_finish"].append(fin_)
                    if own:
                        run_pending()
                        h_q_last = q_group(gi, gb, h_xT, hEr)
                        h_u_last = conv_in(gb, 512, 16 + gi * 512, h_xT, hRr)
                        st_["xT_free"][gb] = [st_["xT_free"][gb], h_q_last, h_u_last]
                        st_["rows_free"][gb] = [h_q_last, h_u_last]
                    else:
                        st_["rows_free"][gb] = st_["kps_free"][:] + st_["t1_free"][:]
                    pre = fe_stats_b(ncx)
                run_pending()
                h_kv_last = h_kv_all
                gb, h_xT, hEr, hRr, rsth = pre
                def conv_in_halo():
                    hl = None
                    for cc in range(4):
                        cn = st_["cv_no"]; st_["cv_no"] += 1
                        b = cn % 2
                        N = 32
                        ha = hg = None
                        for c in range(NCH):
                            ha = PE.op(lambda e, c=c, b=b, cc=cc: e.matmul(kps[b][:, 0:N], lhsT=w_in_sb[:, c, 1536 + cc * 128:1536 + (cc + 1) * 128],
                                                                           rhs=xT[gb][:, c, 0:N], start=(c == 0), stop=(c == NCH - 1)),
                                       [h_xT, hw_cols(1536, 2560), st_["kps_free"][b]] if c == 0 else [], signal=(c == NCH - 1))
                        for c in range(NCH):
                            hg = PE.op(lambda e, c=c, b=b, cc=cc: e.matmul(vps[b][:, 0:N], lhsT=w_in_sb[:, c, 2048 + cc * 128:2048 + (cc + 1) * 128],
                                                                           rhs=xT[gb][:, c, 0:N], start=(c == 0), stop=(c == NCH - 1)),
                                       [st_["vps_free"][b]] if c == 0 else [], signal=(c == NCH - 1))
                        h1 = DVE.op(lambda e, b=b: e.tensor_tensor(cva[0][:, 0:N], vps[b][:, 0:N], Rrow[gb][:, 0:N], op=ALU.mult),
                                    [hg, hRr, st_["cva_free"][0]])
                        st_["vps_free"][b] = h1
                        h2 = ACT.op(lambda e, b=b: e.activation(cvb[0][:, 0:N], cva[0][:, 0:N], AF.Exp, scale=-1.0), [h1, st_["cvb_free"][0]])
                        h3a = DVE.op(lambda e, b=b: e.tensor_scalar(cva[0][:, 0:N], cvb[0][:, 0:N], 1.0, None, op0=ALU.add), [h2])
                        h3 = DVE.op(lambda e, b=b: e.reciprocal(cva[0][:, 0:N], cva[0][:, 0:N]), [h3a])
                        h4 = DVE.op(lambda e, b=b: e.tensor_tensor(cvb[0][:, 0:N], cva[0][:, 0:N], Rrow[gb][:, 0:N], op=ALU.mult), [h3])
                        DVE.op(lambda e, b=b, cc=cc: e.tensor_tensor(uT[:, cc, 0:16], kps[b][:, 0:16], cvb[0][:, 0:16], op=ALU.mult),
                               [h4, ha], signal=False)
                        h5 = DVE.op(lambda e, b=b, cc=cc: e.tensor_tensor(uT[:, cc, 16 + NT:32 + NT], kps[b][:, 16:32], cvb[0][:, 16:32], op=ALU.mult), [])
                        st_["cva_free"][0] = h5
                        st_["cvb_free"][0] = h5
                        st_["kps_free"][b] = h5
                        hl = h5
                    return hl
                h_u_halo = conv_in_halo()
                h_A1_done = [h_u_halo, h_u_last, h_q_last] + h_kv_last
                for E_ in (PE, ACT, DVE, POOL, SP):
                    E_.wait(h_A1_done)
                    E_.wait([st_["kps_free"], st_["vps_free"], st_["rows_free"], st_["RE_free"], st_["ptr_free"], st_["st_free"]])
                print("SBUF remaining (A1):", nc.sbuf_bytes_remaining)
                flush()

            esA2 = ExitStack()
            with esA2:
                Dg = sb("Dg", [128, 4, CK, 128], BF16, esA2)
                yv2 = [sb(f"yv{i}", [128, 4, 512], F32, esA2) for i in range(2)]
                ysq2 = [sb(f"ysq{i}", [128, 4, 512], F32, esA2) for i in range(2)]
                mean_sb = sb("mean_sb", [128, 512], F32, esA2)
                m2 = sb("m2", [128, 512], F32, esA2)
                rs_ln = sb("rs_ln", [128, 512], F32, esA2)
                z1 = [sb(f"z1_{i}", [128, 512], F32, esA2) for i in range(2)]
                z2 = [sb(f"z2_{i}", [128, 512], F32, esA2) for i in range(2)]
                hg4 = sb("hg4", [128, 4], F32, esA2)
                hb4 = sb("hb4", [128, 4], F32, esA2)
                onesf512 = sb("onesf512", [128, 128], F32, esA2)
                yps = [ps(f"yps{i}", [128, 512], F32, esA2) for i in range(2)]
                mps = ps("mps", [128, 512], F32, esA2)
                eps_ = ps("eps_", [128, 512], F32, esA2)

                hd = None
                for cc in range(4):
                    for k in range(CK):
                        hd = DVE.op(lambda e, cc=cc, k=k: e.tensor_scalar(Dg[:, cc, k, :], identf[:, :], cwT[:, cc, k:k + 1], None, op0=ALU.mult),
                                    [h_cwT, h_A1_done] if (cc == 0 and k == 0) else [], signal=(k == CK - 1))
                h_Dg = hd
                h_a = DVE.op(lambda e: e.tensor_scalar(hg4[:], lg4[:], 0.5, None, op0=ALU.mult), [h_const])
                h_b = DVE.op(lambda e: e.tensor_scalar(hb4[:], lb4[:], 0.5, None, op0=ALU.mult), [h_const])
                h_o5 = POOL.op(lambda e: e.memset(onesf512[:], 1.0 / 512.0), [h_A1_done])
                yps_free = [None, None]
                yv_free = [None, None]
                stat_free = None
                z_free = [None, None]
                h_conv_last = None
                n_y = [0]
                hy_of = {}

                def conv_part(g):
                    yb = g % 2
                    yv, ysq = yv2[yb], ysq2[yb]
                    h_y = []
                    for cc in range(4):
                        b = n_y[0] % 2; n_y[0] += 1
                        hm = None
                        for k in range(CK):
                            hm = PE.op(lambda e, cc=cc, k=k, b=b, g=g: e.matmul(yps[b][:, :], lhsT=Dg[:, cc, k, :],
                                                                                rhs=uT[:, cc, g * 512 + k + 1: g * 512 + k + 1 + 512],
                                                                                start=(k == 0), stop=(k == CK - 1)),
                                       [h_Dg, yps_free[b]] if k == 0 else [], signal=(k == CK - 1))
                        hy = DVE.op(lambda e, cc=cc, b=b, yv=yv: e.tensor_scalar(yv[:, cc, :], yps[b][:, :], cb4[:, cc:cc + 1], None, op0=ALU.add),
                                    [hm, yv_free[yb] if cc == 0 else None])
                        yps_free[b] = hy
                        hs = ACT.op(lambda e, cc=cc, yv=yv, ysq=ysq: e.activation(ysq[:, cc, :], yv[:, cc, :], AF.Square), [hy])
                        h_y.append((hy, hs))
                    hy_of[g] = h_y

                def ln_part(g):
                    nonlocal stat_free, h_conv_last
                    yb = g % 2
                    yv, ysq = yv2[yb], ysq2[yb]
                    h_y = hy_of[g]
                    hmean = None
                    for cc in range(4):
                        hmean = PE.op(lambda e, cc=cc, yv=yv: e.matmul(mps[:, :], lhsT=onesf512[:, :], rhs=yv[:, cc, :], start=(cc == 0), stop=(cc == 3)),
                                      [h_y[cc][0], h_o5, stat_free], signal=(cc == 3))
                    hey = None
                    for cc in range(4):
                        hey = PE.op(lambda e, cc=cc, ysq=ysq: e.matmul(eps_[:, :], lhsT=onesf512[:, :], rhs=ysq[:, cc, :], start=(cc == 0), stop=(cc == 3)),
                                    [h_y[cc][1]], signal=(cc == 3))
                    h1 = ACT.op(lambda e: e.activation(mean_sb[:, :], mps[:, :], AF.Copy), [hmean, z_free])
                    h2 = DVE.op(lambda e: e.tensor_tensor(m2[:, :], mean_sb[:, :], mean_sb[:, :], op=ALU.mult), [h1, stat_free])
                    h3 = DVE.op(lambda e: e.tensor_tensor(m2[:, :], eps_[:, :], m2[:, :], op=ALU.subtract), [h2, hey])
                    h4 = DVE.op(lambda e: e.tensor_scalar(m2[:, :], m2[:, :], EPS, None, op0=ALU.add), [h3])
                    h5 = rsqrt_act(rs_ln[:, :], m2[:, :], [h4])
                    hl = None
                    for cc in range(4):
                        b = cc % 2
                        a1 = DVE.op(lambda e, cc=cc, b=b, yv=yv: e.tensor_tensor(z1[b][:, :], yv[:, cc, :], mean_sb[:, :], op=ALU.subtract), [h1, z_free[b]])
                        a2 = DVE.op(lambda e, b=b: e.tensor_tensor(z1[b][:, :], z1[b][:, :], rs_ln[:, :], op=ALU.mult), [a1, h5])
                        a3 = DVE.op(lambda e, cc=cc, b=b: e.tensor_scalar(z1[b][:, :], z1[b][:, :], hg4[:, cc:cc + 1], hb4[:, cc:cc + 1],
                                                                          op0=ALU.mult, op1=ALU.add), [a2, h_a, h_b])
                        a4 = ACT.op(lambda e, b=b: e.activation(z2[b][:, :], z1[b][:, :], AF.Tanh), [a3])
                        a5 = DVE.op(lambda e, cc=cc, b=b, g=g: e.scalar_tensor_tensor(conv_o[:, cc, g * 512:(g + 1) * 512], z2[b][:, :], 1.0, z1[b][:, :],
                                                                                      op0=ALU.add, op1=ALU.mult), [a4])
                        z_free[b] = a5
                        hl = a5
                    yv_free[yb] = hl
                    stat_free = h5
                    h_conv_last = hl

                conv_part(0)
                for g in range(NG):
                    if g + 1 < NG:
                        conv_part(g + 1)
                    ln_part(g)
                for E_ in (PE, ACT, DVE, POOL, SP):
                    E_.wait([h_conv_last])
                print("SBUF remaining (A2):", nc.sbuf_bytes_remaining)
                flush()

            esB = ExitStack()
            with esB:
                K1 = [sb(f"K1_{i}", [128, CHK * 128], BF16, esB) for i in range(NSLOT)]
                K2 = [sb(f"K2_{i}", [128, CHK * 128], BF16, esB) for i in range(NSLOT)]
                Vs = [sb(f"Vs_{i}", [128, CHK * 128], BF16, esB) for i in range(NSLOT)]
                PT = [sb(f"PT_{i}", [128, 1024], BF16, esB) for i in range(NPT)]
                dtmp = [sb(f"dtmp{i}", [128, 1024], F32, esB) for i in range(2)]
                bd = sb("bd", [128, NH, 896], F32, esB)
                oT = sb("oT", [128, NH, NT], F32, esB)
                cO = [sb(f"cO{i}", [128, 512], F32, esB) for i in range(2)]
                cl = [sb(f"cl{i}", [128, 512], F32, esB) for i in range(2)]
                accD2 = [sb(f"accD{i}", [128, RSD], F32, esB) for i in range(2)]
                Sps = [ps(f"Sps{i}", [128, 1024], F32, esB) for i in range(2)]
                Ops = [ps(f"Ops{i}", [128, 512], F32, esB) for i in range(2)]
                Lps = [ps(f"Lps{i}", [128, 512], F32, esB) for i in range(2)]

                ds_bd = dsem("d_bd")
                h_bd = None
                for hh in range(NH):
                    h_bd = SP.dma(bd[:, hh, :], c_bdiag[hh], ds_bd, deps=[h_conv_last])
                hz = [h_conv_last for i in range(NSLOT)]
                ds_slot = [dsem(f"d_slot{i}") for i in range(NSLOT)]
                slot_free = [hz[i] for i in range(NSLOT)]
                slot_ld = [None] * NSLOT
                pt_free = [None] * NPT
                s_free = [None, None]
                dt_free = [None, None]
                ol_free = None
                c_free = None
                acc_free = [None, None]
                h_kv_ready = h_kv_last
                gstep = 0
                units = [(hh, g) for hh in range(NH) for g in range(NG)]

                def unit_blocks(hh, g):
                    wb = WIN_BLOCKS[hh]
                    if wb is None or 4 + 2 * wb >= NKB:
                        return [list(range(NKB))]
                    lo, hi = 4 * g - wb, 4 * g + 3 + wb
                    if lo < 0:
                        return [list(range(NKB + lo, NKB)), list(range(0, hi + 1))]
                    if hi >= NKB:
                        return [list(range(lo, NKB)), list(range(0, hi - NKB + 1))]
                    return [list(range(lo, hi + 1))]

                fills = []
                unit_steps = []
                for u, (hh, g) in enumerate(units):
                    stl = []
                    for run in unit_blocks(hh, g):
                        for a in range(0, len(run), CHK):
                            piece = run[a:a + CHK]
                            fi = len(fills)
                            fills.append((u, piece[0], len(piece)))
                            for kb, j in enumerate(piece):
                                stl.append((fi, kb, j, kb == len(piece) - 1))
                    unit_steps.append(stl)

                def issue_fill(fi):
                    u, j0, nkb = fills[fi]
                    hh, g = units[u]
                    s = fi % NSLOT
                    deps = [slot_free[s], h_kv_ready]
                    c0 = j0 * 128
                    wdt = nkb * 128
                    SP.dma(K1[s][0:64, 0:wdt], kT1[hh, :, c0:c0 + wdt], ds_slot[s], deps=deps)
                    SP.dma(K2[s][0:64, 0:wdt], kT2[hh, :, c0:c0 + wdt], ds_slot[s])
                    SP.dma(K1[s][64:68, 0:wdt], c_kaug[hh, g, :, c0:c0 + wdt], ds_slot[s])
                    SP.dma(K2[s][64:68, 0:wdt], c_kaug[hh, g, :, c0:c0 + wdt], ds_slot[s])
                    slot_ld[s] = SP.dma(Vs[s][:, 0:wdt], vdr[hh, :, j0:j0 + nkb, :].rearrange("p k e -> p (k e)"), ds_slot[s])

                nissued = 0
                for _ in range(min(NSLOT, len(fills))):
                    issue_fill(nissued); nissued += 1

                pendq = []
                dve_q = []
                epi_state = dict(c_free=None, last=None)

                def emit_av(pd):
                    nonlocal ol_free
                    (s, kb, pslot, h_p, last_in_fill, first, last, h_rs, uinfo) = pd
                    deps = [h_p, ol_free if first else None]
                    PE.op(lambda e: e.matmul(Ops[0][:, :], lhsT=Vs[s][:, kb * 128:(kb + 1) * 128], rhs=PT[pslot][:, 0:512], start=first, stop=last),
                          deps, signal=False)
                    PE.op(lambda e: e.matmul(Ops[1][:, :], lhsT=Vs[s][:, kb * 128:(kb + 1) * 128], rhs=PT[pslot][:, 512:1024], start=first, stop=last), [], signal=False)
                    hv = PE.op(lambda e: e.matmul(Lps[1][:, RSD - 512:512], lhsT=onesb[:, :], rhs=PT[pslot][:, RSD:1024], start=first, stop=last),
                               [epi_state.get("l_free") if first else None])
                    pt_free[pslot] = [hv] + h_rs
                    if last_in_fill:
                        slot_free[s] = hv
                    if last:
                        epilogue(hv, h_rs[0], uinfo)
                    return hv

                def epilogue(hv, r1, uinfo):
                    nonlocal ol_free
                    hh, g, ab = uinfo
                    accD = accD2[ab]
                    PE.op(lambda e: e.matmul(Lps[0][:, :], lhsT=onesf[:, :], rhs=accD[:, 0:512], start=True, stop=True), [r1, epi_state.get("l_free")], signal=False)
                    hl_ = PE.op(lambda e: e.matmul(Lps[1][:, 0:RSD - 512], lhsT=onesf[:, :], rhs=accD[:, 512:RSD], start=True, stop=True), [])
                    acc_free[ab] = hl_
                    e1 = DVE.op(lambda e: e.tensor_copy(cO[0][:, :], Ops[0][:, :]), [hv, epi_state["c_free"]])
                    e3 = DVE.op(lambda e: e.tensor_copy(cO[1][:, :], Ops[1][:, :]), [])
                    e4 = DVE.op(lambda e: e.tensor_copy(cl[1][:, :], Lps[1][:, :]), [hl_])
                    e2 = DVE.op(lambda e: e.tensor_copy(cl[0][:, :], Lps[0][:, :]), [])
                    ol_free = e3
                    epi_state["l_free"] = e2
                    box = dict(h=e2)
                    dve_q.extend([None, None, None])

                    def mk(eng, fn, extra=None):
                        def run():
                            box["h"] = eng.op(fn, [box["h"], extra])
                            epi_state["last"] = box["h"]
                        return run
                    dve_q.append(mk(ACT, lambda e: e.activation(cl[0][:, :], cl[0][:, :], AF.Ln)))
                    dve_q.append(mk(ACT, lambda e: e.activation(cl[0][:, :], cl[0][:, :], AF.Exp, scale=-1.0)))
                    dve_q.append(mk(ACT, lambda e: e.activation(cl[1][:, :], cl[1][:, :], AF.Ln)))
                    dve_q.append(mk(ACT, lambda e: e.activation(cl[1][:, :], cl[1][:, :], AF.Exp, scale=-1.0)))
                    dve_q.append(mk(DVE, lambda e: e.tensor_tensor(cO[0][:, :], cO[0][:, :], cl[0][:, :], op=ALU.mult)))
                    dve_q.append(mk(DVE, lambda e: e.tensor_tensor(cO[1][:, :], cO[1][:, :], cl[1][:, :], op=ALU.mult)))

                    def fin():
                        box["h"] = DVE.op(lambda e, hh=hh, g=g: e.scalar_tensor_tensor(oT[:, hh, g * 512:(g + 1) * 512], cO[1][:, :], neglam[:, 0:1], cO[0][:, :],
                                                                                     op0=ALU.mult, op1=ALU.add), [box["h"], h_lam])
                        epi_state["c_free"] = box["h"]
                        epi_state["last"] = box["h"]
                    dve_q.append(fin)

                def drain_dve(n):
                    for _ in range(n):
                        if dve_q:
                            f_ = dve_q.pop(0)
                            if f_ is not None:
                                f_()

                for u, (hh, g) in enumerate(units):
                    stl = unit_steps[u]
                    nst = len(stl)
                    for si, (fi, kb, j, last_in_fill) in enumerate(stl):
                        s = fi % NSLOT
                        sbk = gstep % 2; pslot = gstep % NPT
                        diag = (4 * g <= j < 4 * g + 4)
                        first = (si == 0); last = (si == nst - 1)
                        if first:
                            drain_dve(100)
                        PE.op(lambda e, s=s, kb=kb, sbk=sbk, hh=hh, g=g: e.matmul(Sps[sbk][:, 0:512], lhsT=K1[s][0:68, kb * 128:(kb + 1) * 128],
                                                                              rhs=Q1[0:68, hh, g * 512:(g + 1) * 512], start=True, stop=True),
                              [slot_ld[s], s_free[sbk], h_q_last, h_qaug], signal=False)
                        h_s = PE.op(lambda e, s=s, kb=kb, sbk=sbk, hh=hh, g=g: e.matmul(Sps[sbk][:, 512:1024], lhsT=K2[s][0:68, kb * 128:(kb + 1) * 128],
                                                                                    rhs=Q2[0:68, hh, g * 512:(g + 1) * 512], start=True, stop=True), [])
                        if len(pendq) >= 2:
                            pd = pendq.pop(0)
                            emit_av(pd)
                            if pd[4] and nissued < len(fills):
                                issue_fill(nissued); nissued += 1
                        if diag:
                            x0 = 384 - 128 * (j - 4 * g)
                            DVE.op(lambda e, sbk=sbk, hh=hh, x0=x0: e.tensor_tensor(dtmp[sbk][:, 0:512], Sps[sbk][:, 0:512], bd[:, hh, x0:x0 + 512], op=ALU.add),
                                   [h_s, dt_free[sbk], h_bd], signal=False)
                            h_d = DVE.op(lambda e, sbk=sbk, hh=hh, x0=x0: e.tensor_tensor(dtmp[sbk][:, 512:1024], Sps[sbk][:, 512:1024], bd[:, hh, x0:x0 + 512], op=ALU.add), [])
                            h_p = ACT.op(lambda e, sbk=sbk, pslot=pslot: e.activation(PT[pslot][:, :], dtmp[sbk][:, :], AF.Exp, scale=0.125),
                                         [h_d, pt_free[pslot]])
                            dt_free[sbk] = h_p
                        else:
                            h_p = ACT.op(lambda e, sbk=sbk, pslot=pslot: e.activation(PT[pslot][:, :], Sps[sbk][:, :], AF.Exp, scale=0.125),
                                         [h_s, pt_free[pslot]])
                        s_free[sbk] = h_p
                        ab = u % 2
                        accD = accD2[ab]
                        if first:
                            r1 = DVE.op(lambda e, pslot=pslot, accD=accD: e.tensor_copy(accD[:, :], PT[pslot][:, 0:RSD]), [h_p, acc_free[ab]])
                        else:
                            r1 = DVE.op(lambda e, pslot=pslot, accD=accD: e.tensor_tensor(accD[:, :], accD[:, :], PT[pslot][:, 0:RSD], op=ALU.add), [h_p])
                        drain_dve(1)
                        pendq.append((s, kb, pslot, h_p, last_in_fill, first, last, [r1], (hh, g, ab)))
                        gstep += 1
                while pendq:
                    pd = pendq.pop(0)
                    emit_av(pd)
                    if pd[4] and nissued < len(fills):
                        issue_fill(nissued); nissued += 1
                drain_dve(100)
                c_free = epi_state["c_free"]
                h_attn = c_free
                for E_ in (PE, ACT, DVE, POOL, SP):
                    E_.wait([h_attn])

                hsq = sb("hsq", [128, 512], F32, esB)
                hrs = sb("hrs", [128, 512], F32, esB)
                hfree = None
                hl = None
                for hh in range(NH):
                    for g in range(NG):
                        b = (hh * NG + g) % 2
                        a1 = ACT.op(lambda e, hh=hh, g=g: e.activation(hsq[:, :], oT[:, hh, g * 512:(g + 1) * 512], AF.Square), [h_attn, hfree])
                        a2 = PE.op(lambda e, b=b: e.matmul(Ops[b][:, :], lhsT=onesf[:, :], rhs=hsq[:, :], start=True, stop=True), [a1])
                        a3 = DVE.op(lambda e, b=b: e.tensor_scalar(hrs[:, :], Ops[b][:, :], 1.0 / 128.0, EPS, op0=ALU.mult, op1=ALU.add), [a2])
                        a4 = rsqrt_act(hrs[:, :], hrs[:, :], [a3])
                        a5 = DVE.op(lambda e, hh=hh, g=g: e.scalar_tensor_tensor(attn_o[:, hh, g * 512:(g + 1) * 512], oT[:, hh, g * 512:(g + 1) * 512],
                                                                                 hn8[:, 0:1], hrs[:, :], op0=ALU.mult, op1=ALU.mult), [a4, h_hn])
                        hfree = a5
                        hl = a5
                h_attn_o = hl
                for E_ in (PE, ACT, DVE, POOL, SP):
                    E_.wait([h_attn_o])
                print("SBUF remaining (B):", nc.sbuf_bytes_remaining)
                flush()

        esC = ExitStack()
        with esC:
            WS = [sb(f"WS{i}", [128, 1024], BF16, esC) for i in range(NWS)]
            x1 = sb("x1", [128, 4, D], F32, esC)
            xn = [sb(f"xn{i}", [128, D], BF16, esC) for i in range(2)]
            hT = sb("hT", [128, NCH, 512], BF16, esC)
            hid = sb("hid", [128, NF, 512], BF16, esC)
            thb = [sb(f"thb{i}", [128, 512], F32, esC) for i in range(2)]
            t2b = [sb(f"t2b{i}", [128, 512], F32, esC) for i in range(2)]
            sgt = sb("sgt", [128, 4, D], F32, esC)
            pld = [sb(f"pld{i}", [128, PLE], F32, esC) for i in range(2)]
            pbf = [sb(f"pbf{i}", [128, PLE], BF16, esC) for i in range(2)]
            pT = sb("pT", [128, 2, 512], BF16, esC)
            stc = [sb(f"stc{i}", [128, 4], F32, esC) for i in range(4)]
            junkc = sb("junkc", [128, D], BF16, esC)
            xin = [sb(f"xin{i}", [128, D], F32, esC) for i in range(2)]
            B = [ps(f"B{i}", [128, 512], F32, esC) for i in range(8)]

            ds_ws = [dsem(f"d_ws{i}") for i in range(NWS)]
            ds_xin = [dsem(f"d_xin{i}") for i in range(2)]
            ds_pl = [dsem(f"d_pl{i}") for i in range(2)]
            ds_out = dsem("d_out")
            ws_free = [h_attn_o] * NWS

            def piece_list():
                L = []
                for c in range(NCH):
                    L.append(("wo", c))
                for f in range(NF):
                    L.append(("wg", f)); L.append(("wu", f))
                for f in range(NF):
                    L.append(("wd", f))
                for c in range(NCH):
                    L.append(("pg", c))
                for c in range(2):
                    L.append(("pp", c))
                return L
            PL = piece_list()
            NP = len(PL)
            allp = [(g, i) for g in range(NG) for i in range(NP)]
            piece_h = {}
            npi = [0]

            def issue_piece():
                if npi[0] >= len(allp):
                    return
                g, i = allp[npi[0]]
                s = npi[0] % NWS
                piece_h[(g, i)] = (s, SP.dma(WS[s][:, :], wsc[i], ds_ws[s], deps=[ws_free[s], h_cv]))
                npi[0] += 1

            for _ in range(NWS):
                issue_piece()

            def use_piece(g, i):
                return piece_h[(g, i)]

            def done_piece(g, i, h):
                s, _ = piece_h[(g, i)]
                ws_free[s] = h
                issue_piece()

            stg = [sb(f"stg{i}", [128, 3, 4], F32, esC) for i in range(2)]
            stg_free = [None, None]
            n_rg = [0]

            def rms_group(gvec, tile_deps, hT_dep):
                sg = n_rg[0] % 2; n_rg[0] += 1
                h1s = []
                for i in range(4):
                    h1s.append(ACT.op(lambda e, i=i: e.activation(junkc[:, :], x1[:, i, :], AF.Square, accum_out=stg[sg][:, 0, i:i + 1]),
                                      [tile_deps[i], stg_free[sg] if i == 0 else None]))
                h2 = DVE.op(lambda e: e.tensor_scalar(stg[sg][:, 1, :], stg[sg][:, 0, :], 1.0 / D, EPS, op0=ALU.mult, op1=ALU.add), h1s)
                h3 = rsqrt_act(stg[sg][:, 2, :], stg[sg][:, 1, :], [h2])
                h5 = None
                h4s = {}

                def do_xn(i):
                    b = i % 2
                    h4s[i] = DVE.op(lambda e, i=i, b=b: e.tensor_scalar(xn[b][:, :], x1[:, i, :], stg[sg][:, 2, i:i + 1], None, op0=ALU.mult),
                                    [h3, xn_free[b], tile_deps[i]])
                do_xn(0)
                for i in range(4):
                    b = i % 2
                    bk = 6 + b
                    h4 = h4s[i]
                    hp = None
                    for c in range(NCH):
                        hp = PE.op(lambda e, c=c, b=b, bk=bk: e.transpose(B[bk][:, :].bitcast(BF16)[:, c * 128:(c + 1) * 128], xn[b][:, c * 128:(c + 1) * 128], identb[:, :]),
                                   [h4, bank_free[bk]] if c == 0 else [], signal=(c == NCH - 1))
                    xn_free[b] = hp
                    if i + 1 < 4:
                        do_xn(i + 1)
                    h5 = DVE.op(lambda e, i=i, bk=bk: e.tensor_tensor(hT[:, :, i * 128:(i + 1) * 128],
                                                                     B[bk][:, :].bitcast(BF16).rearrange("p (c t) -> p c t", c=NCH),
                                                                     gvec[:, :].unsqueeze(2).to_broadcast([128, NCH, 128]), op=ALU.mult),
                                [hp, hT_dep if i == 0 else None])
                    bank_free[bk] = h5
                stg_free[sg] = h5
                return h5

            xn_free = [None, None]
            bank_free = [None] * 8
            ptr_free = None
            xin_free = [None, None]
            pld_free = [None, None]
            pbf_free = [None, None]
            hT_free = None
            hid_free = None
            x1_free = None
            sgt_free = None
            pT_free = None
            n_r = 0
            nx = 0
            npl = 0
            h_out_last = None
            for g in range(NG):
                wo = [use_piece(g, c) for c in range(NCH)]
                h_hT = None
                h_x1 = []
                for i in range(4):
                    xs = nx % 2; nx += 1
                    h_xl = SP.dma(xin[xs][:, :], xr[g * 512 + i * 128: g * 512 + (i + 1) * 128, :], ds_xin[xs], deps=[xin_free[xs]])
                    for half in range(2):
                        bk = (i % 2) * 2 + half
                        hm = None
                        for c in range(NCH):
                            lhs = attn_o[:, c, g * 512 + i * 128: g * 512 + (i + 1) * 128] if c < 4 else conv_o[:, c - 4, g * 512 + i * 128: g * 512 + (i + 1) * 128]
                            hm = PE.op(lambda e, lhs=lhs, c=c, bk=bk, half=half: e.matmul(B[bk][:, :], lhsT=lhs, rhs=WS[wo[c][0]][:, half * 512:(half + 1) * 512],
                                                                                         start=(c == 0), stop=(c == NCH - 1)),
                                       [wo[c][1], bank_free[bk], h_attn_o] if True else [], signal=(c == NCH - 1))
                        ha = DVE.op(lambda e, i=i, bk=bk, half=half, xs=xs: e.tensor_tensor(x1[:, i, half * 512:(half + 1) * 512], B[bk][:, :],
                                                                                         xin[xs][:, half * 512:(half + 1) * 512], op=ALU.add),
                                    [hm, h_xl, x1_free if (i == 0 and half == 0) else None])
                        bank_free[bk] = ha
                    xin_free[xs] = ha
                    if i == 3:
                        for c in range(NCH):
                            done_piece(g, c, hm)
                    h_x1.append(ha)
                h_pT = None
                for i in range(4):
                    pb = npl % 2; npl += 1
                    h_pl = SP.dma(pld[pb][:, :], p_own[g * 512 + i * 128: g * 512 + (i + 1) * 128, :], ds_pl[pb], deps=[pld_free[pb]])
                    h_pc = DVE.op(lambda e, pb=pb: e.tensor_copy(pbf[pb][:, :], pld[pb][:, :]), [h_pl, pbf_free[pb]])
                    pld_free[pb] = h_pc
                    hp = None
                    for c in range(2):
                        hp = PE.op(lambda e, c=c, pb=pb: e.transpose(B[5][:, :].bitcast(BF16)[:, c * 128:(c + 1) * 128], pbf[pb][:, c * 128:(c + 1) * 128], identb[:, :]),
                                   [h_pc, bank_free[5]] if c == 0 else [], signal=(c == 1))
                    pbf_free[pb] = hp
                    h_pT = DVE.op(lambda e, i=i: e.tensor_copy(pT[:, :, i * 128:(i + 1) * 128],
                                                               B[5][:, :].bitcast(BF16)[:, 0:256].rearrange("p (c t) -> p c t", c=2)), [hp, pT_free if i == 0 else None])
                    bank_free[5] = h_pT
                h_hT = rms_group(gFn, h_x1, hT_free)
                hid_w = None
                for f in range(NF):
                    ig, iu = NCH + 2 * f, NCH + 2 * f + 1
                    sg_, hgp = use_piece(g, ig)
                    su_, hup = use_piece(g, iu)
                    bg = (f % 2) * 2; bu = bg + 1
                    hm1 = hm2 = None
                    for c in range(NCH):
                        hm1 = PE.op(lambda e, c=c, sg_=sg_, bg=bg: e.matmul(B[bg][:, :], lhsT=WS[sg_][:, c * 128:(c + 1) * 128], rhs=hT[:, c, :],
                                                                           start=(c == 0), stop=(c == NCH - 1)),
                                    [hgp, h_hT, bank_free[bg]] if c == 0 else [], signal=(c == NCH - 1))
                    for c in range(NCH):
                        hm2 = PE.op(lambda e, c=c, su_=su_, bu=bu: e.matmul(B[bu][:, :], lhsT=WS[su_][:, c * 128:(c + 1) * 128], rhs=hT[:, c, :],
                                                                           start=(c == 0), stop=(c == NCH - 1)),
                                    [hup, bank_free[bu]] if c == 0 else [], signal=(c == NCH - 1))
                    done_piece(g, ig, hm1)
                    done_piece(g, iu, hm2)
                    tb = f % 2
                    a1 = ACT.op(lambda e, tb=tb, bg=bg: e.activation(thb[tb][:, :], B[bg][:, :], AF.Tanh, scale=0.5), [hm1])
                    a2 = DVE.op(lambda e, tb=tb, bg=bg: e.scalar_tensor_tensor(t2b[tb][:, :], thb[tb][:, :], 1.0, B[bg][:, :], op0=ALU.add, op1=ALU.mult), [a1])
                    bank_free[bg] = a2
                    a3 = DVE.op(lambda e, tb=tb, bu=bu, f=f: e.scalar_tensor_tensor(hid[:, f, :], t2b[tb][:, :], 0.5, B[bu][:, :], op0=ALU.mult, op1=ALU.mult),
                                [a2, hm2, hid_free if f == 0 else None])
                    bank_free[bu] = a3
                    hid_w = a3
                hT_free = hm2
                hm = None
                for f in range(NF):
                    ip = NCH + 2 * NF + f
                    sd_, hdp = use_piece(g, ip)
                    for i in range(4):
                        for half in range(2):
                            bk = i * 2 + half
                            hm = PE.op(lambda e, f=f, i=i, half=half, bk=bk, sd_=sd_: e.matmul(B[bk][:, :], lhsT=hid[:, f, i * 128:(i + 1) * 128],
                                                                                              rhs=WS[sd_][:, half * 512:(half + 1) * 512],
                                                                                              start=(f == 0), stop=(f == NF - 1)),
                                       [hdp, hid_w, bank_free[bk]] if f == 0 else ([hdp] if (i == 0 and half == 0) else []),
                                       signal=(i == 3 and half == 1))
                    done_piece(g, ip, hm)
                hid_free = hm
                h_x2 = []
                for i in range(4):
                    for half in range(2):
                        bk = i * 2 + half
                        ha = DVE.op(lambda e, i=i, half=half, bk=bk: e.tensor_tensor(x1[:, i, half * 512:(half + 1) * 512], B[bk][:, :],
                                                                                   x1[:, i, half * 512:(half + 1) * 512], op=ALU.add), [hm])
                        bank_free[bk] = ha
                    h_x2.append(ha)
                h_hT = rms_group(gPn, h_x2, hT_free)
                base = NCH + 3 * NF
                pgp = [use_piece(g, base + c) for c in range(NCH)]
                ppp = [use_piece(g, base + NCH + c) for c in range(2)]
                hm = None
                hh2 = None
                for i in range(4):
                    for half in range(2):
                        n_ = i * 2 + half
                        bkA = (n_ % 4) * 2
                        bkB = bkA + 1
                        hmA = None
                        for c in range(NCH):
                            hmA = PE.op(lambda e, c=c, i=i, half=half, bkA=bkA: e.matmul(B[bkA][:, :], lhsT=hT[:, c, i * 128:(i + 1) * 128],
                                                                                       rhs=WS[pgp[c][0]][:, half * 512:(half + 1) * 512],
                                                                                       start=(c == 0), stop=(c == NCH - 1)),
                                        [pgp[c][1], h_hT, bank_free[bkA]], signal=(c == NCH - 1))
                        hmB = None
                        for c in range(2):
                            hmB = PE.op(lambda e, c=c, i=i, half=half, bkB=bkB: e.matmul(B[bkB][:, :], lhsT=pT[:, c, i * 128:(i + 1) * 128],
                                                                                       rhs=WS[ppp[c][0]][:, half * 512:(half + 1) * 512],
                                                                                       start=(c == 0), stop=(c == 1)),
                                        [ppp[c][1], h_pT, bank_free[bkB]], signal=(c == 1))
                        hm = hmB
                        a1 = ACT.op(lambda e, i=i, half=half, bkA=bkA: e.activation(sgt[:, i, half * 512:(half + 1) * 512], B[bkA][:, :], AF.Tanh, scale=0.5),
                                    [hmA, sgt_free if (i == 0 and half == 0) else None])
                        bank_free[bkA] = a1
                        b0 = DVE.op(lambda e, i=i, half=half: e.tensor_scalar(sgt[:, i, half * 512:(half + 1) * 512], sgt[:, i, half * 512:(half + 1) * 512],
                                                                             0.5, 0.5, op0=ALU.mult, op1=ALU.add), [a1])
                        b1 = DVE.op(lambda e, i=i, half=half, bkB=bkB: e.tensor_tensor(sgt[:, i, half * 512:(half + 1) * 512], sgt[:, i, half * 512:(half + 1) * 512],
                                                                                     B[bkB][:, :], op=ALU.mult), [b0, hmB])
                        bank_free[bkB] = b1
                        hh2 = DVE.op(lambda e, i=i, half=half: e.tensor_tensor(sgt[:, i, half * 512:(half + 1) * 512], sgt[:, i, half * 512:(half + 1) * 512],
                                                                             x1[:, i, half * 512:(half + 1) * 512], op=ALU.add), [b1])
                    h_out_last = SP.dma(out_own[g * 512 + i * 128: g * 512 + (i + 1) * 128, :], sgt[:, i, :], ds_out, deps=[hh2])
                for c in range(NCH + 2):
                    done_piece(g, base + c, hm)
                hT_free = hm
                pT_free = hm
                x1_free = hh2
                sgt_free = h_out_last
            for E_ in (SP, POOL, ACT, DVE, PE):
                E_.wait([h_out_last])
            print("SBUF remaining (C):", nc.sbuf_bytes_remaining)

            flush()
    return nc


def host_consts(NG, core):
    NT = NG * 512
    S = NCORES * NT
    t0 = core * NT
    bf = ml_dtypes.bfloat16
    identf = np.eye(128, dtype=np.float32)
    bones = np.zeros((128, 128), np.float32)
    bones[:64, :64] = 1.0
    bones[64:, 64:] = 1.0
    kpos = (np.arange(S) + t0) % S
    kaug = np.zeros((NH, NG, 4, S), np.float32)
    qaug = np.zeros((NH, 4, 512), np.float32)
    bdiag = np.zeros((NH, 128, 896), np.float32)
    qi = np.arange(512)
    for h in range(NH):
        s8 = 8.0 * SLOPES[h]
        qaug[h, 0] = 1.0
        qaug[h, 1] = 1.0
        qaug[h, 2] = -s8 * 128.0 * (qi // 128)
        qaug[h, 3] = -s8 * (qi % 128)
        ki = np.arange(128)[:, None]
        xx = np.arange(896)[None, :]
        bdiag[h] = -s8 * np.abs(xx - 384 - ki)
        for g in range(NG):
            q0 = t0 + g * 512
            sign = np.where(kpos < q0, 1.0, np.where(kpos >= q0 + 512, -1.0, 0.0))
            rel = kpos - q0
            blk = np.floor_divide(rel, 128)
            rr = rel - blk * 128
            kaug[h, g, 0] = sign * s8 * 128.0 * blk
            kaug[h, g, 1] = sign * s8 * rr
            kaug[h, g, 2] = sign
            kaug[h, g, 3] = sign
    for a in (kaug, qaug):
        assert np.array_equal(a.astype(bf).astype(np.float32), a)
    return dict(c_identb=identf.astype(bf), c_identf=identf, c_bones=bones.astype(bf),
                c_kaug=kaug.astype(bf), c_qaug=qaug.astype(bf), c_bdiag=bdiag)


_PROG_CACHE = {}


def run(inputs, NG):
    NT = NG * 512
    S = NCORES * NT
    x = np.ascontiguousarray(np.asarray(inputs["x"], np.float32).reshape(S, D))
    p = np.ascontiguousarray(np.asarray(inputs["p"], np.float32).reshape(S, PLE))
    if NG not in _PROG_CACHE:
        _PROG_CACHE[NG] = build_program(NG)
    nc = _PROG_CACHE[NG]
    sq = lambda k: np.ascontiguousarray(np.asarray(inputs[k], np.float32)[0])
    shared = {k: sq(k) for k in ("w_in", "w_out", "w_gate", "w_up", "w_down", "w_ple_gate", "w_ple_proj",
                                 "lambda_q1", "lambda_k1", "lambda_q2", "lambda_k2", "conv_w")}
    fm = lambda k: np.ascontiguousarray(sq(k).reshape(-1, 128).T)
    for k in ("attn_norm", "ffn_norm", "ple_norm", "conv_b", "conv_ln_g", "conv_ln_b"):
        shared[k] = fm(k)
    for k in ("q_norm", "k_norm"):
        shared[k] = np.ascontiguousarray(np.concatenate([sq(k), sq(k)]).reshape(128, 1))
    shared["head_norm"] = np.ascontiguousarray(sq("head_norm").reshape(128, 1))
    xpad = np.concatenate([np.zeros((16, D), np.float32), x, np.zeros((16, D), np.float32)], axis=0)
    in_maps = []
    for c in range(NCORES):
        t0 = c * NT
        m = dict(shared)
        m["xr"] = np.ascontiguousarray(np.roll(x, -t0, axis=0))
        m["xhalo"] = np.ascontiguousarray(np.concatenate([xpad[t0:t0 + 16], xpad[t0 + NT + 16:t0 + NT + 32]], axis=0))
        m["p_own"] = np.ascontiguousarray(p[t0:t0 + NT])
        m.update(host_consts(NG, c))
        in_maps.append(m)
    res = run_bass_kernel_spmd(nc, in_maps, core_ids=list(range(NCORES)))
    out = np.concatenate([np.asarray(r["out_own"], np.float32) for r in res.results], axis=0)
    return out.reshape(1, S, D)


def kernel(**inputs):
    return run(inputs, 4)
```

```python
import numpy as np
import ml_dtypes
from contextlib import ExitStack

import concourse.bass as bass
import concourse.mybir as mybir
from concourse.bass_utils import run_bass_kernel_spmd

F32 = mybir.dt.float32
BF16 = mybir.dt.bfloat16
AF = mybir.ActivationFunctionType
ALU = mybir.AluOpType
AX = mybir.AxisListType

NCORES = 8
D = 1024
NCH = D // 128
NH = 4
DH = 64
FF = 2816
NF = FF // 128
PLE = 256
CK = 31
EPS = 1e-6
SLOPES = [2.0 ** (-8.0 * (h + 1) / NH) for h in range(NH)]
LAM_INIT = 0.8 - 0.6 * 1.0
CHK = 16
NSLOT = 3
NPT = 4
NWS = 14
RSD = 768
WIN_BLOCKS = [5, 18, None, None]


class H:
    __slots__ = ("key", "sem", "val")

    def __init__(self, key, sem, val):
        self.key, self.sem, self.val = key, sem, val


class Eng:
    def __init__(self, name, sem):
        self.name, self.sem = name, sem
        self.count = 0
        self.waited = {}
        self.ops = []

    def _waits(self, deps):
        for d in deps:
            if d is None:
                continue
            if isinstance(d, (list, tuple)):
                self._waits(d)
                continue
            if self.waited.get(d.key, 0) < d.val:
                self.waited[d.key] = d.val
                self.ops.append(("wait", d.sem, d.val))

    def op(self, fn, deps=(), signal=True):
        self._waits(deps)
        if signal:
            self.count += 1
            self.ops.append(("ins", fn, self.sem, 1))
            return H(self.name, self.sem, self.count)
        self.ops.append(("ins", fn, None, 0))
        return None

    def dma(self, out, in_, dsem, deps=(), **kw):
        self._waits(deps)
        dsem.total += 16
        self.ops.append(("ins", (lambda e, o=out, i=in_, k=kw: e.dma_start(out=o, in_=i, **k)), dsem.sem, 16))
        return H(dsem.name, dsem.sem, dsem.total)

    def wait(self, deps):
        self._waits(deps)

    def replay(self, e):
        for rec in self.ops:
            if rec[0] == "wait":
                e.wait_ge(rec[1], rec[2])
            else:
                ins = rec[1](e)
                if rec[2] is not None:
                    ins.then_inc(rec[2], rec[3])


class DSem:
    def __init__(self, name, sem):
        self.name, self.sem, self.total = name, sem, 0


def build_program(NG):
    NT = NG * 512
    S = NCORES * NT
    NGRP = S // 512
    NKB = S // 128
    NCHK = NKB // CHK
    UW = NT + 32

    nc = bass.Bass("TRN2", target_bir_lowering=False)

    def din(name, shape, dt=F32):
        return nc.dram_tensor(name, list(shape), dt, kind="ExternalInput").ap()

    xr = din("xr", [S, D])
    xhalo = din("xhalo", [32, D])
    p_own = din("p_own", [NT, PLE])
    w_in = din("w_in", [D, 2560])
    w_out = din("w_out", [D, D])
    w_gate = din("w_gate", [D, FF])
    w_up = din("w_up", [D, FF])
    w_down = din("w_down", [FF, D])
    w_pg = din("w_ple_gate", [D, D])
    w_pp = din("w_ple_proj", [PLE, D])
    attn_norm = din("attn_norm", [128, NCH])
    ffn_norm = din("ffn_norm", [128, NCH])
    ple_norm = din("ple_norm", [128, NCH])
    q_norm = din("q_norm", [128, 1])
    k_norm = din("k_norm", [128, 1])
    lq1 = din("lambda_q1", [DH]); lk1 = din("lambda_k1", [DH])
    lq2 = din("lambda_q2", [DH]); lk2 = din("lambda_k2", [DH])
    head_norm = din("head_norm", [128, 1])
    conv_w = din("conv_w", [CK, 512])
    conv_b = din("conv_b", [128, 4])
    ln_g = din("conv_ln_g", [128, 4])
    ln_b = din("conv_ln_b", [128, 4])
    c_identb = din("c_identb", [128, 128], BF16)
    c_identf = din("c_identf", [128, 128])
    c_bones = din("c_bones", [128, 128], BF16)
    c_kaug = din("c_kaug", [NH, NG, 4, S], BF16)
    c_qaug = din("c_qaug", [NH, 4, 512], BF16)
    c_bdiag = din("c_bdiag", [NH, 128, 896])

    out_own = nc.dram_tensor("out_own", [NT, D], F32, kind="ExternalOutput").ap()

    kT1 = nc.dram_tensor("kT1", [NH, 64, S], BF16).ap()
    kT2 = nc.dram_tensor("kT2", [NH, 64, S], BF16).ap()
    vdr = nc.dram_tensor("vdr", [NH, 128, NKB, 128], BF16).ap()
    NPIECE = NCH + 3 * NF + NCH + 2
    wsc = nc.dram_tensor("wsc", [NPIECE, 128, 1024], BF16).ap()

    es = ExitStack()
    with es:
        def sb(name, shape, dt=F32, stack=es):
            return stack.enter_context(nc.sbuf_tensor(name, list(shape), dt))

        def ps(name, shape, dt=F32, stack=es):
            return stack.enter_context(nc.psum_tensor(name, list(shape), dt))

        def newsem(name):
            return es.enter_context(nc.semaphore(name))

        PE = Eng("pe", newsem("s_pe"))
        ACT = Eng("act", newsem("s_act"))
        DVE = Eng("dve", newsem("s_dve"))
        POOL = Eng("pool", newsem("s_pool"))
        SP = Eng("sp", newsem("s_sp"))
        _dsn = [0]
        ENGS = (PE, ACT, DVE, POOL, SP)

        def flush():
            with nc.allow_non_contiguous_dma("small param loads"):
                with nc.Block() as block:
                    @block.tensor
                    def _(e):
                        PE.replay(e)

                    @block.scalar
                    def _(e):
                        ACT.replay(e)

                    @block.vector
                    def _(e):
                        DVE.replay(e)

                    @block.gpsimd
                    def _(e):
                        POOL.replay(e)

                    @block.sync
                    def _(e):
                        SP.replay(e)
            for E_ in ENGS:
                E_.ops = []

        def dsem(name):
            _dsn[0] += 1
            return DSem(name, newsem(name))

        def rsqrt_act(out_ap, in_ap, deps):
            h1_ = ACT.op(lambda e: e.activation(out_ap, in_ap, AF.Ln), deps)
            return ACT.op(lambda e: e.activation(out_ap, out_ap, AF.Exp, scale=-0.5), [h1_])

        identb = sb("identb", [128, 128], BF16)
        identf = sb("identf", [128, 128])
        bones = sb("bones", [128, 128], BF16)
        onesb = sb("onesb", [128, 128], BF16)
        onesf = sb("onesf", [128, 128])
        negh = sb("negh", [128, 512])
        gA = sb("gA", [128, NCH]); gFn = sb("gFn", [128, NCH]); gPn = sb("gPn", [128, NCH])
        gq2 = sb("gq2", [128, 1]); gk2 = sb("gk2", [128, 1])
        hn8 = sb("hn8", [128, 1])
        cw_sb = sb("cw_sb", [CK, 512])
        cwT = sb("cwT", [128, 4, 32])
        cb4 = sb("cb4", [128, 4]); lg4 = sb("lg4", [128, 4]); lb4 = sb("lb4", [128, 4])
        lamt = sb("lamt", [128, 4, DH])
        lamw = sb("lamw", [128, 8])
        neglam = sb("neglam", [128, 1])
        conv_o = sb("conv_o", [128, 4, NT], BF16)
        attn_o = sb("attn_o", [128, NH, NT], BF16)

        ds_c = dsem("d_const")
        with nc.allow_non_contiguous_dma("small param loads"):
            hc = []
            hc.append(SP.dma(identb[:], c_identb, ds_c))
            hc.append(SP.dma(identf[:], c_identf, ds_c))
            hc.append(SP.dma(bones[:], c_bones, ds_c))
            hc.append(SP.dma(gA[:], attn_norm, ds_c))
            hc.append(SP.dma(gFn[:], ffn_norm, ds_c))
            hc.append(SP.dma(gPn[:], ple_norm, ds_c))
            hc.append(SP.dma(gq2[:], q_norm, ds_c))
            hc.append(SP.dma(gk2[:], k_norm, ds_c))
            hc.append(SP.dma(hn8[:], head_norm, ds_c))
            hc.append(SP.dma(cw_sb[:], conv_w, ds_c))
            hc.append(SP.dma(cb4[:], conv_b, ds_c))
            hc.append(SP.dma(lg4[:], ln_g, ds_c))
            hc.append(SP.dma(lb4[:], ln_b, ds_c))
            for i, v in enumerate((lq1, lk1, lq2, lk2)):
                hc.append(SP.dma(lamt[:, i, :], v.rearrange("(o d) -> o d", o=1).partition_broadcast(128), ds_c))
        h_const = hc[-1]

        h_m1 = POOL.op(lambda e: e.memset(onesb[:], 1.0))
        h_m2 = POOL.op(lambda e: e.memset(onesf[:], 1.0))
        h_m3 = POOL.op(lambda e: e.memset(negh[:], -0.5))
        h_init = h_m3

        h = DVE.op(lambda e: e.tensor_tensor(lamt[:, 0, :], lamt[:, 0, :], lamt[:, 1, :], op=ALU.mult), [h_const])
        h = DVE.op(lambda e: e.tensor_tensor(lamt[:, 2, :], lamt[:, 2, :], lamt[:, 3, :], op=ALU.mult), [h])
        h = DVE.op(lambda e: e.reduce_sum(lamw[:, 0:1], lamt[:, 0, :], axis=AX.X), [h])
        h = DVE.op(lambda e: e.reduce_sum(lamw[:, 1:2], lamt[:, 2, :], axis=AX.X), [h])
        h = ACT.op(lambda e: e.activation(lamw[:, 2:4], lamw[:, 0:2], AF.Exp), [h])
        h = DVE.op(lambda e: e.tensor_tensor(lamw[:, 4:5], lamw[:, 3:4], lamw[:, 2:3], op=ALU.subtract), [h])
        h = DVE.op(lambda e: e.tensor_scalar(neglam[:], lamw[:, 4:5], -LAM_INIT, None, op0=ALU.add), [h])
        h_hn = DVE.op(lambda e: e.tensor_scalar(hn8[:], hn8[:], 1.0 - LAM_INIT, None, op0=ALU.mult), [h_const])
        h_lam = h

        esAB = ExitStack()
        with esAB:
            Q1 = sb("Q1", [128, NH, NT], BF16, esAB)
            Q2 = sb("Q2", [128, NH, NT], BF16, esAB)
            uT = sb("uT", [128, 4, UW], BF16, esAB)

            ds_qa = dsem("d_qaug")
            hq = []
            for hh in range(NH):
                for g in range(NG):
                    hq.append(SP.dma(Q1[64:68, hh, g * 512:(g + 1) * 512], c_qaug[hh], ds_qa))
                    hq.append(SP.dma(Q2[64:68, hh, g * 512:(g + 1) * 512], c_qaug[hh], ds_qa))
            h_qaug = hq[-1]

            esA = ExitStack()
            with esA:
                w_in_sb = sb("w_in_sb", [128, NCH, 2560], BF16, esA)
                NXT = 4
                xt = [sb(f"xt{i}", [128, D], F32, esA) for i in range(NXT)]
                xb = [sb(f"xb{i}", [128, D], BF16, esA) for i in range(2)]
                junk = sb("junk", [128, D], BF16, esA)
                xT = [sb(f"xT{i}", [128, NCH, 512], BF16, esA) for i in range(2)]
                sq = [sb(f"sq{i}", [128, 512], BF16, esA) for i in range(2)]
                kst = [sb(f"kst{i}", [128, NH, 512], BF16, esA) for i in range(1)]
                vst = [sb(f"vst{i}", [128, NH, 4, 128], BF16, esA) for i in range(1)]
                Rrow = [sb(f"Rrow{i}", [128, 512], F32, esA) for i in range(2)]
                Erow = [sb(f"Erow{i}", [128, 512], F32, esA) for i in range(2)]
                t1 = [sb(f"t1_{i}", [128, 512], F32, esA) for i in range(2)]
                rk = [sb(f"rk{i}", [128, 512], F32, esA) for i in range(2)]
                stq = [sb(f"stq{i}", [128, 3, 4], F32, esA) for i in range(2)]
                q2st = sb("q2st", [128, NH, 512], BF16, esA)
                ds_q2 = dsem("d_q2")
                st_q2free = [None]
                dgR = [sb(f"dgR{i}", [128, 128], F32, esA) for i in range(2)]
                dgE = [sb(f"dgE{i}", [128, 128], F32, esA) for i in range(4)]
                cva = [sb(f"cva{i}", [128, 512], F32, esA) for i in range(1)]
                cvb = [sb(f"cvb{i}", [128, 512], F32, esA) for i in range(1)]

                ptr = [ps(f"ptr{i}", [128, D], BF16, esA) for i in range(2)]
                REp = ps("REp", [128, 512], F32, esA)
                kps = [ps(f"kps{i}", [128, 512], F32, esA) for i in range(2)]
                ssp = ps("ssp", [128, 512], F32, esA)
                vps = [ps(f"vps{i}", [128, 512], F32, esA) for i in range(2)]

                wblocks = [(768, 1024), (1280, 1536), (512, 768), (1024, 1280), (0, 512), (1536, 2048), (2048, 2560)]
                h_wblk = []
                for bi_, (c0_, c1_) in enumerate(wblocks):
                    dsw = dsem(f"d_win{bi_}")
                    h_wblk.append(POOL.dma(w_in_sb[:, :, c0_:c1_], w_in[:, c0_:c1_].rearrange("(c p) n -> p c n", p=128), dsw))

                def hw_cols(a, b_):
                    return [h_wblk[i] for i, (c0_, c1_) in enumerate(wblocks) if c0_ < b_ and a < c1_]
                h_win = None

                hcw = None
                for cc in range(4):
                    hm = PE.op(lambda e, cc=cc: e.matmul(ssp[:, 0:CK], lhsT=cw_sb[0:CK, cc * 128:(cc + 1) * 128],
                                                         rhs=identf[0:CK, 0:CK], start=True, stop=True),
                               [h_const, hcw])
                    hcw = DVE.op(lambda e, cc=cc: e.tensor_copy(cwT[:, cc, 0:CK], ssp[:, 0:CK]), [hm])
                h_cwT = hcw

                ds_x = [dsem(f"d_x{i}") for i in range(NXT)]
                ds_kst = [dsem(f"d_kst{i}") for i in range(1)]
                ds_vst = [dsem(f"d_vst{i}") for i in range(1)]

                state = dict(xt_free=[None] * NXT, xb_free=[None] * 2, ptr_free=[None, None], xT_free=[None] * 2,
                             st_free=[None] * 2, dg_free=[None] * 4, dgR_free=[None] * 2, Rrow_free=None, RE_free=None, rows_free=[None] * 2,
                             kps_free=[None] * 2, ssp_free=h_cwT, sq_free=[None] * 2, t1_free=[None] * 2,
                             rk_free=[None] * 2, kst_free=[None] * 2, vps_free=[None] * 2, vst_free=[None] * 2,
                             cva_free=[None] * 2, cvb_free=[None] * 2, tile_no=0, grp_no=0, kq_no=0, v_no=0,
                             cv_no=0)
                st_ = state

                def _nheads(gi_):
                    n = 0
                    for hh in range(NH):
                        wb = WIN_BLOCKS[hh]
                        if wb is None or 4 * NG + 2 * wb >= NKB:
                            n += 1
                        elif any((j <= 4 * NG - 1 + wb) or (j >= NKB - wb) for j in range(4 * gi_, 4 * gi_ + 4)):
                            n += 1
                    return n
                gorder = sorted(range(NG, NGRP), key=lambda gi_: (_nheads(gi_), gi_)) + list(range(NG))
                tile_srcs = []
                for gi_ in gorder:
                    for i_ in range(4):
                        tile_srcs.append((xr[gi_ * 512 + i_ * 128: gi_ * 512 + (i_ + 1) * 128, :], 128))
                tile_srcs.append((xhalo[:, :], 32))
                ld_h = {}
                st_["loads"] = 0

                def issue_load():
                    tn = st_["loads"]
                    if tn >= len(tile_srcs):
                        return
                    src, rows_ = tile_srcs[tn]
                    xs = tn % NXT
                    ld_h[tn] = SP.dma(xt[xs][0:rows_, :], src, ds_x[xs], deps=[st_["xt_free"][xs]])
                    st_["loads"] += 1

                for _ in range(NXT):
                    issue_load()

                def fe_begin(ntile, rows, need_R):
                    gb = st_["grp_no"] % 2
                    st_["grp_no"] += 1
                    return dict(gb=gb, ntile=ntile, rows=rows, need_R=need_R, h_sqs=[], casts={}, tns=[], h_xT=None)

                def fe_cast(cx, i):
                    rows = cx["rows"]
                    tn = cx["tns"][i]
                    xs = tn % NXT; bs = tn % 2
                    h_cast = DVE.op(lambda e, xs=xs, bs=bs: e.tensor_copy(xb[bs][0:rows, :], xt[xs][0:rows, :]),
                                    [ld_h[tn], st_["xb_free"][bs]])
                    cx["casts"][i] = h_cast

                def fe_tile(cx, i):
                    gb, rows, ntile = cx["gb"], cx["rows"], cx["ntile"]
                    if i == 0:
                        for k in range(ntile):
                            cx["tns"].append(st_["tile_no"]); st_["tile_no"] += 1
                    tn = cx["tns"][i]
                    xs = tn % NXT; bs = tn % 2; pb_ = tn % 2
                    h_ld = ld_h[tn]
                    h_sq = ACT.op(lambda e, xs=xs, i=i: e.activation(junk[0:rows, :], xt[xs][0:rows, :], AF.Square,
                                                                   accum_out=stq[gb][0:rows, 0, i:i + 1]),
                                  [h_ld, st_["st_free"][gb] if i == 0 else None])
                    cx["h_sqs"].append(h_sq)
                    if i == 0:
                        fe_cast(cx, 0)
                    h_cast = cx["casts"][i]
                    st_["xt_free"][xs] = [h_sq, h_cast]
                    issue_load()
                    hp = None
                    for c in range(NCH):
                        hp = PE.op(lambda e, c=c, bs=bs, pb_=pb_: e.transpose(ptr[pb_][:, c * 128:c * 128 + rows],
                                                                             xb[bs][0:rows, c * 128:(c + 1) * 128],
                                                                             identb[0:rows, 0:rows]),
                                   [h_cast, st_["ptr_free"][pb_], h_const] if c == 0 else [], signal=(c == NCH - 1))
                    st_["xb_free"][bs] = hp
                    if i + 1 < ntile:
                        fe_cast(cx, i + 1)
                    h_ev = DVE.op(lambda e, i=i, pb_=pb_: e.tensor_tensor(
                        xT[gb][:, :, i * rows:(i + 1) * rows],
                        ptr[pb_][:, :].rearrange("p (c t) -> p c t", c=NCH)[:, :, 0:rows],
                        gA[:, :].unsqueeze(2).to_broadcast([128, NCH, rows]), op=ALU.mult),
                        [hp, st_["xT_free"][gb] if i == 0 else None])
                    st_["ptr_free"][pb_] = h_ev
                    cx["h_xT"] = h_ev

                def fe_stats_a(cx):
                    gb, rows, ntile = cx["gb"], cx["rows"], cx["ntile"]
                    h1 = DVE.op(lambda e: e.tensor_scalar(stq[gb][0:rows, 1, 0:ntile], stq[gb][0:rows, 0, 0:ntile], 1.0 / D, EPS,
                                                          op0=ALU.mult, op1=ALU.add), cx["h_sqs"])
                    h2 = rsqrt_act(stq[gb][0:rows, 2, 0:ntile], stq[gb][0:rows, 1, 0:ntile], [h1])
                    cx["h1"], cx["h2"] = h1, h2
                    cx["h3s"] = []
                    for i in range(ntile):
                        h3 = DVE.op(lambda e, i=i: e.tensor_scalar(dgE[i][0:rows, 0:rows], identf[0:rows, 0:rows],
                                                                   stq[gb][0:rows, 1, i:i + 1], EPS, op0=ALU.mult, op1=ALU.mult),
                                    [h1, st_["dg_free"][i], h_const])
                        cx["h3s"].append(h3)

                def fe_stats_b(cx):
                    gb, rows, ntile, need_R = cx["gb"], cx["rows"], cx["ntile"], cx["need_R"]
                    N = ntile * rows
                    h1, h2 = cx["h1"], cx["h2"]
                    rst_handles = [(i, h2) for i in range(ntile)]
                    hE = None
                    for i in range(ntile):
                        hE = PE.op(lambda e, i=i: e.matmul(REp[:, i * rows:(i + 1) * rows],
                                                           lhsT=onesf[0:rows, :], rhs=dgE[i][0:rows, 0:rows],
                                                           start=True, stop=True),
                                   [cx["h3s"][i], st_["RE_free"] if i == 0 else None, h_init])
                        st_["dg_free"][i] = hE
                    hEr = ACT.op(lambda e, gb=gb: e.activation(Erow[gb][:, 0:N], REp[:, 0:N], AF.Copy), [hE, st_["rows_free"][gb], st_["rows_last_t1"][gb]])
                    st_["RE_free"] = hEr
                    hRr = None
                    if need_R:
                        hR = None
                        for i in range(ntile):
                            ds_ = i % 2
                            h4 = DVE.op(lambda e, i=i, ds_=ds_: e.tensor_scalar(dgR[ds_][0:rows, 0:rows], identf[0:rows, 0:rows],
                                                                               stq[gb][0:rows, 2, i:i + 1], None, op0=ALU.mult),
                                        [h2, st_["dgR_free"][ds_]])
                            hR = PE.op(lambda e, i=i, ds_=ds_: e.matmul(REp[:, i * rows:(i + 1) * rows],
                                                                       lhsT=onesf[0:rows, :], rhs=dgR[ds_][0:rows, 0:rows],
                                                                       start=True, stop=True), [h4, hEr if i == 0 else None])
                            st_["dgR_free"][ds_] = hR
                        hRr = ACT.op(lambda e, gb=gb: e.activation(Rrow[gb][:, 0:N], REp[:, 0:N], AF.Copy), [hR])
                        st_["RE_free"] = hRr
                    st_["st_free"][gb] = [hE, h2]
                    return gb, cx["h_xT"], hEr, hRr, rst_handles

                def fe_stats(cx):
                    fe_stats_a(cx)
                    return fe_stats_b(cx)

                def pn_stage1(gb, N, col0, h_xT):
                    kn = st_["kq_no"]; st_["kq_no"] += 1
                    b = kn % 2
                    hm = None
                    for c in range(NCH):
                        hm = PE.op(lambda e, c=c, b=b: e.matmul(kps[b][:, 0:N], lhsT=w_in_sb[:, c, col0:col0 + 128],
                                                                rhs=xT[gb][:, c, 0:N], start=(c == 0), stop=(c == NCH - 1)),
                                   [h_xT, hw_cols(col0, col0 + 128), st_["kps_free"][b]] if c == 0 else [], signal=(c == NCH - 1))
                    hs = ACT.op(lambda e, b=b: e.activation(sq[b][:, 0:N], kps[b][:, 0:N], AF.Square), [hm, st_["sq_free"][b]])
                    return dict(b=b, hm=hm, hs=hs, N=N, gb=gb)

                def pn_stage2ss(cx):
                    b, hs, N = cx["b"], cx["hs"], cx["N"]
                    hss = PE.op(lambda e, b=b: e.matmul(ssp[:, 0:N], lhsT=bones[:, :], rhs=sq[b][:, 0:N], start=True, stop=True),
                                [hs, st_["ssp_free"], h_const])
                    st_["sq_free"][b] = hss
                    cx["hss"] = hss

                def pn_stage2n(cx, hEr):
                    b, N, gb = cx["b"], cx["N"], cx["gb"]
                    ht = DVE.op(lambda e, b=b: e.scalar_tensor_tensor(t1[b][:, 0:N], ssp[:, 0:N], 1.0 / DH, Erow[gb][:, 0:N],
                                                                      op0=ALU.mult, op1=ALU.add),
                                [cx["hss"], hEr, st_["t1_free"][b]])
                    st_["ssp_free"] = ht
                    st_["rows_last_t1"][gb] = ht
                    hr = rsqrt_act(rk[b][:, 0:N], t1[b][:, 0:N], [ht, st_["rk_free"][b]])
                    st_["t1_free"][b] = hr
                    cx["hr"] = hr

                def pn_stage2a(cx, hEr):
                    pn_stage2ss(cx)
                    pn_stage2n(cx, hEr)

                def pn_stage2b(cx, writer):
                    b = cx["b"]
                    hw = writer(b, cx["hr"], cx["hm"])
                    st_["rk_free"][b] = hw
                    st_["kps_free"][b] = hw
                    return hw

                def pn_stage2(cx, hEr, writer):
                    pn_stage2a(cx, hEr)
                    return pn_stage2b(cx, writer)

                def proj_norm(gb, N, col0, h_xT, hEr, gvec, writer):
                    return pn_stage2(pn_stage1(gb, N, col0, h_xT), hEr, writer)

                st_["pending"] = []
                st_["rows_last_t1"] = [None, None]
                st_["pending_a"] = []
                st_["deferred_finish"] = []

                def run_pending_a():
                    while st_["pending_a"]:
                        st_["pending_a"].pop(0)()

                def run_pending():
                    run_pending_a()
                    while st_["pending"]:
                        st_["pending"].pop(0)()
                    while st_["deferred_finish"]:
                        st_["deferred_finish"].pop(0)()

                def heads_needed(gi):
                    res = []
                    for hh in range(NH):
                        wb = WIN_BLOCKS[hh]
                        if wb is None or 4 * NG + 2 * wb >= NKB:
                            res.append(hh)
                            continue
                        need = any((j <= 4 * NG - 1 + wb) or (j >= NKB - wb) for j in range(4 * gi, 4 * gi + 4))
                        if need:
                            res.append(hh)
                    return res

                def kv_group(gi, gb, h_xT, hEr, rst_handles):
                    sb_ = 0
                    heads = heads_needed(gi)
                    h_lo = min(heads)
                    assert heads == list(range(h_lo, NH))
                    nvh = NH - h_lo
                    hws = []
                    hvs = []
                    cxs = {}
                    first_k = [True]

                    def do_s1(hh):
                        cxs[hh] = pn_stage1(gb, 512, 512 + hh * 128, h_xT)

                    def do_s2a(hh):
                        pn_stage2ss(cxs[hh])
                        st_["pending_a"].append(lambda hh=hh: pn_stage2n(cxs[hh], hEr))

                        def wr(hh=hh):
                            def writer(b, hr, hm, hh=hh):
                                dep = st_["kst_free"][sb_] if first_k[0] else None
                                first_k[0] = False
                                return DVE.op(lambda e: e.scalar_tensor_tensor(kst[sb_][:, hh, :], kps[b][:, :], gk2[:, 0:1], rk[b][:, :],
                                                                               op0=ALU.mult, op1=ALU.mult), [hr, hm, dep])
                            hws.append(pn_stage2b(cxs[hh], writer))
                        st_["pending"].append(wr)

                    def do_v(i):
                        vn = st_["v_no"]; st_["v_no"] += 1
                        b = vn % 2
                        hm = None
                        for c in range(NCH):
                            hm = PE.op(lambda e, c=c, b=b, i=i: e.matmul(vps[b][:, 0:nvh * 128], lhsT=xT[gb][:, c, i * 128:(i + 1) * 128],
                                                                         rhs=w_in_sb[:, c, 1024 + h_lo * 128:1536], start=(c == 0), stop=(c == NCH - 1)),
                                       [h_xT, hw_cols(1024 + h_lo * 128, 1536), st_["vps_free"][b]] if c == 0 else [], signal=(c == NCH - 1))
                        st_["last_v_mm"] = hm
                        ss_, h2 = rst_handles[i]
                        hv = ACT.op(lambda e, b=b, i=i, ss_=ss_: e.activation(vst[sb_][:, h_lo:NH, i, :],
                                                                            vps[b][:, 0:nvh * 128].rearrange("p (h e) -> p h e", h=nvh),
                                                                            AF.Copy, scale=stq[gb][:, 2, ss_:ss_ + 1]),
                                    [hm, h2, st_["vst_free"][sb_] if len(hvs) == 0 else None])
                        st_["vps_free"][b] = hv
                        st_["st_free"][gb] = [st_["st_free"][gb], hv]
                        hvs.append(hv)

                    hq = {0: [], 1: [], 2: [], 3: []}
                    if len(heads) == 4:
                        for i_, hh in enumerate(heads):
                            hq[i_].append(hh)
                    elif len(heads) == 3:
                        hq[0].append(heads[0]); hq[1].append(heads[1]); hq[2].append(heads[2])
                    else:
                        hq[0].append(heads[0]); hq[2].append(heads[1])

                    def quarter(i):
                        run_pending()
                        for hh in hq[i]:
                            do_s1(hh)
                        do_v(i)
                        for hh in hq[i]:
                            do_s2a(hh)
                        if i == 3:
                            st_["xT_free"][gb] = [cxs[heads[-1]]["hm"], st_["last_v_mm"]]

                    def finish():
                        hk1 = SP.dma(kT1[h_lo:NH, :, gi * 512:(gi + 1) * 512].rearrange("h d t -> d h t"), kst[sb_][0:64, h_lo:NH, :], ds_kst[sb_], deps=hws)
                        hk2 = SP.dma(kT2[h_lo:NH, :, gi * 512:(gi + 1) * 512].rearrange("h d t -> d h t"), kst[sb_][64:128, h_lo:NH, :], ds_kst[sb_])
                        st_["kst_free"][sb_] = hk2
                        hvd = SP.dma(vdr[h_lo:NH, :, gi * 4:(gi + 1) * 4, :].rearrange("h p k e -> p h k e"), vst[sb_][:, h_lo:NH, :, :], ds_vst[sb_], deps=hvs)
                        st_["vst_free"][sb_] = hvd
                        return hk2, hvd
                    return quarter, finish

                def q_group(g, gb, h_xT, hEr):
                    hws = []
                    cxq = {}

                    def s1(hh):
                        cxq[hh] = pn_stage1(gb, 512, hh * 128, h_xT)

                    def s2a(hh):
                        pn_stage2a(cxq[hh], hEr)

                    def s2b(hh):
                        def writer(b, hr, hm, hh=hh):
                            DVE.op(lambda e: e.scalar_tensor_tensor(Q1[0:64, hh, g * 512:(g + 1) * 512], kps[b][0:64, :], gq2[0:64, 0:1],
                                                                    rk[b][0:64, :], op0=ALU.mult, op1=ALU.mult), [hr, hm], signal=False)
                            return DVE.op(lambda e: e.scalar_tensor_tensor(q2st[64:128, hh, :], kps[b][64:128, :],
                                                                           gq2[64:128, 0:1], rk[b][64:128, :], op0=ALU.mult, op1=ALU.mult),
                                          [st_q2free[0] if hh == 0 else None])
                        hws.append(pn_stage2b(cxq[hh], writer))
                    s1(0); s1(1); s2a(0); s2a(1); s2b(0); s1(2); s2b(1); s1(3); s2a(2); s2a(3); s2b(2); s2b(3)
                    hmv = SP.dma(Q2[0:64, :, g * 512:(g + 1) * 512], q2st[64:128, :, :], ds_q2, deps=[hws[-1]])
                    st_q2free[0] = hmv
                    return [hws[-1], hmv]

                def conv_in(gb, N, ucol0, h_xT, hRr):
                    hl = None
                    for cc in range(4):
                        cn = st_["cv_no"]; st_["cv_no"] += 1
                        b = cn % 2
                        ha = hg = None
                        for c in range(NCH):
                            ha = PE.op(lambda e, c=c, b=b, cc=cc: e.matmul(kps[b][:, 0:N], lhsT=w_in_sb[:, c, 1536 + cc * 128:1536 + (cc + 1) * 128],
                                                                           rhs=xT[gb][:, c, 0:N], start=(c == 0), stop=(c == NCH - 1)),
                                       [h_xT, hw_cols(1536, 2560), st_["kps_free"][b]] if c == 0 else [], signal=(c == NCH - 1))
                        for c in range(NCH):
                            hg = PE.op(lambda e, c=c, b=b, cc=cc: e.matmul(vps[b][:, 0:N], lhsT=w_in_sb[:, c, 2048 + cc * 128:2048 + (cc + 1) * 128],
                                                                           rhs=xT[gb][:, c, 0:N], start=(c == 0), stop=(c == NCH - 1)),
                                       [st_["vps_free"][b]] if c == 0 else [], signal=(c == NCH - 1))
                        h1 = DVE.op(lambda e, b=b: e.tensor_tensor(t1[b][:, 0:N], vps[b][:, 0:N], Rrow[gb][:, 0:N], op=ALU.mult),
                                    [hg, hRr, st_["t1_free"][b]])
                        st_["vps_free"][b] = h1
                        h2 = ACT.op(lambda e, b=b: e.activation(rk[b][:, 0:N], t1[b][:, 0:N], AF.Tanh, scale=0.5), [h1, st_["rk_free"][b]])
                        h4 = DVE.op(lambda e, b=b: e.scalar_tensor_tensor(t1[b][:, 0:N], rk[b][:, 0:N], 1.0, Rrow[gb][:, 0:N], op0=ALU.add, op1=ALU.mult), [h2])
                        h5 = DVE.op(lambda e, b=b, cc=cc: e.scalar_tensor_tensor(uT[:, cc, ucol0:ucol0 + N], kps[b][:, 0:N], 0.5, t1[b][:, 0:N],
                                                                                 op0=ALU.mult, op1=ALU.mult), [h4, ha])
                        st_["t1_free"][b] = h5
                        st_["rk_free"][b] = h5
                        st_["kps_free"][b] = h5
                        hl = h5
                    return hl

                h_kv_all = []
                h_q_last = None
                h_u_last = None
                cx0 = fe_begin(4, 128, gorder[0] < NG)
                for i in range(4):
                    fe_tile(cx0, i)
                pre = fe_stats(cx0)
                for oi, gi in enumerate(gorder):
                    own = gi < NG
                    if oi + 1 < NGRP:
                        ncx = fe_begin(4, 128, gorder[oi + 1] < NG)
                    else:
                        ncx = fe_begin(1, 32, True)
                    gb, h_xT, hEr, hRr, rsth = pre
                    quarter, finish = kv_group(gi, gb, h_xT, hEr, rsth)
                    for i in range(4):
                        run_pending_a()
                        if i < ncx["ntile"]:
                            fe_tile(ncx, i)
                            if i == ncx["ntile"] - 1:
                                fe_stats_a(ncx)
                        quarter(i)
                    def fin_(finish=finish):
                        hk, hv = finish()
                        h_kv_all.extend([hk, hv])
                    st_["deferred_finish"].append(fin_)
                    if own:
                        run_pending()
                        h_q_last = q_group(gi, gb, h_xT, hEr)
                        h_u_last = conv_in(gb, 512, 16 + gi * 512, h_xT, hRr)
                        st_["xT_free"][gb] = [st_["xT_free"][gb], h_q_last, h_u_last]
                        st_["rows_free"][gb] = [h_q_last, h_u_last]
                    else:
                        st_["rows_free"][gb] = st_["kps_free"][:] + st_["t1_free"][:]
                    pre = fe_stats_b(ncx)
                run_pending()
                h_kv_last = h_kv_all
                gb, h_xT, hEr, hRr, rsth = pre
                def conv_in_halo():
                    hl = None
                    for cc in range(4):
                        cn = st_["cv_no"]; st_["cv_no"] += 1
                        b = cn % 2
                        N = 32
                        ha = hg = None
                        for c in range(NCH):
                            ha = PE.op(lambda e, c=c, b=b, cc=cc: e.matmul(kps[b][:, 0:N], lhsT=w_in_sb[:, c, 1536 + cc * 128:1536 + (cc + 1) * 128],
                                                                           rhs=xT[gb][:, c, 0:N], start=(c == 0), stop=(c == NCH - 1)),
                                       [h_xT, hw_cols(1536, 2560), st_["kps_free"][b]] if c == 0 else [], signal=(c == NCH - 1))
                        for c in range(NCH):
                            hg = PE.op(lambda e, c=c, b=b, cc=cc: e.matmul(vps[b][:, 0:N], lhsT=w_in_sb[:, c, 2048 + cc * 128:2048 + (cc + 1) * 128],
                                                                           rhs=xT[gb][:, c, 0:N], start=(c == 0), stop=(c == NCH - 1)),
                                       [st_["vps_free"][b]] if c == 0 else [], signal=(c == NCH - 1))
                        h1 = DVE.op(lambda e, b=b: e.tensor_tensor(cva[0][:, 0:N], vps[b][:, 0:N], Rrow[gb][:, 0:N], op=ALU.mult),
                                    [hg, hRr, st_["cva_free"][0]])
                        st_["vps_free"][b] = h1
                        h2 = ACT.op(lambda e, b=b: e.activation(cvb[0][:, 0:N], cva[0][:, 0:N], AF.Exp, scale=-1.0), [h1, st_["cvb_free"][0]])
                        h3a = DVE.op(lambda e, b=b: e.tensor_scalar(cva[0][:, 0:N], cvb[0][:, 0:N], 1.0, None, op0=ALU.add), [h2])
                        h3 = DVE.op(lambda e, b=b: e.reciprocal(cva[0][:, 0:N], cva[0][:, 0:N]), [h3a])
                        h4 = DVE.op(lambda e, b=b: e.tensor_tensor(cvb[0][:, 0:N], cva[0][:, 0:N], Rrow[gb][:, 0:N], op=ALU.mult), [h3])
                        DVE.op(lambda e, b=b, cc=cc: e.tensor_tensor(uT[:, cc, 0:16], kps[b][:, 0:16], cvb[0][:, 0:16], op=ALU.mult),
                               [h4, ha], signal=False)
                        h5 = DVE.op(lambda e, b=b, cc=cc: e.tensor_tensor(uT[:, cc, 16 + NT:32 + NT], kps[b][:, 16:32], cvb[0][:, 16:32], op=ALU.mult), [])
                        st_["cva_free"][0] = h5
                        st_["cvb_free"][0] = h5
                        st_["kps_free"][b] = h5
                        hl = h5
                    return hl
                h_u_halo = conv_in_halo()
                h_A1_done = [h_u_halo, h_u_last, h_q_last] + h_kv_last
                for E_ in (PE, ACT, DVE, POOL, SP):
                    E_.wait(h_A1_done)
                    E_.wait([st_["kps_free"], st_["vps_free"], st_["rows_free"], st_["RE_free"], st_["ptr_free"], st_["st_free"]])
                print("SBUF remaining (A1):", nc.sbuf_bytes_remaining)
                flush()

            esA2 = ExitStack()
            with esA2:
                Dg = sb("Dg", [128, 4, CK, 128], BF16, esA2)
                yv2 = [sb(f"yv{i}", [128, 4, 512], F32, esA2) for i in range(2)]
                ysq2 = [sb(f"ysq{i}", [128, 4, 512], F32, esA2) for i in range(2)]
                mean_sb = sb("mean_sb", [128, 512], F32, esA2)
                m2 = sb("m2", [128, 512], F32, esA2)
                rs_ln = sb("rs_ln", [128, 512], F32, esA2)
                z1 = [sb(f"z1_{i}", [128, 512], F32, esA2) for i in range(2)]
                z2 = [sb(f"z2_{i}", [128, 512], F32, esA2) for i in range(2)]
                hg4 = sb("hg4", [128, 4], F32, esA2)
                hb4 = sb("hb4", [128, 4], F32, esA2)
                onesf512 = sb("onesf512", [128, 128], F32, esA2)
                yps = [ps(f"yps{i}", [128, 512], F32, esA2) for i in range(2)]
                mps = ps("mps", [128, 512], F32, esA2)
                eps_ = ps("eps_", [128, 512], F32, esA2)

                hd = None
                for cc in range(4):
                    for k in range(CK):
                        hd = DVE.op(lambda e, cc=cc, k=k: e.tensor_scalar(Dg[:, cc, k, :], identf[:, :], cwT[:, cc, k:k + 1], None, op0=ALU.mult),
                                    [h_cwT, h_A1_done] if (cc == 0 and k == 0) else [], signal=(k == CK - 1))
                h_Dg = hd
                h_a = DVE.op(lambda e: e.tensor_scalar(hg4[:], lg4[:], 0.5, None, op0=ALU.mult), [h_const])
                h_b = DVE.op(lambda e: e.tensor_scalar(hb4[:], lb4[:], 0.5, None, op0=ALU.mult), [h_const])
                h_o5 = POOL.op(lambda e: e.memset(onesf512[:], 1.0 / 512.0), [h_A1_done])
                yps_free = [None, None]
                yv_free = [None, None]
                stat_free = None
                z_free = [None, None]
                h_conv_last = None
                n_y = [0]
                hy_of = {}

                def conv_part(g):
                    yb = g % 2
                    yv, ysq = yv2[yb], ysq2[yb]
                    h_y = []
                    for cc in range(4):
                        b = n_y[0] % 2; n_y[0] += 1
                        hm = None
                        for k in range(CK):
                            hm = PE.op(lambda e, cc=cc, k=k, b=b, g=g: e.matmul(yps[b][:, :], lhsT=Dg[:, cc, k, :],
                                                                                rhs=uT[:, cc, g * 512 + k + 1: g * 512 + k + 1 + 512],
                                                                                start=(k == 0), stop=(k == CK - 1)),
                                       [h_Dg, yps_free[b]] if k == 0 else [], signal=(k == CK - 1))
                        hy = DVE.op(lambda e, cc=cc, b=b, yv=yv: e.tensor_scalar(yv[:, cc, :], yps[b][:, :], cb4[:, cc:cc + 1], None, op0=ALU.add),
                                    [hm, yv_free[yb] if cc == 0 else None])
                        yps_free[b] = hy
                        hs = ACT.op(lambda e, cc=cc, yv=yv, ysq=ysq: e.activation(ysq[:, cc, :], yv[:, cc, :], AF.Square), [hy])
                        h_y.append((hy, hs))
                    hy_of[g] = h_y

                def ln_part(g):
                    nonlocal stat_free, h_conv_last
                    yb = g % 2
                    yv, ysq = yv2[yb], ysq2[yb]
                    h_y = hy_of[g]
                    hmean = None
                    for cc in range(4):
                        hmean = PE.op(lambda e, cc=cc, yv=yv: e.matmul(mps[:, :], lhsT=onesf512[:, :], rhs=yv[:, cc, :], start=(cc == 0), stop=(cc == 3)),
                                      [h_y[cc][0], h_o5, stat_free], signal=(cc == 3))
                    hey = None
                    for cc in range(4):
                        hey = PE.op(lambda e, cc=cc, ysq=ysq: e.matmul(eps_[:, :], lhsT=onesf512[:, :], rhs=ysq[:, cc, :], start=(cc == 0), stop=(cc == 3)),
                                    [h_y[cc][1]], signal=(cc == 3))
                    h1 = ACT.op(lambda e: e.activation(mean_sb[:, :], mps[:, :], AF.Copy), [hmean, z_free])
                    h2 = DVE.op(lambda e: e.tensor_tensor(m2[:, :], mean_sb[:, :], mean_sb[:, :], op=ALU.mult), [h1, stat_free])
                    h3 = DVE.op(lambda e: e.tensor_tensor(m2[:, :], eps_[:, :], m2[:, :], op=ALU.subtract), [h2, hey])
                    h4 = DVE.op(lambda e: e.tensor_scalar(m2[:, :], m2[:, :], EPS, None, op0=ALU.add), [h3])
                    h5 = rsqrt_act(rs_ln[:, :], m2[:, :], [h4])
                    hl = None
                    for cc in range(4):
                        b = cc % 2
                        a1 = DVE.op(lambda e, cc=cc, b=b, yv=yv: e.tensor_tensor(z1[b][:, :], yv[:, cc, :], mean_sb[:, :], op=ALU.subtract), [h1, z_free[b]])
                        a2 = DVE.op(lambda e, b=b: e.tensor_tensor(z1[b][:, :], z1[b][:, :], rs_ln[:, :], op=ALU.mult), [a1, h5])
                        a3 = DVE.op(lambda e, cc=cc, b=b: e.tensor_scalar(z1[b][:, :], z1[b][:, :], hg4[:, cc:cc + 1], hb4[:, cc:cc + 1],
                                                                          op0=ALU.mult, op1=ALU.add), [a2, h_a, h_b])
                        a4 = ACT.op(lambda e, b=b: e.activation(z2[b][:, :], z1[b][:, :], AF.Tanh), [a3])
                        a5 = DVE.op(lambda e, cc=cc, b=b, g=g: e.scalar_tensor_tensor(conv_o[:, cc, g * 512:(g + 1) * 512], z2[b][:, :], 1.0, z1[b][:, :],
                                                                                      op0=ALU.add, op1=ALU.mult), [a4])
                        z_free[b] = a5
                        hl = a5
                    yv_free[yb] = hl
                    stat_free = h5
                    h_conv_last = hl

                conv_part(0)
                for g in range(NG):
                    if g + 1 < NG:
                        conv_part(g + 1)
                    ln_part(g)
                for E_ in (PE, ACT, DVE, POOL, SP):
                    E_.wait([h_conv_last])
                print("SBUF remaining (A2):", nc.sbuf_bytes_remaining)
                flush()

            esB = ExitStack()
            with esB:
                K1 = [sb(f"K1_{i}", [128, CHK * 128], BF16, esB) for i in range(NSLOT)]
                K2 = [sb(f"K2_{i}", [128, CHK * 128], BF16, esB) for i in range(NSLOT)]
                Vs = [sb(f"Vs_{i}", [128, CHK * 128], BF16, esB) for i in range(NSLOT)]
                PT = [sb(f"PT_{i}", [128, 1024], BF16, esB) for i in range(NPT)]
                dtmp = [sb(f"dtmp{i}", [128, 1024], F32, esB) for i in range(2)]
                bd = sb("bd", [128, NH, 896], F32, esB)
                oT = sb("oT", [128, NH, NT], F32, esB)
                cO = [sb(f"cO{i}", [128, 512], F32, esB) for i in range(2)]
                cl = [sb(f"cl{i}", [128, 512], F32, esB) for i in range(2)]
                accD2 = [sb(f"accD{i}", [128, RSD], F32, esB) for i in range(2)]
                Sps = [ps(f"Sps{i}", [128, 1024], F32, esB) for i in range(2)]
                Ops = [ps(f"Ops{i}", [128, 512], F32, esB) for i in range(2)]
                Lps = [ps(f"Lps{i}", [128, 512], F32, esB) for i in range(2)]

                ds_cv = dsem("d_cv")
                h_cv = None
                _pl = [("wo", c) for c in range(NCH)]
                for f in range(NF):
                    _pl.append(("wg", f)); _pl.append(("wu", f))
                _pl += [("wd", f) for f in range(NF)] + [("pg", c) for c in range(NCH)] + [("pp", c) for c in range(2)]
                for pi_, (kind, idx) in enumerate(_pl):
                    if kind == "wo":
                        src, dst = w_out[idx * 128:(idx + 1) * 128, :], wsc[pi_]
                    elif kind in ("wg", "wu"):
                        wsrc = w_gate if kind == "wg" else w_up
                        src = wsrc[:, idx * 128:(idx + 1) * 128].rearrange("(c p) n -> p c n", p=128)
                        dst = wsc[pi_].rearrange("p (c n) -> p c n", c=NCH)
                    elif kind == "wd":
                        src, dst = w_down[idx * 128:(idx + 1) * 128, :], wsc[pi_]
                    elif kind == "pg":
                        src, dst = w_pg[idx * 128:(idx + 1) * 128, :], wsc[pi_]
                    else:
                        src, dst = w_pp[idx * 128:(idx + 1) * 128, :], wsc[pi_]
                    h_cv = POOL.dma(dst, src, ds_cv)

                ds_bd = dsem("d_bd")
                h_bd = None
                for hh in range(NH):
                    h_bd = SP.dma(bd[:, hh, :], c_bdiag[hh], ds_bd, deps=[h_conv_last])
                hz = [h_conv_last for i in range(NSLOT)]
                ds_slot = [dsem(f"d_slot{i}") for i in range(NSLOT)]
                slot_free = [hz[i] for i in range(NSLOT)]
                slot_ld = [None] * NSLOT
                pt_free = [None] * NPT
                s_free = [None, None]
                dt_free = [None, None]
                ol_free = None
                c_free = None
                acc_free = [None, None]
                h_kv_ready = h_kv_last
                gstep = 0
                units = [(hh, g) for hh in range(NH) for g in range(NG)]

                def unit_blocks(hh, g):
                    wb = WIN_BLOCKS[hh]
                    if wb is None or 4 + 2 * wb >= NKB:
                        return [list(range(NKB))]
                    lo, hi = 4 * g - wb, 4 * g + 3 + wb
                    if lo < 0:
                        return [list(range(NKB + lo, NKB)), list(range(0, hi + 1))]
                    if hi >= NKB:
                        return [list(range(lo, NKB)), list(range(0, hi - NKB + 1))]
                    return [list(range(lo, hi + 1))]

                fills = []
                unit_steps = []
                for u, (hh, g) in enumerate(units):
                    stl = []
                    for run in unit_blocks(hh, g):
                        for a in range(0, len(run), CHK):
                            piece = run[a:a + CHK]
                            fi = len(fills)
                            fills.append((u, piece[0], len(piece)))
                            for kb, j in enumerate(piece):
                                stl.append((fi, kb, j, kb == len(piece) - 1))
                    unit_steps.append(stl)

                def issue_fill(fi):
                    u, j0, nkb = fills[fi]
                    hh, g = units[u]
                    s = fi % NSLOT
                    deps = [slot_free[s], h_kv_ready]
                    c0 = j0 * 128
                    wdt = nkb * 128
                    SP.dma(K1[s][0:64, 0:wdt], kT1[hh, :, c0:c0 + wdt], ds_slot[s], deps=deps)
                    SP.dma(K2[s][0:64, 0:wdt], kT2[hh, :, c0:c0 + wdt], ds_slot[s])
                    SP.dma(K1[s][64:68, 0:wdt], c_kaug[hh, g, :, c0:c0 + wdt], ds_slot[s])
                    SP.dma(K2[s][64:68, 0:wdt], c_kaug[hh, g, :, c0:c0 + wdt], ds_slot[s])
                    slot_ld[s] = SP.dma(Vs[s][:, 0:wdt], vdr[hh, :, j0:j0 + nkb, :].rearrange("p k e -> p (k e)"), ds_slot[s])

                nissued = 0
                for _ in range(min(NSLOT, len(fills))):
                    issue_fill(nissued); nissued += 1

                pendq = []
                dve_q = []
                epi_state = dict(c_free=None, last=None)

                def emit_av(pd):
                    nonlocal ol_free
                    (s, kb, pslot, h_p, last_in_fill, first, last, h_rs, uinfo) = pd
                    deps = [h_p, ol_free if first else None]
                    PE.op(lambda e: e.matmul(Ops[0][:, :], lhsT=Vs[s][:, kb * 128:(kb + 1) * 128], rhs=PT[pslot][:, 0:512], start=first, stop=last),
                          deps, signal=False)
                    PE.op(lambda e: e.matmul(Ops[1][:, :], lhsT=Vs[s][:, kb * 128:(kb + 1) * 128], rhs=PT[pslot][:, 512:1024], start=first, stop=last), [], signal=False)
                    hv = PE.op(lambda e: e.matmul(Lps[1][:, RSD - 512:512], lhsT=onesb[:, :], rhs=PT[pslot][:, RSD:1024], start=first, stop=last),
                               [epi_state.get("l_free") if first else None])
                    pt_free[pslot] = [hv] + h_rs
                    if last_in_fill:
                        slot_free[s] = hv
                    if last:
                        epilogue(hv, h_rs[0], uinfo)
                    return hv

                def epilogue(hv, r1, uinfo):
                    nonlocal ol_free
                    hh, g, ab = uinfo
                    accD = accD2[ab]
                    PE.op(lambda e: e.matmul(Lps[0][:, :], lhsT=onesf[:, :], rhs=accD[:, 0:512], start=True, stop=True), [r1, epi_state.get("l_free")], signal=False)
                    hl_ = PE.op(lambda e: e.matmul(Lps[1][:, 0:RSD - 512], lhsT=onesf[:, :], rhs=accD[:, 512:RSD], start=True, stop=True), [])
                    acc_free[ab] = hl_
                    e1 = DVE.op(lambda e: e.tensor_copy(cO[0][:, :], Ops[0][:, :]), [hv, epi_state["c_free"]])
                    e3 = DVE.op(lambda e: e.tensor_copy(cO[1][:, :], Ops[1][:, :]), [])
                    e4 = DVE.op(lambda e: e.tensor_copy(cl[1][:, :], Lps[1][:, :]), [hl_])
                    e2 = DVE.op(lambda e: e.tensor_copy(cl[0][:, :], Lps[0][:, :]), [])
                    ol_free = e3
                    epi_state["l_free"] = e2
                    box = dict(h=e2)
                    dve_q.extend([None, None, None])

                    def mk(eng, fn, extra=None):
                        def run():
                            box["h"] = eng.op(fn, [box["h"], extra])
                            epi_state["last"] = box["h"]
                        return run
                    dve_q.append(mk(ACT, lambda e: e.activation(cl[0][:, :], cl[0][:, :], AF.Ln)))
                    dve_q.append(mk(ACT, lambda e: e.activation(cl[0][:, :], cl[0][:, :], AF.Exp, scale=-1.0)))
                    dve_q.append(mk(ACT, lambda e: e.activation(cl[1][:, :], cl[1][:, :], AF.Ln)))
                    dve_q.append(mk(ACT, lambda e: e.activation(cl[1][:, :], cl[1][:, :], AF.Exp, scale=-1.0)))
                    dve_q.append(mk(DVE, lambda e: e.tensor_tensor(cO[0][:, :], cO[0][:, :], cl[0][:, :], op=ALU.mult)))
                    dve_q.append(mk(DVE, lambda e: e.tensor_tensor(cO[1][:, :], cO[1][:, :], cl[1][:, :], op=ALU.mult)))

                    def fin():
                        box["h"] = DVE.op(lambda e, hh=hh, g=g: e.scalar_tensor_tensor(oT[:, hh, g * 512:(g + 1) * 512], cO[1][:, :], neglam[:, 0:1], cO[0][:, :],
                                                                                     op0=ALU.mult, op1=ALU.add), [box["h"], h_lam])
                        epi_state["c_free"] = box["h"]
                        epi_state["last"] = box["h"]
                    dve_q.append(fin)

                def drain_dve(n):
                    for _ in range(n):
                        if dve_q:
                            f_ = dve_q.pop(0)
                            if f_ is not None:
                                f_()

                for u, (hh, g) in enumerate(units):
                    stl = unit_steps[u]
                    nst = len(stl)
                    for si, (fi, kb, j, last_in_fill) in enumerate(stl):
                        s = fi % NSLOT
                        sbk = gstep % 2; pslot = gstep % NPT
                        diag = (4 * g <= j < 4 * g + 4)
                        first = (si == 0); last = (si == nst - 1)
                        if first:
                            drain_dve(100)
                        PE.op(lambda e, s=s, kb=kb, sbk=sbk, hh=hh, g=g: e.matmul(Sps[sbk][:, 0:512], lhsT=K1[s][0:68, kb * 128:(kb + 1) * 128],
                                                                              rhs=Q1[0:68, hh, g * 512:(g + 1) * 512], start=True, stop=True),
                              [slot_ld[s], s_free[sbk], h_q_last, h_qaug], signal=False)
                        h_s = PE.op(lambda e, s=s, kb=kb, sbk=sbk, hh=hh, g=g: e.matmul(Sps[sbk][:, 512:1024], lhsT=K2[s][0:68, kb * 128:(kb + 1) * 128],
                                                                                    rhs=Q2[0:68, hh, g * 512:(g + 1) * 512], start=True, stop=True), [])
                        if len(pendq) >= 2:
                            pd = pendq.pop(0)
                            emit_av(pd)
                            if pd[4] and nissued < len(fills):
                                issue_fill(nissued); nissued += 1
                        if diag:
                            x0 = 384 - 128 * (j - 4 * g)
                            DVE.op(lambda e, sbk=sbk, hh=hh, x0=x0: e.tensor_tensor(dtmp[sbk][:, 0:512], Sps[sbk][:, 0:512], bd[:, hh, x0:x0 + 512], op=ALU.add),
                                   [h_s, dt_free[sbk], h_bd], signal=False)
                            h_d = DVE.op(lambda e, sbk=sbk, hh=hh, x0=x0: e.tensor_tensor(dtmp[sbk][:, 512:1024], Sps[sbk][:, 512:1024], bd[:, hh, x0:x0 + 512], op=ALU.add), [])
                            h_p = ACT.op(lambda e, sbk=sbk, pslot=pslot: e.activation(PT[pslot][:, :], dtmp[sbk][:, :], AF.Exp, scale=0.125),
                                         [h_d, pt_free[pslot]])
                            dt_free[sbk] = h_p
                        else:
                            h_p = ACT.op(lambda e, sbk=sbk, pslot=pslot: e.activation(PT[pslot][:, :], Sps[sbk][:, :], AF.Exp, scale=0.125),
                                         [h_s, pt_free[pslot]])
                        s_free[sbk] = h_p
                        ab = u % 2
                        accD = accD2[ab]
                        if first:
                            r1 = DVE.op(lambda e, pslot=pslot, accD=accD: e.tensor_copy(accD[:, :], PT[pslot][:, 0:RSD]), [h_p, acc_free[ab]])
                        else:
                            r1 = DVE.op(lambda e, pslot=pslot, accD=accD: e.tensor_tensor(accD[:, :], accD[:, :], PT[pslot][:, 0:RSD], op=ALU.add), [h_p])
                        drain_dve(1)
                        pendq.append((s, kb, pslot, h_p, last_in_fill, first, last, [r1], (hh, g, ab)))
                        gstep += 1
                while pendq:
                    pd = pendq.pop(0)
                    emit_av(pd)
                    if pd[4] and nissued < len(fills):
                        issue_fill(nissued); nissued += 1
                drain_dve(100)
                c_free = epi_state["c_free"]
                h_attn = c_free
                for E_ in (PE, ACT, DVE, POOL, SP):
                    E_.wait([h_attn])

                hsq = sb("hsq", [128, 512], F32, esB)
                hrs = sb("hrs", [128, 512], F32, esB)
                hfree = None
                hl = None
                for hh in range(NH):
                    for g in range(NG):
                        b = (hh * NG + g) % 2
                        a1 = ACT.op(lambda e, hh=hh, g=g: e.activation(hsq[:, :], oT[:, hh, g * 512:(g + 1) * 512], AF.Square), [h_attn, hfree])
                        a2 = PE.op(lambda e, b=b: e.matmul(Ops[b][:, :], lhsT=onesf[:, :], rhs=hsq[:, :], start=True, stop=True), [a1])
                        a3 = DVE.op(lambda e, b=b: e.tensor_scalar(hrs[:, :], Ops[b][:, :], 1.0 / 128.0, EPS, op0=ALU.mult, op1=ALU.add), [a2])
                        a4 = rsqrt_act(hrs[:, :], hrs[:, :], [a3])
                        a5 = DVE.op(lambda e, hh=hh, g=g: e.scalar_tensor_tensor(attn_o[:, hh, g * 512:(g + 1) * 512], oT[:, hh, g * 512:(g + 1) * 512],
                                                                                 hn8[:, 0:1], hrs[:, :], op0=ALU.mult, op1=ALU.mult), [a4, h_hn])
                        hfree = a5
                        hl = a5
                h_attn_o = hl
                for E_ in (PE, ACT, DVE, POOL, SP):
                    E_.wait([h_attn_o])
                print("SBUF remaining (B):", nc.sbuf_bytes_remaining)
                flush()

        esC = ExitStack()
        with esC:
            WS = [sb(f"WS{i}", [128, 1024], BF16, esC) for i in range(NWS)]
            x1 = sb("x1", [128, 4, D], F32, esC)
            xn = [sb(f"xn{i}", [128, D], BF16, esC) for i in range(2)]
            hT = sb("hT", [128, NCH, 512], BF16, esC)
            hid = sb("hid", [128, NF, 512], BF16, esC)
            thb = [sb(f"thb{i}", [128, 512], F32, esC) for i in range(2)]
            t2b = [sb(f"t2b{i}", [128, 512], F32, esC) for i in range(2)]
            sgt = sb("sgt", [128, 4, D], F32, esC)
            pld = [sb(f"pld{i}", [128, PLE], F32, esC) for i in range(2)]
            pbf = [sb(f"pbf{i}", [128, PLE], BF16, esC) for i in range(2)]
            pT = sb("pT", [128, 2, 512], BF16, esC)
            stc = [sb(f"stc{i}", [128, 4], F32, esC) for i in range(4)]
            junkc = sb("junkc", [128, D], BF16, esC)
            xin = [sb(f"xin{i}", [128, D], F32, esC) for i in range(2)]
            B = [ps(f"B{i}", [128, 512], F32, esC) for i in range(8)]

            ds_ws = [dsem(f"d_ws{i}") for i in range(NWS)]
            ds_xin = [dsem(f"d_xin{i}") for i in range(2)]
            ds_pl = [dsem(f"d_pl{i}") for i in range(2)]
            ds_out = dsem("d_out")
            ws_free = [h_attn_o] * NWS

            def piece_list():
                L = []
                for c in range(NCH):
                    L.append(("wo", c))
                for f in range(NF):
                    L.append(("wg", f)); L.append(("wu", f))
                for f in range(NF):
                    L.append(("wd", f))
                for c in range(NCH):
                    L.append(("pg", c))
                for c in range(2):
                    L.append(("pp", c))
                return L
            PL = piece_list()
            NP = len(PL)
            allp = [(g, i) for g in range(NG) for i in range(NP)]
            piece_h = {}
            npi = [0]

            def issue_piece():
                if npi[0] >= len(allp):
                    return
                g, i = allp[npi[0]]
                s = npi[0] % NWS
                piece_h[(g, i)] = (s, SP.dma(WS[s][:, :], wsc[i], ds_ws[s], deps=[ws_free[s], h_cv]))
                npi[0] += 1

            for _ in range(NWS):
                issue_piece()

            def use_piece(g, i):
                return piece_h[(g, i)]

            def done_piece(g, i, h):
                s, _ = piece_h[(g, i)]
                ws_free[s] = h
                issue_piece()

            stg = [sb(f"stg{i}", [128, 3, 4], F32, esC) for i in range(2)]
            stg_free = [None, None]
            n_rg = [0]

            def rms_group(gvec, tile_deps, hT_dep):
                sg = n_rg[0] % 2; n_rg[0] += 1
                h1s = []
                for i in range(4):
                    h1s.append(ACT.op(lambda e, i=i: e.activation(junkc[:, :], x1[:, i, :], AF.Square, accum_out=stg[sg][:, 0, i:i + 1]),
                                      [tile_deps[i], stg_free[sg] if i == 0 else None]))
                h2 = DVE.op(lambda e: e.tensor_scalar(stg[sg][:, 1, :], stg[sg][:, 0, :], 1.0 / D, EPS, op0=ALU.mult, op1=ALU.add), h1s)
                h3 = rsqrt_act(stg[sg][:, 2, :], stg[sg][:, 1, :], [h2])
                h5 = None
                h4s = {}

                def do_xn(i):
                    b = i % 2
                    h4s[i] = DVE.op(lambda e, i=i, b=b: e.tensor_scalar(xn[b][:, :], x1[:, i, :], stg[sg][:, 2, i:i + 1], None, op0=ALU.mult),
                                    [h3, xn_free[b], tile_deps[i]])
                do_xn(0)
                for i in range(4):
                    b = i % 2
                    bk = 6 + b
                    h4 = h4s[i]
                    hp = None
                    for c in range(NCH):
                        hp = PE.op(lambda e, c=c, b=b, bk=bk: e.transpose(B[bk][:, :].bitcast(BF16)[:, c * 128:(c + 1) * 128], xn[b][:, c * 128:(c + 1) * 128], identb[:, :]),
                                   [h4, bank_free[bk]] if c == 0 else [], signal=(c == NCH - 1))
                    xn_free[b] = hp
                    if i + 1 < 4:
                        do_xn(i + 1)
                    h5 = DVE.op(lambda e, i=i, bk=bk: e.tensor_tensor(hT[:, :, i * 128:(i + 1) * 128],
                                                                     B[bk][:, :].bitcast(BF16).rearrange("p (c t) -> p c t", c=NCH),
                                                                     gvec[:, :].unsqueeze(2).to_broadcast([128, NCH, 128]), op=ALU.mult),
                                [hp, hT_dep if i == 0 else None])
                    bank_free[bk] = h5
                stg_free[sg] = h5
                return h5

            xn_free = [None, None]
            bank_free = [None] * 8
            ptr_free = None
            xin_free = [None, None]
            pld_free = [None, None]
            pbf_free = [None, None]
            hT_free = None
            hid_free = None
            x1_free = None
            sgt_free = None
            pT_free = None
            n_r = 0
            nx = 0
            npl = 0
            h_out_last = None
            for g in range(NG):
                wo = [use_piece(g, c) for c in range(NCH)]
                h_hT = None
                h_x1 = []
                for i in range(4):
                    xs = nx % 2; nx += 1
                    h_xl = SP.dma(xin[xs][:, :], xr[g * 512 + i * 128: g * 512 + (i + 1) * 128, :], ds_xin[xs], deps=[xin_free[xs]])
                    for half in range(2):
                        bk = (i % 2) * 2 + half
                        hm = None
                        for c in range(NCH):
                            lhs = attn_o[:, c, g * 512 + i * 128: g * 512 + (i + 1) * 128] if c < 4 else conv_o[:, c - 4, g * 512 + i * 128: g * 512 + (i + 1) * 128]
                            hm = PE.op(lambda e, lhs=lhs, c=c, bk=bk, half=half: e.matmul(B[bk][:, :], lhsT=lhs, rhs=WS[wo[c][0]][:, half * 512:(half + 1) * 512],
                                                                                         start=(c == 0), stop=(c == NCH - 1)),
                                       [wo[c][1], bank_free[bk], h_attn_o] if True else [], signal=(c == NCH - 1))
                        ha = DVE.op(lambda e, i=i, bk=bk, half=half, xs=xs: e.tensor_tensor(x1[:, i, half * 512:(half + 1) * 512], B[bk][:, :],
                                                                                         xin[xs][:, half * 512:(half + 1) * 512], op=ALU.add),
                                    [hm, h_xl, x1_free if (i == 0 and half == 0) else None])
                        bank_free[bk] = ha
                    xin_free[xs] = ha
                    if i == 3:
                        for c in range(NCH):
                            done_piece(g, c, hm)
                    h_x1.append(ha)
                h_pT = None
                for i in range(4):
                    pb = npl % 2; npl += 1
                    h_pl = SP.dma(pld[pb][:, :], p_own[g * 512 + i * 128: g * 512 + (i + 1) * 128, :], ds_pl[pb], deps=[pld_free[pb]])
                    h_pc = DVE.op(lambda e, pb=pb: e.tensor_copy(pbf[pb][:, :], pld[pb][:, :]), [h_pl, pbf_free[pb]])
                    pld_free[pb] = h_pc
                    hp = None
                    for c in range(2):
                        hp = PE.op(lambda e, c=c, pb=pb: e.transpose(B[5][:, :].bitcast(BF16)[:, c * 128:(c + 1) * 128], pbf[pb][:, c * 128:(c + 1) * 128], identb[:, :]),
                                   [h_pc, bank_free[5]] if c == 0 else [], signal=(c == 1))
                    pbf_free[pb] = hp
                    h_pT = DVE.op(lambda e, i=i: e.tensor_copy(pT[:, :, i * 128:(i + 1) * 128],
                                                               B[5][:, :].bitcast(BF16)[:, 0:256].rearrange("p (c t) -> p c t", c=2)), [hp, pT_free if i == 0 else None])
                    bank_free[5] = h_pT
                h_hT = rms_group(gFn, h_x1, hT_free)
                hid_w = None
                for f in range(NF):
                    ig, iu = NCH + 2 * f, NCH + 2 * f + 1
                    sg_, hgp = use_piece(g, ig)
                    su_, hup = use_piece(g, iu)
                    bg = (f % 2) * 2; bu = bg + 1
                    hm1 = hm2 = None
                    for c in range(NCH):
                        hm1 = PE.op(lambda e, c=c, sg_=sg_, bg=bg: e.matmul(B[bg][:, :], lhsT=WS[sg_][:, c * 128:(c + 1) * 128], rhs=hT[:, c, :],
                                                                           start=(c == 0), stop=(c == NCH - 1)),
                                    [hgp, h_hT, bank_free[bg]] if c == 0 else [], signal=(c == NCH - 1))
                    for c in range(NCH):
                        hm2 = PE.op(lambda e, c=c, su_=su_, bu=bu: e.matmul(B[bu][:, :], lhsT=WS[su_][:, c * 128:(c + 1) * 128], rhs=hT[:, c, :],
                                                                           start=(c == 0), stop=(c == NCH - 1)),
                                    [hup, bank_free[bu]] if c == 0 else [], signal=(c == NCH - 1))
                    done_piece(g, ig, hm1)
                    done_piece(g, iu, hm2)
                    tb = f % 2
                    a1 = ACT.op(lambda e, tb=tb, bg=bg: e.activation(thb[tb][:, :], B[bg][:, :], AF.Tanh, scale=0.5), [hm1])
                    a2 = DVE.op(lambda e, tb=tb, bg=bg: e.scalar_tensor_tensor(t2b[tb][:, :], thb[tb][:, :], 1.0, B[bg][:, :], op0=ALU.add, op1=ALU.mult), [a1])
                    bank_free[bg] = a2
                    a3 = DVE.op(lambda e, tb=tb, bu=bu, f=f: e.scalar_tensor_tensor(hid[:, f, :], t2b[tb][:, :], 0.5, B[bu][:, :], op0=ALU.mult, op1=ALU.mult),
                                [a2, hm2, hid_free if f == 0 else None])
                    bank_free[bu] = a3
                    hid_w = a3
                hT_free = hm2
                hm = None
                for f in range(NF):
                    ip = NCH + 2 * NF + f
                    sd_, hdp = use_piece(g, ip)
                    for i in range(4):
                        for half in range(2):
                            bk = i * 2 + half
                            hm = PE.op(lambda e, f=f, i=i, half=half, bk=bk, sd_=sd_: e.matmul(B[bk][:, :], lhsT=hid[:, f, i * 128:(i + 1) * 128],
                                                                                              rhs=WS[sd_][:, half * 512:(half + 1) * 512],
                                                                                              start=(f == 0), stop=(f == NF - 1)),
                                       [hdp, hid_w, bank_free[bk]] if f == 0 else ([hdp] if (i == 0 and half == 0) else []),
                                       signal=(i == 3 and half == 1))
                    done_piece(g, ip, hm)
                hid_free = hm
                h_x2 = []
                for i in range(4):
                    for half in range(2):
                        bk = i * 2 + half
                        ha = DVE.op(lambda e, i=i, half=half, bk=bk: e.tensor_tensor(x1[:, i, half * 512:(half + 1) * 512], B[bk][:, :],
                                                                                   x1[:, i, half * 512:(half + 1) * 512], op=ALU.add), [hm])
                        bank_free[bk] = ha
                    h_x2.append(ha)
                h_hT = rms_group(gPn, h_x2, hT_free)
                base = NCH + 3 * NF
                pgp = [use_piece(g, base + c) for c in range(NCH)]
                ppp = [use_piece(g, base + NCH + c) for c in range(2)]
                hm = None
                hh2 = None
                for i in range(4):
                    for half in range(2):
                        n_ = i * 2 + half
                        bkA = (n_ % 4) * 2
                        bkB = bkA + 1
                        hmA = None
                        for c in range(NCH):
                            hmA = PE.op(lambda e, c=c, i=i, half=half, bkA=bkA: e.matmul(B[bkA][:, :], lhsT=hT[:, c, i * 128:(i + 1) * 128],
                                                                                       rhs=WS[pgp[c][0]][:, half * 512:(half + 1) * 512],
                                                                                       start=(c == 0), stop=(c == NCH - 1)),
                                        [pgp[c][1], h_hT, bank_free[bkA]], signal=(c == NCH - 1))
                        hmB = None
                        for c in range(2):
                            hmB = PE.op(lambda e, c=c, i=i, half=half, bkB=bkB: e.matmul(B[bkB][:, :], lhsT=pT[:, c, i * 128:(i + 1) * 128],
                                                                                       rhs=WS[ppp[c][0]][:, half * 512:(half + 1) * 512],
                                                                                       start=(c == 0), stop=(c == 1)),
                                        [ppp[c][1], h_pT, bank_free[bkB]], signal=(c == 1))
                        hm = hmB
                        a1 = ACT.op(lambda e, i=i, half=half, bkA=bkA: e.activation(sgt[:, i, half * 512:(half + 1) * 512], B[bkA][:, :], AF.Tanh, scale=0.5),
                                    [hmA, sgt_free if (i == 0 and half == 0) else None])
                        bank_free[bkA] = a1
                        b0 = DVE.op(lambda e, i=i, half=half: e.tensor_scalar(sgt[:, i, half * 512:(half + 1) * 512], sgt[:, i, half * 512:(half + 1) * 512],
                                                                             0.5, 0.5, op0=ALU.mult, op1=ALU.add), [a1])
                        b1 = DVE.op(lambda e, i=i, half=half, bkB=bkB: e.tensor_tensor(sgt[:, i, half * 512:(half + 1) * 512], sgt[:, i, half * 512:(half + 1) * 512],
                                                                                     B[bkB][:, :], op=ALU.mult), [b0, hmB])
                        bank_free[bkB] = b1
                        hh2 = DVE.op(lambda e, i=i, half=half: e.tensor_tensor(sgt[:, i, half * 512:(half + 1) * 512], sgt[:, i, half * 512:(half + 1) * 512],
                                                                             x1[:, i, half * 512:(half + 1) * 512], op=ALU.add), [b1])
                    h_out_last = SP.dma(out_own[g * 512 + i * 128: g * 512 + (i + 1) * 128, :], sgt[:, i, :], ds_out, deps=[hh2])
                for c in range(NCH + 2):
                    done_piece(g, base + c, hm)
                hT_free = hm
                pT_free = hm
                x1_free = hh2
                sgt_free = h_out_last
            for E_ in (SP, POOL, ACT, DVE, PE):
                E_.wait([h_out_last])
            print("SBUF remaining (C):", nc.sbuf_bytes_remaining)

            flush()
    return nc


def host_consts(NG, core):
    NT = NG * 512
    S = NCORES * NT
    t0 = core * NT
    bf = ml_dtypes.bfloat16
    identf = np.eye(128, dtype=np.float32)
    bones = np.zeros((128, 128), np.float32)
    bones[:64, :64] = 1.0
    bones[64:, 64:] = 1.0
    kpos = (np.arange(S) + t0) % S
    kaug = np.zeros((NH, NG, 4, S), np.float32)
    qaug = np.zeros((NH, 4, 512), np.float32)
    bdiag = np.zeros((NH, 128, 896), np.float32)
    qi = np.arange(512)
    for h in range(NH):
        s8 = 8.0 * SLOPES[h]
        qaug[h, 0] = 1.0
        qaug[h, 1] = 1.0
        qaug[h, 2] = -s8 * 128.0 * (qi // 128)
        qaug[h, 3] = -s8 * (qi % 128)
        ki = np.arange(128)[:, None]
        xx = np.arange(896)[None, :]
        bdiag[h] = -s8 * np.abs(xx - 384 - ki)
        for g in range(NG):
            q0 = t0 + g * 512
            sign = np.where(kpos < q0, 1.0, np.where(kpos >= q0 + 512, -1.0, 0.0))
            rel = kpos - q0
            blk = np.floor_divide(rel, 128)
            rr = rel - blk * 128
            kaug[h, g, 0] = sign * s8 * 128.0 * blk
            kaug[h, g, 1] = sign * s8 * rr
            kaug[h, g, 2] = sign
            kaug[h, g, 3] = sign
    for a in (kaug, qaug):
        assert np.array_equal(a.astype(bf).astype(np.float32), a)
    return dict(c_identb=identf.astype(bf), c_identf=identf, c_bones=bones.astype(bf),
                c_kaug=kaug.astype(bf), c_qaug=qaug.astype(bf), c_bdiag=bdiag)


_PROG_CACHE = {}


def run(inputs, NG):
    NT = NG * 512
    S = NCORES * NT
    x = np.ascontiguousarray(np.asarray(inputs["x"], np.float32).reshape(S, D))
    p = np.ascontiguousarray(np.asarray(inputs["p"], np.float32).reshape(S, PLE))
    if NG not in _PROG_CACHE:
        _PROG_CACHE[NG] = build_program(NG)
    nc = _PROG_CACHE[NG]
    sq = lambda k: np.ascontiguousarray(np.asarray(inputs[k], np.float32)[0])
    shared = {k: sq(k) for k in ("w_in", "w_out", "w_gate", "w_up", "w_down", "w_ple_gate", "w_ple_proj",
                                 "lambda_q1", "lambda_k1", "lambda_q2", "lambda_k2", "conv_w")}
    fm = lambda k: np.ascontiguousarray(sq(k).reshape(-1, 128).T)
    for k in ("attn_norm", "ffn_norm", "ple_norm", "conv_b", "conv_ln_g", "conv_ln_b"):
        shared[k] = fm(k)
    for k in ("q_norm", "k_norm"):
        shared[k] = np.ascontiguousarray(np.concatenate([sq(k), sq(k)]).reshape(128, 1))
    shared["head_norm"] = np.ascontiguousarray(sq("head_norm").reshape(128, 1))
    xpad = np.concatenate([np.zeros((16, D), np.float32), x, np.zeros((16, D), np.float32)], axis=0)
    in_maps = []
    for c in range(NCORES):
        t0 = c * NT
        m = dict(shared)
        m["xr"] = np.ascontiguousarray(np.roll(x, -t0, axis=0))
        m["xhalo"] = np.ascontiguousarray(np.concatenate([xpad[t0:t0 + 16], xpad[t0 + NT + 16:t0 + NT + 32]], axis=0))
        m["p_own"] = np.ascontiguousarray(p[t0:t0 + NT])
        m.update(host_consts(NG, c))
        in_maps.append(m)
    res = run_bass_kernel_spmd(nc, in_maps, core_ids=list(range(NCORES)))
    out = np.concatenate([np.asarray(r["out_own"], np.float32) for r in res.results], axis=0)
    return out.reshape(1, S, D)


def kernel(**inputs):
    return run(inputs, 4)
```

```python
import numpy as np
import ml_dtypes
from contextlib import ExitStack

import concourse.bass as bass
import concourse.mybir as mybir
from concourse.bass_utils import run_bass_kernel_spmd

F32 = mybir.dt.float32
BF16 = mybir.dt.bfloat16
AF = mybir.ActivationFunctionType
ALU = mybir.AluOpType
AX = mybir.AxisListType

NCORES = 8
D = 1024
NCH = D // 128
NH = 4
DH = 64
FF = 2816
NF = FF // 128
PLE = 256
CK = 31
EPS = 1e-6
SLOPES = [2.0 ** (-8.0 * (h + 1) / NH) for h in range(NH)]
LAM_INIT = 0.8 - 0.6 * 1.0
CHK = 16
NSLOT = 3
NPT = 4
NWS = 14
RSD = 768
WIN_BLOCKS = [5, 18, None, None]


class H:
    __slots__ = ("key", "sem", "val")

    def __init__(self, key, sem, val):
        self.key, self.sem, self.val = key, sem, val


class Eng:
    def __init__(self, name, sem):
        self.name, self.sem = name, sem
        self.count = 0
        self.waited = {}
        self.ops = []

    def _waits(self, deps):
        for d in deps:
            if d is None:
                continue
            if isinstance(d, (list, tuple)):
                self._waits(d)
                continue
            if self.waited.get(d.key, 0) < d.val:
                self.waited[d.key] = d.val
                self.ops.append(("wait", d.sem, d.val))

    def op(self, fn, deps=(), signal=True):
        self._waits(deps)
        if signal:
            self.count += 1
            self.ops.append(("ins", fn, self.sem, 1))
            return H(self.name, self.sem, self.count)
        self.ops.append(("ins", fn, None, 0))
        return None

    def dma(self, out, in_, dsem, deps=(), **kw):
        self._waits(deps)
        dsem.total += 16
        self.ops.append(("ins", (lambda e, o=out, i=in_, k=kw: e.dma_start(out=o, in_=i, **k)), dsem.sem, 16))
        return H(dsem.name, dsem.sem, dsem.total)

    def wait(self, deps):
        self._waits(deps)

    def replay(self, e):
        for rec in self.ops:
            if rec[0] == "wait":
                e.wait_ge(rec[1], rec[2])
            else:
                ins = rec[1](e)
                if rec[2] is not None:
                    ins.then_inc(rec[2], rec[3])


class DSem:
    def __init__(self, name, sem):
        self.name, self.sem, self.total = name, sem, 0


def build_program(NG):
    NT = NG * 512
    S = NCORES * NT
    NGRP = S // 512
    NKB = S // 128
    NCHK = NKB // CHK
    UW = NT + 32

    nc = bass.Bass("TRN2", target_bir_lowering=False)

    def din(name, shape, dt=F32):
        return nc.dram_tensor(name, list(shape), dt, kind="ExternalInput").ap()

    xr = din("xr", [S, D])
    xhalo = din("xhalo", [32, D])
    p_own = din("p_own", [NT, PLE])
    w_in = din("w_in", [D, 2560])
    w_out = din("w_out", [D, D])
    w_gate = din("w_gate", [D, FF])
    w_up = din("w_up", [D, FF])
    w_down = din("w_down", [FF, D])
    w_pg = din("w_ple_gate", [D, D])
    w_pp = din("w_ple_proj", [PLE, D])
    attn_norm = din("attn_norm", [128, NCH])
    ffn_norm = din("ffn_norm", [128, NCH])
    ple_norm = din("ple_norm", [128, NCH])
    q_norm = din("q_norm", [128, 1])
    k_norm = din("k_norm", [128, 1])
    lq1 = din("lambda_q1", [DH]); lk1 = din("lambda_k1", [DH])
    lq2 = din("lambda_q2", [DH]); lk2 = din("lambda_k2", [DH])
    head_norm = din("head_norm", [128, 1])
    conv_w = din("conv_w", [CK, 512])
    conv_b = din("conv_b", [128, 4])
    ln_g = din("conv_ln_g", [128, 4])
    ln_b = din("conv_ln_b", [128, 4])
    c_identb = din("c_identb", [128, 128], BF16)
    c_identf = din("c_identf", [128, 128])
    c_bones = din("c_bones", [128, 128], BF16)
    c_kaug = din("c_kaug", [NH, NG, 4, S], BF16)
    c_qaug = din("c_qaug", [NH, 4, 512], BF16)
    c_bdiag = din("c_bdiag", [NH, 128, 896])

    out_own = nc.dram_tensor("out_own", [NT, D], F32, kind="ExternalOutput").ap()

    kT1 = nc.dram_tensor("kT1", [NH, 64, S], BF16).ap()
    kT2 = nc.dram_tensor("kT2", [NH, 64, S], BF16).ap()
    vdr = nc.dram_tensor("vdr", [NH, 128, NKB, 128], BF16).ap()
    NPIECE = NCH + 3 * NF + NCH + 2
    wsc = nc.dram_tensor("wsc", [NPIECE, 128, 1024], BF16).ap()

    es = ExitStack()
    with es:
        def sb(name, shape, dt=F32, stack=es):
            return stack.enter_context(nc.sbuf_tensor(name, list(shape), dt))

        def ps(name, shape, dt=F32, stack=es):
            return stack.enter_context(nc.psum_tensor(name, list(shape), dt))

        def newsem(name):
            return es.enter_context(nc.semaphore(name))

        PE = Eng("pe", newsem("s_pe"))
        ACT = Eng("act", newsem("s_act"))
        DVE = Eng("dve", newsem("s_dve"))
        POOL = Eng("pool", newsem("s_pool"))
        SP = Eng("sp", newsem("s_sp"))
        _dsn = [0]
        ENGS = (PE, ACT, DVE, POOL, SP)

        def flush():
            with nc.allow_non_contiguous_dma("small param loads"):
                with nc.Block() as block:
                    @block.tensor
                    def _(e):
                        PE.replay(e)

                    @block.scalar
                    def _(e):
                        ACT.replay(e)

                    @block.vector
                    def _(e):
                        DVE.replay(e)

                    @block.gpsimd
                    def _(e):
                        POOL.replay(e)

                    @block.sync
                    def _(e):
                        SP.replay(e)
            for E_ in ENGS:
                E_.ops = []

        def dsem(name):
            _dsn[0] += 1
            return DSem(name, newsem(name))

        def rsqrt_act(out_ap, in_ap, deps):
            h1_ = ACT.op(lambda e: e.activation(out_ap, in_ap, AF.Ln), deps)
            return ACT.op(lambda e: e.activation(out_ap, out_ap, AF.Exp, scale=-0.5), [h1_])

        identb = sb("identb", [128, 128], BF16)
        identf = sb("identf", [128, 128])
        bones = sb("bones", [128, 128], BF16)
        onesb = sb("onesb", [128, 128], BF16)
        onesf = sb("onesf", [128, 128])
        negh = sb("negh", [128, 512])
        gA = sb("gA", [128, NCH]); gFn = sb("gFn", [128, NCH]); gPn = sb("gPn", [128, NCH])
        gq2 = sb("gq2", [128, 1]); gk2 = sb("gk2", [128, 1])
        hn8 = sb("hn8", [128, 1])
        cw_sb = sb("cw_sb", [CK, 512])
        cwT = sb("cwT", [128, 4, 32])
        cb4 = sb("cb4", [128, 4]); lg4 = sb("lg4", [128, 4]); lb4 = sb("lb4", [128, 4])
        lamt = sb("lamt", [128, 4, DH])
        lamw = sb("lamw", [128, 8])
        neglam = sb("neglam", [128, 1])
        conv_o = sb("conv_o", [128, 4, NT], BF16)
        attn_o = sb("attn_o", [128, NH, NT], BF16)

        ds_c = dsem("d_const")
        with nc.allow_non_contiguous_dma("small param loads"):
            hc = []
            hc.append(SP.dma(identb[:], c_identb, ds_c))
            hc.append(SP.dma(identf[:], c_identf, ds_c))
            hc.append(SP.dma(bones[:], c_bones, ds_c))
            hc.append(SP.dma(gA[:], attn_norm, ds_c))
            hc.append(SP.dma(gFn[:], ffn_norm, ds_c))
            hc.append(SP.dma(gPn[:], ple_norm, ds_c))
            hc.append(SP.dma(gq2[:], q_norm, ds_c))
            hc.append(SP.dma(gk2[:], k_norm, ds_c))
            hc.append(SP.dma(hn8[:], head_norm, ds_c))
            hc.append(SP.dma(cw_sb[:], conv_w, ds_c))
            hc.append(SP.dma(cb4[:], conv_b, ds_c))
            hc.append(SP.dma(lg4[:], ln_g, ds_c))
            hc.append(SP.dma(lb4[:], ln_b, ds_c))
            for i, v in enumerate((lq1, lk1, lq2, lk2)):
                hc.append(SP.dma(lamt[:, i, :], v.rearrange("(o d) -> o d", o=1).partition_broadcast(128), ds_c))
        h_const = hc[-1]

        h_m1 = POOL.op(lambda e: e.memset(onesb[:], 1.0))
        h_m2 = POOL.op(lambda e: e.memset(onesf[:], 1.0))
        h_m3 = POOL.op(lambda e: e.memset(negh[:], -0.5))
        h_init = h_m3

        h = DVE.op(lambda e: e.tensor_tensor(lamt[:, 0, :], lamt[:, 0, :], lamt[:, 1, :], op=ALU.mult), [h_const])
        h = DVE.op(lambda e: e.tensor_tensor(lamt[:, 2, :], lamt[:, 2, :], lamt[:, 3, :], op=ALU.mult), [h])
        h = DVE.op(lambda e: e.reduce_sum(lamw[:, 0:1], lamt[:, 0, :], axis=AX.X), [h])
        h = DVE.op(lambda e: e.reduce_sum(lamw[:, 1:2], lamt[:, 2, :], axis=AX.X), [h])
        h = ACT.op(lambda e: e.activation(lamw[:, 2:4], lamw[:, 0:2], AF.Exp), [h])
        h = DVE.op(lambda e: e.tensor_tensor(lamw[:, 4:5], lamw[:, 3:4], lamw[:, 2:3], op=ALU.subtract), [h])
        h = DVE.op(lambda e: e.tensor_scalar(neglam[:], lamw[:, 4:5], -LAM_INIT, None, op0=ALU.add), [h])
        h_hn = DVE.op(lambda e: e.tensor_scalar(hn8[:], hn8[:], 1.0 - LAM_INIT, None, op0=ALU.mult), [h_const])
        h_lam = h

        esAB = ExitStack()
        with esAB:
            Q1 = sb("Q1", [128, NH, NT], BF16, esAB)
            Q2 = sb("Q2", [128, NH, NT], BF16, esAB)
            uT = sb("uT", [128, 4, UW], BF16, esAB)

            ds_qa = dsem("d_qaug")
            hq = []
            for hh in range(NH):
                for g in range(NG):
                    hq.append(SP.dma(Q1[64:68, hh, g * 512:(g + 1) * 512], c_qaug[hh], ds_qa))
                    hq.append(SP.dma(Q2[64:68, hh, g * 512:(g + 1) * 512], c_qaug[hh], ds_qa))
            h_qaug = hq[-1]

            esA = ExitStack()
            with esA:
                w_in_sb = sb("w_in_sb", [128, NCH, 2560], BF16, esA)
                NXT = 4
                xt = [sb(f"xt{i}", [128, D], F32, esA) for i in range(NXT)]
                xb = [sb(f"xb{i}", [128, D], BF16, esA) for i in range(2)]
                junk = sb("junk", [128, D], BF16, esA)
                xT = [sb(f"xT{i}", [128, NCH, 512], BF16, esA) for i in range(2)]
                sq = [sb(f"sq{i}", [128, 512], BF16, esA) for i in range(2)]
                kst = [sb(f"kst{i}", [128, NH, 512], BF16, esA) for i in range(1)]
                vst = [sb(f"vst{i}", [128, NH, 4, 128], BF16, esA) for i in range(1)]
                Rrow = [sb(f"Rrow{i}", [128, 512], F32, esA) for i in range(2)]
                Erow = [sb(f"Erow{i}", [128, 512], F32, esA) for i in range(2)]
                t1 = [sb(f"t1_{i}", [128, 512], F32, esA) for i in range(2)]
                rk = [sb(f"rk{i}", [128, 512], F32, esA) for i in range(2)]
                stq = [sb(f"stq{i}", [128, 3, 4], F32, esA) for i in range(2)]
                q2st = sb("q2st", [128, NH, 512], BF16, esA)
                ds_q2 = dsem("d_q2")
                st_q2free = [None]
                dgR = [sb(f"dgR{i}", [128, 128], F32, esA) for i in range(2)]
                dgE = [sb(f"dgE{i}", [128, 128], F32, esA) for i in range(4)]
                cva = [sb(f"cva{i}", [128, 512], F32, esA) for i in range(1)]
                cvb = [sb(f"cvb{i}", [128, 512], F32, esA) for i in range(1)]

                ptr = [ps(f"ptr{i}", [128, D], BF16, esA) for i in range(2)]
                REp = ps("REp", [128, 512], F32, esA)
                kps = [ps(f"kps{i}", [128, 512], F32, esA) for i in range(2)]
                ssp = ps("ssp", [128, 512], F32, esA)
                vps = [ps(f"vps{i}", [128, 512], F32, esA) for i in range(2)]

                wblocks = [(768, 1024), (1280, 1536), (512, 768), (1024, 1280), (0, 512), (1536, 2048), (2048, 2560)]
                h_wblk = []
                for bi_, (c0_, c1_) in enumerate(wblocks):
                    dsw = dsem(f"d_win{bi_}")
                    h_wblk.append(POOL.dma(w_in_sb[:, :, c0_:c1_], w_in[:, c0_:c1_].rearrange("(c p) n -> p c n", p=128), dsw))

                def hw_cols(a, b_):
                    return [h_wblk[i] for i, (c0_, c1_) in enumerate(wblocks) if c0_ < b_ and a < c1_]
                h_win = None

                hcw = None
                for cc in range(4):
                    hm = PE.op(lambda e, cc=cc: e.matmul(ssp[:, 0:CK], lhsT=cw_sb[0:CK, cc * 128:(cc + 1) * 128],
                                                         rhs=identf[0:CK, 0:CK], start=True, stop=True),
                               [h_const, hcw])
                    hcw = DVE.op(lambda e, cc=cc: e.tensor_copy(cwT[:, cc, 0:CK], ssp[:, 0:CK]), [hm])
                h_cwT = hcw

                ds_x = [dsem(f"d_x{i}") for i in range(NXT)]
                ds_kst = [dsem(f"d_kst{i}") for i in range(1)]
                ds_vst = [dsem(f"d_vst{i}") for i in range(1)]

                state = dict(xt_free=[None] * NXT, xb_free=[None] * 2, ptr_free=[None, None], xT_free=[None] * 2,
                             st_free=[None] * 2, dg_free=[None] * 4, dgR_free=[None] * 2, Rrow_free=None, RE_free=None, rows_free=[None] * 2,
                             kps_free=[None] * 2, ssp_free=h_cwT, sq_free=[None] * 2, t1_free=[None] * 2,
                             rk_free=[None] * 2, kst_free=[None] * 2, vps_free=[None] * 2, vst_free=[None] * 2,
                             cva_free=[None] * 2, cvb_free=[None] * 2, tile_no=0, grp_no=0, kq_no=0, v_no=0,
                             cv_no=0)
                st_ = state

                def _nheads(gi_):
                    n = 0
                    for hh in range(NH):
                        wb = WIN_BLOCKS[hh]
                        if wb is None or 4 * NG + 2 * wb >= NKB:
                            n += 1
                        elif any((j <= 4 * NG - 1 + wb) or (j >= NKB - wb) for j in range(4 * gi_, 4 * gi_ + 4)):
                            n += 1
                    return n
                gorder = sorted(range(NG, NGRP), key=lambda gi_: (_nheads(gi_), gi_)) + list(range(NG))
                tile_srcs = []
                for gi_ in gorder:
                    for i_ in range(4):
                        tile_srcs.append((xr[gi_ * 512 + i_ * 128: gi_ * 512 + (i_ + 1) * 128, :], 128))
                tile_srcs.append((xhalo[:, :], 32))
                ld_h = {}
                st_["loads"] = 0

                def issue_load():
                    tn = st_["loads"]
                    if tn >= len(tile_srcs):
                        return
                    src, rows_ = tile_srcs[tn]
                    xs = tn % NXT
                    ld_h[tn] = SP.dma(xt[xs][0:rows_, :], src, ds_x[xs], deps=[st_["xt_free"][xs]])
                    st_["loads"] += 1

                for _ in range(NXT):
                    issue_load()

                def fe_begin(ntile, rows, need_R):
                    gb = st_["grp_no"] % 2
                    st_["grp_no"] += 1
                    return dict(gb=gb, ntile=ntile, rows=rows, need_R=need_R, h_sqs=[], casts={}, tns=[], h_xT=None)

                def fe_cast(cx, i):
                    rows = cx["rows"]
                    tn = cx["tns"][i]
                    xs = tn % NXT; bs = tn % 2
                    h_cast = DVE.op(lambda e, xs=xs, bs=bs: e.tensor_copy(xb[bs][0:rows, :], xt[xs][0:rows, :]),
                                    [ld_h[tn], st_["xb_free"][bs]])
                    cx["casts"][i] = h_cast

                def fe_tile(cx, i):
                    gb, rows, ntile = cx["gb"], cx["rows"], cx["ntile"]
                    if i == 0:
                        for k in range(ntile):
                            cx["tns"].append(st_["tile_no"]); st_["tile_no"] += 1
                    tn = cx["tns"][i]
                    xs = tn % NXT; bs = tn % 2; pb_ = tn % 2
                    h_ld = ld_h[tn]
                    h_sq = ACT.op(lambda e, xs=xs, i=i: e.activation(junk[0:rows, :], xt[xs][0:rows, :], AF.Square,
                                                                   accum_out=stq[gb][0:rows, 0, i:i + 1]),
                                  [h_ld, st_["st_free"][gb] if i == 0 else None])
                    cx["h_sqs"].append(h_sq)
                    if i == 0:
                        fe_cast(cx, 0)
                    h_cast = cx["casts"][i]
                    st_["xt_free"][xs] = [h_sq, h_cast]
                    issue_load()
                    hp = None
                    for c in range(NCH):
                        hp = PE.op(lambda e, c=c, bs=bs, pb_=pb_: e.transpose(ptr[pb_][:, c * 128:c * 128 + rows],
                                                                             xb[bs][0:rows, c * 128:(c + 1) * 128],
                                                                             identb[0:rows, 0:rows]),
                                   [h_cast, st_["ptr_free"][pb_], h_const] if c == 0 else [], signal=(c == NCH - 1))
                    st_["xb_free"][bs] = hp
                    if i + 1 < ntile:
                        fe_cast(cx, i + 1)
                    h_ev = DVE.op(lambda e, i=i, pb_=pb_: e.tensor_tensor(
                        xT[gb][:, :, i * rows:(i + 1) * rows],
                        ptr[pb_][:, :].rearrange("p (c t) -> p c t", c=NCH)[:, :, 0:rows],
                        gA[:, :].unsqueeze(2).to_broadcast([128, NCH, rows]), op=ALU.mult),
                        [hp, st_["xT_free"][gb] if i == 0 else None])
                    st_["ptr_free"][pb_] = h_ev
                    cx["h_xT"] = h_ev

                def fe_stats_a(cx):
                    gb, rows, ntile = cx["gb"], cx["rows"], cx["ntile"]
                    h1 = DVE.op(lambda e: e.tensor_scalar(stq[gb][0:rows, 1, 0:ntile], stq[gb][0:rows, 0, 0:ntile], 1.0 / D, EPS,
                                                          op0=ALU.mult, op1=ALU.add), cx["h_sqs"])
                    h2 = rsqrt_act(stq[gb][0:rows, 2, 0:ntile], stq[gb][0:rows, 1, 0:ntile], [h1])
                    cx["h1"], cx["h2"] = h1, h2
                    cx["h3s"] = []
                    for i in range(ntile):
                        h3 = DVE.op(lambda e, i=i: e.tensor_scalar(dgE[i][0:rows, 0:rows], identf[0:rows, 0:rows],
                                                                   stq[gb][0:rows, 1, i:i + 1], EPS, op0=ALU.mult, op1=ALU.mult),
                                    [h1, st_["dg_free"][i], h_const])
                        cx["h3s"].append(h3)

                def fe_stats_b(cx):
                    gb, rows, ntile, need_R = cx["gb"], cx["rows"], cx["ntile"], cx["need_R"]
                    N = ntile * rows
                    h1, h2 = cx["h1"], cx["h2"]
                    rst_handles = [(i, h2) for i in range(ntile)]
                    hE = None
                    for i in range(ntile):
                        hE = PE.op(lambda e, i=i: e.matmul(REp[:, i * rows:(i + 1) * rows],
                                                           lhsT=onesf[0:rows, :], rhs=dgE[i][0:rows, 0:rows],
                                                           start=True, stop=True),
                                   [cx["h3s"][i], st_["RE_free"] if i == 0 else None, h_init])
                        st_["dg_free"][i] = hE
                    hEr = ACT.op(lambda e, gb=gb: e.activation(Erow[gb][:, 0:N], REp[:, 0:N], AF.Copy), [hE, st_["rows_free"][gb], st_["rows_last_t1"][gb]])
                    st_["RE_free"] = hEr
                    hRr = None
                    if need_R:
                        hR = None
                        for i in range(ntile):
                            ds_ = i % 2
                            h4 = DVE.op(lambda e, i=i, ds_=ds_: e.tensor_scalar(dgR[ds_][0:rows, 0:rows], identf[0:rows, 0:rows],
                                                                               stq[gb][0:rows, 2, i:i + 1], None, op0=ALU.mult),
                                        [h2, st_["dgR_free"][ds_]])
                            hR = PE.op(lambda e, i=i, ds_=ds_: e.matmul(REp[:, i * rows:(i + 1) * rows],
                                                                       lhsT=onesf[0:rows, :], rhs=dgR[ds_][0:rows, 0:rows],
                                                                       start=True, stop=True), [h4, hEr if i == 0 else None])
                            st_["dgR_free"][ds_] = hR
                        hRr = ACT.op(lambda e, gb=gb: e.activation(Rrow[gb][:, 0:N], REp[:, 0:N], AF.Copy), [hR])
                        st_["RE_free"] = hRr
                    st_["st_free"][gb] = [hE, h2]
                    return gb, cx["h_xT"], hEr, hRr, rst_handles

                def fe_stats(cx):
                    fe_stats_a(cx)
                    return fe_stats_b(cx)

                def pn_stage1(gb, N, col0, h_xT):
                    kn = st_["kq_no"]; st_["kq_no"] += 1
                    b = kn % 2
                    hm = None
                    for c in range(NCH):
                        hm = PE.op(lambda e, c=c, b=b: e.matmul(kps[b][:, 0:N], lhsT=w_in_sb[:, c, col0:col0 + 128],
                                                                rhs=xT[gb][:, c, 0:N], start=(c == 0), stop=(c == NCH - 1)),
                                   [h_xT, hw_cols(col0, col0 + 128), st_["kps_free"][b]] if c == 0 else [], signal=(c == NCH - 1))
                    hs = ACT.op(lambda e, b=b: e.activation(sq[b][:, 0:N], kps[b][:, 0:N], AF.Square), [hm, st_["sq_free"][b]])
                    return dict(b=b, hm=hm, hs=hs, N=N, gb=gb)

                def pn_stage2ss(cx):
                    b, hs, N = cx["b"], cx["hs"], cx["N"]
                    hss = PE.op(lambda e, b=b: e.matmul(ssp[:, 0:N], lhsT=bones[:, :], rhs=sq[b][:, 0:N], start=True, stop=True),
                                [hs, st_["ssp_free"], h_const])
                    st_["sq_free"][b] = hss
                    cx["hss"] = hss

                def pn_stage2n(cx, hEr):
                    b, N, gb = cx["b"], cx["N"], cx["gb"]
                    ht = DVE.op(lambda e, b=b: e.scalar_tensor_tensor(t1[b][:, 0:N], ssp[:, 0:N], 1.0 / DH, Erow[gb][:, 0:N],
                                                                      op0=ALU.mult, op1=ALU.add),
                                [cx["hss"], hEr, st_["t1_free"][b]])
                    st_["ssp_free"] = ht
                    st_["rows_last_t1"][gb] = ht
                    hr = rsqrt_act(rk[b][:, 0:N], t1[b][:, 0:N], [ht, st_["rk_free"][b]])
                    st_["t1_free"][b] = hr
                    cx["hr"] = hr

                def pn_stage2a(cx, hEr):
                    pn_stage2ss(cx)
                    pn_stage2n(cx, hEr)

                def pn_stage2b(cx, writer):
                    b = cx["b"]
                    hw = writer(b, cx["hr"], cx["hm"])
                    st_["rk_free"][b] = hw
                    st_["kps_free"][b] = hw
                    return hw

                def pn_stage2(cx, hEr, writer):
                    pn_stage2a(cx, hEr)
                    return pn_stage2b(cx, writer)

                def proj_norm(gb, N, col0, h_xT, hEr, gvec, writer):
                    return pn_stage2(pn_stage1(gb, N, col0, h_xT), hEr, writer)

                st_["pending"] = []
                st_["rows_last_t1"] = [None, None]
                st_["pending_a"] = []
                st_["deferred_finish"] = []

                def run_pending_a():
                    while st_["pending_a"]:
                        st_["pending_a"].pop(0)()

                def run_pending():
                    run_pending_a()
                    while st_["pending"]:
                        st_["pending"].pop(0)()
                    while st_["deferred_finish"]:
                        st_["deferred_finish"].pop(0)()

                def heads_needed(gi):
                    res = []
                    for hh in range(NH):
                        wb = WIN_BLOCKS[hh]
                        if wb is None or 4 * NG + 2 * wb >= NKB:
                            res.append(hh)
                            continue
                        need = any((j <= 4 * NG - 1 + wb) or (j >= NKB - wb) for j in range(4 * gi, 4 * gi + 4))
                        if need:
                            res.append(hh)
                    return res

                def kv_group(gi, gb, h_xT, hEr, rst_handles):
                    sb_ = 0
                    heads = heads_needed(gi)
                    h_lo = min(heads)
                    assert heads == list(range(h_lo, NH))
                    nvh = NH - h_lo
                    hws = []
                    hvs = []
                    cxs = {}
                    first_k = [True]

                    def do_s1(hh):
                        cxs[hh] = pn_stage1(gb, 512, 512 + hh * 128, h_xT)

                    def do_s2a(hh):
                        pn_stage2ss(cxs[hh])
                        st_["pending_a"].append(lambda hh=hh: pn_stage2n(cxs[hh], hEr))

                        def wr(hh=hh):
                            def writer(b, hr, hm, hh=hh):
                                dep = st_["kst_free"][sb_] if first_k[0] else None
                                first_k[0] = False
                                return DVE.op(lambda e: e.scalar_tensor_tensor(kst[sb_][:, hh, :], kps[b][:, :], gk2[:, 0:1], rk[b][:, :],
                                                                               op0=ALU.mult, op1=ALU.mult), [hr, hm, dep])
                            hws.append(pn_stage2b(cxs[hh], writer))
                        st_["pending"].append(wr)

                    def do_v(i):
                        vn = st_["v_no"]; st_["v_no"] += 1
                        b = vn % 2
                        hm = None
                        for c in range(NCH):
                            hm = PE.op(lambda e, c=c, b=b, i=i: e.matmul(vps[b][:, 0:nvh * 128], lhsT=xT[gb][:, c, i * 128:(i + 1) * 128],
                                                                         rhs=w_in_sb[:, c, 1024 + h_lo * 128:1536], start=(c == 0), stop=(c == NCH - 1)),
                                       [h_xT, hw_cols(1024 + h_lo * 128, 1536), st_["vps_free"][b]] if c == 0 else [], signal=(c == NCH - 1))
                        st_["last_v_mm"] = hm
                        ss_, h2 = rst_handles[i]
                        hv = ACT.op(lambda e, b=b, i=i, ss_=ss_: e.activation(vst[sb_][:, h_lo:NH, i, :],
                                                                            vps[b][:, 0:nvh * 128].rearrange("p (h e) -> p h e", h=nvh),
                                                                            AF.Copy, scale=stq[gb][:, 2, ss_:ss_ + 1]),
                                    [hm, h2, st_["vst_free"][sb_] if len(hvs) == 0 else None])
                        st_["vps_free"][b] = hv
                        st_["st_free"][gb] = [st_["st_free"][gb], hv]
                        hvs.append(hv)

                    hq = {0: [], 1: [], 2: [], 3: []}
                    if len(heads) == 4:
                        for i_, hh in enumerate(heads):
                            hq[i_].append(hh)
                    elif len(heads) == 3:
                        hq[0].append(heads[0]); hq[1].append(heads[1]); hq[2].append(heads[2])
                    else:
                        hq[0].append(heads[0]); hq[2].append(heads[1])

                    def quarter(i):
                        run_pending()
                        for hh in hq[i]:
                            do_s1(hh)
                        do_v(i)
                        for hh in hq[i]:
                            do_s2a(hh)
                        if i == 3:
                            st_["xT_free"][gb] = [cxs[heads[-1]]["hm"], st_["last_v_mm"]]

                    def finish():
                        hk1 = POOL.dma(kT1[h_lo:NH, :, gi * 512:(gi + 1) * 512].rearrange("h d t -> d h t"), kst[sb_][0:64, h_lo:NH, :], ds_kst[sb_], deps=hws)
                        hk2 = POOL.dma(kT2[h_lo:NH, :, gi * 512:(gi + 1) * 512].rearrange("h d t -> d h t"), kst[sb_][64:128, h_lo:NH, :], ds_kst[sb_])
                        st_["kst_free"][sb_] = hk2
                        hvd = POOL.dma(vdr[h_lo:NH, :, gi * 4:(gi + 1) * 4, :].rearrange("h p k e -> p h k e"), vst[sb_][:, h_lo:NH, :, :], ds_vst[sb_], deps=hvs)
                        st_["vst_free"][sb_] = hvd
                        return hk2, hvd
                    return quarter, finish

                def q_group(g, gb, h_xT, hEr):
                    hws = []
                    cxq = {}

                    def s1(hh):
                        cxq[hh] = pn_stage1(gb, 512, hh * 128, h_xT)

                    def s2a(hh):
                        pn_stage2a(cxq[hh], hEr)

                    def s2b(hh):
                        def writer(b, hr, hm, hh=hh):
                            DVE.op(lambda e: e.scalar_tensor_tensor(Q1[0:64, hh, g * 512:(g + 1) * 512], kps[b][0:64, :], gq2[0:64, 0:1],
                                                                    rk[b][0:64, :], op0=ALU.mult, op1=ALU.mult), [hr, hm], signal=False)
                            return DVE.op(lambda e: e.scalar_tensor_tensor(q2st[64:128, hh, :], kps[b][64:128, :],
                                                                           gq2[64:128, 0:1], rk[b][64:128, :], op0=ALU.mult, op1=ALU.mult),
                                          [st_q2free[0] if hh == 0 else None])
                        hws.append(pn_stage2b(cxq[hh], writer))
                    s1(0); s1(1); s2a(0); s2a(1); s2b(0); s1(2); s2b(1); s1(3); s2a(2); s2a(3); s2b(2); s2b(3)
                    hmv = SP.dma(Q2[0:64, :, g * 512:(g + 1) * 512], q2st[64:128, :, :], ds_q2, deps=[hws[-1]])
                    st_q2free[0] = hmv
                    return [hws[-1], hmv]

                def conv_in(gb, N, ucol0, h_xT, hRr):
                    hl = None
                    for cc in range(4):
                        cn = st_["cv_no"]; st_["cv_no"] += 1
                        b = cn % 2
                        ha = hg = None
                        for c in range(NCH):
                            ha = PE.op(lambda e, c=c, b=b, cc=cc: e.matmul(kps[b][:, 0:N], lhsT=w_in_sb[:, c, 1536 + cc * 128:1536 + (cc + 1) * 128],
                                                                           rhs=xT[gb][:, c, 0:N], start=(c == 0), stop=(c == NCH - 1)),
                                       [h_xT, hw_cols(1536, 2560), st_["kps_free"][b]] if c == 0 else [], signal=(c == NCH - 1))
                        for c in range(NCH):
                            hg = PE.op(lambda e, c=c, b=b, cc=cc: e.matmul(vps[b][:, 0:N], lhsT=w_in_sb[:, c, 2048 + cc * 128:2048 + (cc + 1) * 128],
                                                                           rhs=xT[gb][:, c, 0:N], start=(c == 0), stop=(c == NCH - 1)),
                                       [st_["vps_free"][b]] if c == 0 else [], signal=(c == NCH - 1))
                        h1 = DVE.op(lambda e, b=b: e.tensor_tensor(t1[b][:, 0:N], vps[b][:, 0:N], Rrow[gb][:, 0:N], op=ALU.mult),
                                    [hg, hRr, st_["t1_free"][b]])
                        st_["vps_free"][b] = h1
                        h2 = ACT.op(lambda e, b=b: e.activation(rk[b][:, 0:N], t1[b][:, 0:N], AF.Tanh, scale=0.5), [h1, st_["rk_free"][b]])
                        h4 = DVE.op(lambda e, b=b: e.scalar_tensor_tensor(t1[b][:, 0:N], rk[b][:, 0:N], 1.0, Rrow[gb][:, 0:N], op0=ALU.add, op1=ALU.mult), [h2])
                        h5 = DVE.op(lambda e, b=b, cc=cc: e.scalar_tensor_tensor(uT[:, cc, ucol0:ucol0 + N], kps[b][:, 0:N], 0.5, t1[b][:, 0:N],
                                                                                 op0=ALU.mult, op1=ALU.mult), [h4, ha])
                        st_["t1_free"][b] = h5
                        st_["rk_free"][b] = h5
                        st_["kps_free"][b] = h5
                        hl = h5
                    return hl

                h_kv_all = []
                h_q_last = None
                h_u_last = None
                cx0 = fe_begin(4, 128, gorder[0] < NG)
                for i in range(4):
                    fe_tile(cx0, i)
                pre = fe_stats(cx0)
                for oi, gi in enumerate(gorder):
                    own = gi < NG
                    if oi + 1 < NGRP:
                        ncx = fe_begin(4, 128, gorder[oi + 1] < NG)
                    else:
                        ncx = fe_begin(1, 32, True)
                    gb, h_xT, hEr, hRr, rsth = pre
                    quarter, finish = kv_group(gi, gb, h_xT, hEr, rsth)
                    for i in range(4):
                        run_pending_a()
                        if i < ncx["ntile"]:
                            fe_tile(ncx, i)
                            if i == ncx["ntile"] - 1:
                                fe_stats_a(ncx)
                        quarter(i)
                    def fin_(finish=finish):
                        hk, hv = finish()
                        h_kv_all.extend([hk, hv])
                    st_["deferred_finish"].append(fin_)
                    if own:
                        run_pending()
                        h_q_last = q_group(gi, gb, h_xT, hEr)
                        h_u_last = conv_in(gb, 512, 16 + gi * 512, h_xT, hRr)
                        st_["xT_free"][gb] = [st_["xT_free"][gb], h_q_last, h_u_last]
                        st_["rows_free"][gb] = [h_q_last, h_u_last]
                    else:
                        st_["rows_free"][gb] = st_["kps_free"][:] + st_["t1_free"][:]
                    pre = fe_stats_b(ncx)
                run_pending()
                h_kv_last = h_kv_all
                gb, h_xT, hEr, hRr, rsth = pre
                def conv_in_halo():
                    hl = None
                    for cc in range(4):
                        cn = st_["cv_no"]; st_["cv_no"] += 1
                        b = cn % 2
                        N = 32
                        ha = hg = None
                        for c in range(NCH):
                            ha = PE.op(lambda e, c=c, b=b, cc=cc: e.matmul(kps[b][:, 0:N], lhsT=w_in_sb[:, c, 1536 + cc * 128:1536 + (cc + 1) * 128],
                                                                           rhs=xT[gb][:, c, 0:N], start=(c == 0), stop=(c == NCH - 1)),
                                       [h_xT, hw_cols(1536, 2560), st_["kps_free"][b]] if c == 0 else [], signal=(c == NCH - 1))
                        for c in range(NCH):
                            hg = PE.op(lambda e, c=c, b=b, cc=cc: e.matmul(vps[b][:, 0:N], lhsT=w_in_sb[:, c, 2048 + cc * 128:2048 + (cc + 1) * 128],
                                                                           rhs=xT[gb][:, c, 0:N], start=(c == 0), stop=(c == NCH - 1)),
                                       [st_["vps_free"][b]] if c == 0 else [], signal=(c == NCH - 1))
                        h1 = DVE.op(lambda e, b=b: e.tensor_tensor(cva[0][:, 0:N], vps[b][:, 0:N], Rrow[gb][:, 0:N], op=ALU.mult),
                                    [hg, hRr, st_["cva_free"][0]])
                        st_["vps_free"][b] = h1
                        h2 = ACT.op(lambda e, b=b: e.activation(cvb[0][:, 0:N], cva[0][:, 0:N], AF.Exp, scale=-1.0), [h1, st_["cvb_free"][0]])
                        h3a = DVE.op(lambda e, b=b: e.tensor_scalar(cva[0][:, 0:N], cvb[0][:, 0:N], 1.0, None, op0=ALU.add), [h2])
                        h3 = DVE.op(lambda e, b=b: e.reciprocal(cva[0][:, 0:N], cva[0][:, 0:N]), [h3a])
                        h4 = DVE.op(lambda e, b=b: e.tensor_tensor(cvb[0][:, 0:N], cva[0][:, 0:N], Rrow[gb][:, 0:N], op=ALU.mult), [h3])
                        DVE.op(lambda e, b=b, cc=cc: e.tensor_tensor(uT[:, cc, 0:16], kps[b][:, 0:16], cvb[0][:, 0:16], op=ALU.mult),
                               [h4, ha], signal=False)
                        h5 = DVE.op(lambda e, b=b, cc=cc: e.tensor_tensor(uT[:, cc, 16 + NT:32 + NT], kps[b][:, 16:32], cvb[0][:, 16:32], op=ALU.mult), [])
                        st_["cva_free"][0] = h5
                        st_["cvb_free"][0] = h5
                        st_["kps_free"][b] = h5
                        hl = h5
                    return hl
                h_u_halo = conv_in_halo()
                h_A1_done = [h_u_halo, h_u_last, h_q_last] + h_kv_last
                for E_ in (PE, ACT, DVE, POOL, SP):
                    E_.wait(h_A1_done)
                    E_.wait([st_["kps_free"], st_["vps_free"], st_["rows_free"], st_["RE_free"], st_["ptr_free"], st_["st_free"]])
                print("SBUF remaining (A1):", nc.sbuf_bytes_remaining)
                flush()

            esA2 = ExitStack()
            with esA2:
                Dg = sb("Dg", [128, 4, CK, 128], BF16, esA2)
                yv2 = [sb(f"yv{i}", [128, 4, 512], F32, esA2) for i in range(2)]
                ysq2 = [sb(f"ysq{i}", [128, 4, 512], F32, esA2) for i in range(2)]
                mean_sb = sb("mean_sb", [128, 512], F32, esA2)
                m2 = sb("m2", [128, 512], F32, esA2)
                rs_ln = sb("rs_ln", [128, 512], F32, esA2)
                z1 = [sb(f"z1_{i}", [128, 512], F32, esA2) for i in range(2)]
                z2 = [sb(f"z2_{i}", [128, 512], F32, esA2) for i in range(2)]
                hg4 = sb("hg4", [128, 4], F32, esA2)
                hb4 = sb("hb4", [128, 4], F32, esA2)
                onesf512 = sb("onesf512", [128, 128], F32, esA2)
                yps = [ps(f"yps{i}", [128, 512], F32, esA2) for i in range(2)]
                mps = ps("mps", [128, 512], F32, esA2)
                eps_ = ps("eps_", [128, 512], F32, esA2)

                hd = None
                for cc in range(4):
                    for k in range(CK):
                        hd = DVE.op(lambda e, cc=cc, k=k: e.tensor_scalar(Dg[:, cc, k, :], identf[:, :], cwT[:, cc, k:k + 1], None, op0=ALU.mult),
                                    [h_cwT, h_A1_done] if (cc == 0 and k == 0) else [], signal=(k == CK - 1))
                h_Dg = hd
                h_a = DVE.op(lambda e: e.tensor_scalar(hg4[:], lg4[:], 0.5, None, op0=ALU.mult), [h_const])
                h_b = DVE.op(lambda e: e.tensor_scalar(hb4[:], lb4[:], 0.5, None, op0=ALU.mult), [h_const])
                h_o5 = POOL.op(lambda e: e.memset(onesf512[:], 1.0 / 512.0), [h_A1_done])
                yps_free = [None, None]
                yv_free = [None, None]
                stat_free = None
                z_free = [None, None]
                h_conv_last = None
                n_y = [0]
                hy_of = {}

                def conv_part(g):
                    yb = g % 2
                    yv, ysq = yv2[yb], ysq2[yb]
                    h_y = []
                    for cc in range(4):
                        b = n_y[0] % 2; n_y[0] += 1
                        hm = None
                        for k in range(CK):
                            hm = PE.op(lambda e, cc=cc, k=k, b=b, g=g: e.matmul(yps[b][:, :], lhsT=Dg[:, cc, k, :],
                                                                                rhs=uT[:, cc, g * 512 + k + 1: g * 512 + k + 1 + 512],
                                                                                start=(k == 0), stop=(k == CK - 1)),
                                       [h_Dg, yps_free[b]] if k == 0 else [], signal=(k == CK - 1))
                        hy = DVE.op(lambda e, cc=cc, b=b, yv=yv: e.tensor_scalar(yv[:, cc, :], yps[b][:, :], cb4[:, cc:cc + 1], None, op0=ALU.add),
                                    [hm, yv_free[yb] if cc == 0 else None])
                        yps_free[b] = hy
                        hs = ACT.op(lambda e, cc=cc, yv=yv, ysq=ysq: e.activation(ysq[:, cc, :], yv[:, cc, :], AF.Square), [hy])
                        h_y.append((hy, hs))
                    hy_of[g] = h_y

                def ln_part(g):
                    nonlocal stat_free, h_conv_last
                    yb = g % 2
                    yv, ysq = yv2[yb], ysq2[yb]
                    h_y = hy_of[g]
                    hmean = None
                    for cc in range(4):
                        hmean = PE.op(lambda e, cc=cc, yv=yv: e.matmul(mps[:, :], lhsT=onesf512[:, :], rhs=yv[:, cc, :], start=(cc == 0), stop=(cc == 3)),
                                      [h_y[cc][0], h_o5, stat_free], signal=(cc == 3))
                    hey = None
                    for cc in range(4):
                        hey = PE.op(lambda e, cc=cc, ysq=ysq: e.matmul(eps_[:, :], lhsT=onesf512[:, :], rhs=ysq[:, cc, :], start=(cc == 0), stop=(cc == 3)),
                                    [h_y[cc][1]], signal=(cc == 3))
                    h1 = ACT.op(lambda e: e.activation(mean_sb[:, :], mps[:, :], AF.Copy), [hmean, z_free])
                    h2 = DVE.op(lambda e: e.tensor_tensor(m2[:, :], mean_sb[:, :], mean_sb[:, :], op=ALU.mult), [h1, stat_free])
                    h3 = DVE.op(lambda e: e.tensor_tensor(m2[:, :], eps_[:, :], m2[:, :], op=ALU.subtract), [h2, hey])
                    h4 = DVE.op(lambda e: e.tensor_scalar(m2[:, :], m2[:, :], EPS, None, op0=ALU.add), [h3])
                    h5 = rsqrt_act(rs_ln[:, :], m2[:, :], [h4])
                    hl = None
                    for cc in range(4):
                        b = cc % 2
                        a1 = DVE.op(lambda e, cc=cc, b=b, yv=yv: e.tensor_tensor(z1[b][:, :], yv[:, cc, :], mean_sb[:, :], op=ALU.subtract), [h1, z_free[b]])
                        a2 = DVE.op(lambda e, b=b: e.tensor_tensor(z1[b][:, :], z1[b][:, :], rs_ln[:, :], op=ALU.mult), [a1, h5])
                        a3 = DVE.op(lambda e, cc=cc, b=b: e.tensor_scalar(z1[b][:, :], z1[b][:, :], hg4[:, cc:cc + 1], hb4[:, cc:cc + 1],
                                                                          op0=ALU.mult, op1=ALU.add), [a2, h_a, h_b])
                        a4 = ACT.op(lambda e, b=b: e.activation(z2[b][:, :], z1[b][:, :], AF.Tanh), [a3])
                        a5 = DVE.op(lambda e, cc=cc, b=b, g=g: e.scalar_tensor_tensor(conv_o[:, cc, g * 512:(g + 1) * 512], z2[b][:, :], 1.0, z1[b][:, :],
                                                                                      op0=ALU.add, op1=ALU.mult), [a4])
                        z_free[b] = a5
                        hl = a5
                    yv_free[yb] = hl
                    stat_free = h5
                    h_conv_last = hl

                conv_part(0)
                for g in range(NG):
                    if g + 1 < NG:
                        conv_part(g + 1)
                    ln_part(g)
                for E_ in (PE, ACT, DVE, POOL, SP):
                    E_.wait([h_conv_last])
                print("SBUF remaining (A2):", nc.sbuf_bytes_remaining)
                flush()

            esB = ExitStack()
            with esB:
                K1 = [sb(f"K1_{i}", [128, CHK * 128], BF16, esB) for i in range(NSLOT)]
                K2 = [sb(f"K2_{i}", [128, CHK * 128], BF16, esB) for i in range(NSLOT)]
                Vs = [sb(f"Vs_{i}", [128, CHK * 128], BF16, esB) for i in range(NSLOT)]
                PT = [sb(f"PT_{i}", [128, 1024], BF16, esB) for i in range(NPT)]
                dtmp = [sb(f"dtmp{i}", [128, 1024], F32, esB) for i in range(2)]
                bd = sb("bd", [128, NH, 896], F32, esB)
                oT = sb("oT", [128, NH, NT], F32, esB)
                cO = [sb(f"cO{i}", [128, 512], F32, esB) for i in range(2)]
                cl = [sb(f"cl{i}", [128, 512], F32, esB) for i in range(2)]
                accD2 = [sb(f"accD{i}", [128, RSD], F32, esB) for i in range(2)]
                Sps = [ps(f"Sps{i}", [128, 1024], F32, esB) for i in range(2)]
                Ops = [ps(f"Ops{i}", [128, 512], F32, esB) for i in range(2)]
                Lps = [ps(f"Lps{i}", [128, 512], F32, esB) for i in range(2)]

                ds_cv = dsem("d_cv")
                h_cv = None
                _pl = [("wo", c) for c in range(NCH)]
                for f in range(NF):
                    _pl.append(("wg", f)); _pl.append(("wu", f))
                _pl += [("wd", f) for f in range(NF)] + [("pg", c) for c in range(NCH)] + [("pp", c) for c in range(2)]
                for pi_, (kind, idx) in enumerate(_pl):
                    if kind == "wo":
                        src, dst = w_out[idx * 128:(idx + 1) * 128, :], wsc[pi_]
                    elif kind in ("wg", "wu"):
                        wsrc = w_gate if kind == "wg" else w_up
                        src = wsrc[:, idx * 128:(idx + 1) * 128].rearrange("(c p) n -> p c n", p=128)
                        dst = wsc[pi_].rearrange("p (c n) -> p c n", c=NCH)
                    elif kind == "wd":
                        src, dst = w_down[idx * 128:(idx + 1) * 128, :], wsc[pi_]
                    elif kind == "pg":
                        src, dst = w_pg[idx * 128:(idx + 1) * 128, :], wsc[pi_]
                    else:
                        src, dst = w_pp[idx * 128:(idx + 1) * 128, :], wsc[pi_]
                    h_cv = POOL.dma(dst, src, ds_cv)

                ds_bd = dsem("d_bd")
                h_bd = None
                for hh in range(NH):
                    h_bd = SP.dma(bd[:, hh, :], c_bdiag[hh], ds_bd, deps=[h_conv_last])
                hz = [h_conv_last for i in range(NSLOT)]
                ds_slot = [dsem(f"d_slot{i}") for i in range(NSLOT)]
                slot_free = [hz[i] for i in range(NSLOT)]
                slot_ld = [None] * NSLOT
                pt_free = [None] * NPT
                s_free = [None, None]
                dt_free = [None, None]
                ol_free = None
                c_free = None
                acc_free = [None, None]
                h_kv_ready = h_kv_last
                gstep = 0
                units = [(hh, g) for hh in range(NH) for g in range(NG)]

                def unit_blocks(hh, g):
                    wb = WIN_BLOCKS[hh]
                    if wb is None or 4 + 2 * wb >= NKB:
                        return [list(range(NKB))]
                    lo, hi = 4 * g - wb, 4 * g + 3 + wb
                    if lo < 0:
                        return [list(range(NKB + lo, NKB)), list(range(0, hi + 1))]
                    if hi >= NKB:
                        return [list(range(lo, NKB)), list(range(0, hi - NKB + 1))]
                    return [list(range(lo, hi + 1))]

                fills = []
                unit_steps = []
                for u, (hh, g) in enumerate(units):
                    stl = []
                    for run in unit_blocks(hh, g):
                        for a in range(0, len(run), CHK):
                            piece = run[a:a + CHK]
                            fi = len(fills)
                            fills.append((u, piece[0], len(piece)))
                            for kb, j in enumerate(piece):
                                stl.append((fi, kb, j, kb == len(piece) - 1))
                    unit_steps.append(stl)

                def issue_fill(fi):
                    u, j0, nkb = fills[fi]
                    hh, g = units[u]
                    s = fi % NSLOT
                    deps = [slot_free[s], h_kv_ready]
                    c0 = j0 * 128
                    wdt = nkb * 128
                    SP.dma(K1[s][0:64, 0:wdt], kT1[hh, :, c0:c0 + wdt], ds_slot[s], deps=deps)
                    SP.dma(K2[s][0:64, 0:wdt], kT2[hh, :, c0:c0 + wdt], ds_slot[s])
                    SP.dma(K1[s][64:68, 0:wdt], c_kaug[hh, g, :, c0:c0 + wdt], ds_slot[s])
                    SP.dma(K2[s][64:68, 0:wdt], c_kaug[hh, g, :, c0:c0 + wdt], ds_slot[s])
                    slot_ld[s] = SP.dma(Vs[s][:, 0:wdt], vdr[hh, :, j0:j0 + nkb, :].rearrange("p k e -> p (k e)"), ds_slot[s])

                nissued = 0
                for _ in range(min(NSLOT, len(fills))):
                    issue_fill(nissued); nissued += 1

                pendq = []
                dve_q = []
                epi_state = dict(c_free=None, last=None)

                def emit_av(pd):
                    nonlocal ol_free
                    (s, kb, pslot, h_p, last_in_fill, first, last, h_rs, uinfo) = pd
                    deps = [h_p, ol_free if first else None]
                    PE.op(lambda e: e.matmul(Ops[0][:, :], lhsT=Vs[s][:, kb * 128:(kb + 1) * 128], rhs=PT[pslot][:, 0:512], start=first, stop=last),
                          deps, signal=False)
                    PE.op(lambda e: e.matmul(Ops[1][:, :], lhsT=Vs[s][:, kb * 128:(kb + 1) * 128], rhs=PT[pslot][:, 512:1024], start=first, stop=last), [], signal=False)
                    hv = PE.op(lambda e: e.matmul(Lps[1][:, RSD - 512:512], lhsT=onesb[:, :], rhs=PT[pslot][:, RSD:1024], start=first, stop=last),
                               [epi_state.get("l_free") if first else None])
                    pt_free[pslot] = [hv] + h_rs
                    if last_in_fill:
                        slot_free[s] = hv
                    if last:
                        epilogue(hv, h_rs[0], uinfo)
                    return hv

                def epilogue(hv, r1, uinfo):
                    nonlocal ol_free
                    hh, g, ab = uinfo
                    accD = accD2[ab]
                    PE.op(lambda e: e.matmul(Lps[0][:, :], lhsT=onesf[:, :], rhs=accD[:, 0:512], start=True, stop=True), [r1, epi_state.get("l_free")], signal=False)
                    hl_ = PE.op(lambda e: e.matmul(Lps[1][:, 0:RSD - 512], lhsT=onesf[:, :], rhs=accD[:, 512:RSD], start=True, stop=True), [])
                    acc_free[ab] = hl_
                    e1 = DVE.op(lambda e: e.tensor_copy(cO[0][:, :], Ops[0][:, :]), [hv, epi_state["c_free"]])
                    e3 = DVE.op(lambda e: e.tensor_copy(cO[1][:, :], Ops[1][:, :]), [])
                    e4 = DVE.op(lambda e: e.tensor_copy(cl[1][:, :], Lps[1][:, :]), [hl_])
                    e2 = DVE.op(lambda e: e.tensor_copy(cl[0][:, :], Lps[0][:, :]), [])
                    ol_free = e3
                    epi_state["l_free"] = e2
                    box = dict(h=e2)
                    dve_q.extend([None, None, None])

                    def mk(eng, fn, extra=None):
                        def run():
                            box["h"] = eng.op(fn, [box["h"], extra])
                            epi_state["last"] = box["h"]
                        return run
                    dve_q.append(mk(ACT, lambda e: e.activation(cl[0][:, :], cl[0][:, :], AF.Ln)))
                    dve_q.append(mk(ACT, lambda e: e.activation(cl[0][:, :], cl[0][:, :], AF.Exp, scale=-1.0)))
                    dve_q.append(mk(ACT, lambda e: e.activation(cl[1][:, :], cl[1][:, :], AF.Ln)))
                    dve_q.append(mk(ACT, lambda e: e.activation(cl[1][:, :], cl[1][:, :], AF.Exp, scale=-1.0)))
                    dve_q.append(mk(DVE, lambda e: e.tensor_tensor(cO[0][:, :], cO[0][:, :], cl[0][:, :], op=ALU.mult)))
                    dve_q.append(mk(DVE, lambda e: e.tensor_tensor(cO[1][:, :], cO[1][:, :], cl[1][:, :], op=ALU.mult)))

                    def fin():
                        box["h"] = DVE.op(lambda e, hh=hh, g=g: e.scalar_tensor_tensor(oT[:, hh, g * 512:(g + 1) * 512], cO[1][:, :], neglam[:, 0:1], cO[0][:, :],
                                                                                     op0=ALU.mult, op1=ALU.add), [box["h"], h_lam])
                        epi_state["c_free"] = box["h"]
                        epi_state["last"] = box["h"]
                    dve_q.append(fin)

                def drain_dve(n):
                    for _ in range(n):
                        if dve_q:
                            f_ = dve_q.pop(0)
                            if f_ is not None:
                                f_()

                for u, (hh, g) in enumerate(units):
                    stl = unit_steps[u]
                    nst = len(stl)
                    for si, (fi, kb, j, last_in_fill) in enumerate(stl):
                        s = fi % NSLOT
                        sbk = gstep % 2; pslot = gstep % NPT
                        diag = (4 * g <= j < 4 * g + 4)
                        first = (si == 0); last = (si == nst - 1)
                        if first:
                            drain_dve(100)
                        PE.op(lambda e, s=s, kb=kb, sbk=sbk, hh=hh, g=g: e.matmul(Sps[sbk][:, 0:512], lhsT=K1[s][0:68, kb * 128:(kb + 1) * 128],
                                                                              rhs=Q1[0:68, hh, g * 512:(g + 1) * 512], start=True, stop=True),
                              [slot_ld[s], s_free[sbk], h_q_last, h_qaug], signal=False)
                        h_s = PE.op(lambda e, s=s, kb=kb, sbk=sbk, hh=hh, g=g: e.matmul(Sps[sbk][:, 512:1024], lhsT=K2[s][0:68, kb * 128:(kb + 1) * 128],
                                                                                    rhs=Q2[0:68, hh, g * 512:(g + 1) * 512], start=True, stop=True), [])
                        if len(pendq) >= 2:
                            pd = pendq.pop(0)
                            emit_av(pd)
                            if pd[4] and nissued < len(fills):
                                issue_fill(nissued); nissued += 1
                        if diag:
                            x0 = 384 - 128 * (j - 4 * g)
                            DVE.op(lambda e, sbk=sbk, hh=hh, x0=x0: e.tensor_tensor(dtmp[sbk][:, 0:512], Sps[sbk][:, 0:512], bd[:, hh, x0:x0 + 512], op=ALU.add),
                                   [h_s, dt_free[sbk], h_bd], signal=False)
                            h_d = DVE.op(lambda e, sbk=sbk, hh=hh, x0=x0: e.tensor_tensor(dtmp[sbk][:, 512:1024], Sps[sbk][:, 512:1024], bd[:, hh, x0:x0 + 512], op=ALU.add), [])
                            h_p = ACT.op(lambda e, sbk=sbk, pslot=pslot: e.activation(PT[pslot][:, :], dtmp[sbk][:, :], AF.Exp, scale=0.125),
                                         [h_d, pt_free[pslot]])
                            dt_free[sbk] = h_p
                        else:
                            h_p = ACT.op(lambda e, sbk=sbk, pslot=pslot: e.activation(PT[pslot][:, :], Sps[sbk][:, :], AF.Exp, scale=0.125),
                                         [h_s, pt_free[pslot]])
                        s_free[sbk] = h_p
                        ab = u % 2
                        accD = accD2[ab]
                        if first:
                            r1 = DVE.op(lambda e, pslot=pslot, accD=accD: e.tensor_copy(accD[:, :], PT[pslot][:, 0:RSD]), [h_p, acc_free[ab]])
                        else:
                            r1 = DVE.op(lambda e, pslot=pslot, accD=accD: e.tensor_tensor(accD[:, :], accD[:, :], PT[pslot][:, 0:RSD], op=ALU.add), [h_p])
                        drain_dve(1)
                        pendq.append((s, kb, pslot, h_p, last_in_fill, first, last, [r1], (hh, g, ab)))
                        gstep += 1
                while pendq:
                    pd = pendq.pop(0)
                    emit_av(pd)
                    if pd[4] and nissued < len(fills):
                        issue_fill(nissued); nissued += 1
                drain_dve(100)
                c_free = epi_state["c_free"]
                h_attn = c_free
                for E_ in (PE, ACT, DVE, POOL, SP):
                    E_.wait([h_attn])

                hsq = sb("hsq", [128, 512], F32, esB)
                hrs = sb("hrs", [128, 512], F32, esB)
                hfree = None
                hl = None
                for hh in range(NH):
                    for g in range(NG):
                        b = (hh * NG + g) % 2
                        a1 = ACT.op(lambda e, hh=hh, g=g: e.activation(hsq[:, :], oT[:, hh, g * 512:(g + 1) * 512], AF.Square), [h_attn, hfree])
                        a2 = PE.op(lambda e, b=b: e.matmul(Ops[b][:, :], lhsT=onesf[:, :], rhs=hsq[:, :], start=True, stop=True), [a1])
                        a3 = DVE.op(lambda e, b=b: e.tensor_scalar(hrs[:, :], Ops[b][:, :], 1.0 / 128.0, EPS, op0=ALU.mult, op1=ALU.add), [a2])
                        a4 = rsqrt_act(hrs[:, :], hrs[:, :], [a3])
                        a5 = DVE.op(lambda e, hh=hh, g=g: e.scalar_tensor_tensor(attn_o[:, hh, g * 512:(g + 1) * 512], oT[:, hh, g * 512:(g + 1) * 512],
                                                                                 hn8[:, 0:1], hrs[:, :], op0=ALU.mult, op1=ALU.mult), [a4, h_hn])
                        hfree = a5
                        hl = a5
                h_attn_o = hl
                for E_ in (PE, ACT, DVE, POOL, SP):
                    E_.wait([h_attn_o])
                print("SBUF remaining (B):", nc.sbuf_bytes_remaining)
                flush()

        esC = ExitStack()
        with esC:
            WS = [sb(f"WS{i}", [128, 1024], BF16, esC) for i in range(NWS)]
            x1 = sb("x1", [128, 4, D], F32, esC)
            xn = [sb(f"xn{i}", [128, D], BF16, esC) for i in range(2)]
            hT = sb("hT", [128, NCH, 512], BF16, esC)
            hid = sb("hid", [128, NF, 512], BF16, esC)
            thb = [sb(f"thb{i}", [128, 512], F32, esC) for i in range(2)]
            t2b = [sb(f"t2b{i}", [128, 512], F32, esC) for i in range(2)]
            sgt = sb("sgt", [128, 4, D], F32, esC)
            pld = [sb(f"pld{i}", [128, PLE], F32, esC) for i in range(2)]
            pbf = [sb(f"pbf{i}", [128, PLE], BF16, esC) for i in range(2)]
            pT = sb("pT", [128, 2, 512], BF16, esC)
            stc = [sb(f"stc{i}", [128, 4], F32, esC) for i in range(4)]
            junkc = sb("junkc", [128, D], BF16, esC)
            xin = [sb(f"xin{i}", [128, D], F32, esC) for i in range(2)]
            B = [ps(f"B{i}", [128, 512], F32, esC) for i in range(8)]

            ds_ws = [dsem(f"d_ws{i}") for i in range(NWS)]
            ds_xin = [dsem(f"d_xin{i}") for i in range(2)]
            ds_pl = [dsem(f"d_pl{i}") for i in range(2)]
            ds_out = dsem("d_out")
            ws_free = [h_attn_o] * NWS

            def piece_list():
                L = []
                for c in range(NCH):
                    L.append(("wo", c))
                for f in range(NF):
                    L.append(("wg", f)); L.append(("wu", f))
                for f in range(NF):
                    L.append(("wd", f))
                for c in range(NCH):
                    L.append(("pg", c))
                for c in range(2):
                    L.append(("pp", c))
                return L
            PL = piece_list()
            NP = len(PL)
            allp = [(g, i) for g in range(NG) for i in range(NP)]
            piece_h = {}
            npi = [0]

            def issue_piece():
                if npi[0] >= len(allp):
                    return
                g, i = allp[npi[0]]
                s = npi[0] % NWS
                piece_h[(g, i)] = (s, SP.dma(WS[s][:, :], wsc[i], ds_ws[s], deps=[ws_free[s], h_cv]))
                npi[0] += 1

            for _ in range(NWS):
                issue_piece()

            def use_piece(g, i):
                return piece_h[(g, i)]

            def done_piece(g, i, h):
                s, _ = piece_h[(g, i)]
                ws_free[s] = h
                issue_piece()

            stg = [sb(f"stg{i}", [128, 3, 4], F32, esC) for i in range(2)]
            stg_free = [None, None]
            n_rg = [0]

            def rms_group(gvec, tile_deps, hT_dep):
                sg = n_rg[0] % 2; n_rg[0] += 1
                h1s = []
                for i in range(4):
                    h1s.append(ACT.op(lambda e, i=i: e.activation(junkc[:, :], x1[:, i, :], AF.Square, accum_out=stg[sg][:, 0, i:i + 1]),
                                      [tile_deps[i], stg_free[sg] if i == 0 else None]))
                h2 = DVE.op(lambda e: e.tensor_scalar(stg[sg][:, 1, :], stg[sg][:, 0, :], 1.0 / D, EPS, op0=ALU.mult, op1=ALU.add), h1s)
                h3 = rsqrt_act(stg[sg][:, 2, :], stg[sg][:, 1, :], [h2])
                h5 = None
                h4s = {}

                def do_xn(i):
                    b = i % 2
                    h4s[i] = DVE.op(lambda e, i=i, b=b: e.tensor_scalar(xn[b][:, :], x1[:, i, :], stg[sg][:, 2, i:i + 1], None, op0=ALU.mult),
                                    [h3, xn_free[b], tile_deps[i]])
                do_xn(0)
                for i in range(4):
                    b = i % 2
                    bk = 6 + b
                    h4 = h4s[i]
                    hp = None
                    for c in range(NCH):
                        hp = PE.op(lambda e, c=c, b=b, bk=bk: e.transpose(B[bk][:, :].bitcast(BF16)[:, c * 128:(c + 1) * 128], xn[b][:, c * 128:(c + 1) * 128], identb[:, :]),
                                   [h4, bank_free[bk]] if c == 0 else [], signal=(c == NCH - 1))
                    xn_free[b] = hp
                    if i + 1 < 4:
                        do_xn(i + 1)
                    h5 = DVE.op(lambda e, i=i, bk=bk: e.tensor_tensor(hT[:, :, i * 128:(i + 1) * 128],
                                                                     B[bk][:, :].bitcast(BF16).rearrange("p (c t) -> p c t", c=NCH),
                                                                     gvec[:, :].unsqueeze(2).to_broadcast([128, NCH, 128]), op=ALU.mult),
                                [hp, hT_dep if i == 0 else None])
                    bank_free[bk] = h5
                stg_free[sg] = h5
                return h5

            xn_free = [None, None]
            bank_free = [None] * 8
            ptr_free = None
            xin_free = [None, None]
            pld_free = [None, None]
            pbf_free = [None, None]
            hT_free = None
            hid_free = None
            x1_free = None
            sgt_free = None
            pT_free = None
            n_r = 0
            nx = 0
            npl = 0
            h_out_last = None
            for g in range(NG):
                wo = [use_piece(g, c) for c in range(NCH)]
                h_hT = None
                h_x1 = []
                for i in range(4):
                    xs = nx % 2; nx += 1
                    h_xl = SP.dma(xin[xs][:, :], xr[g * 512 + i * 128: g * 512 + (i + 1) * 128, :], ds_xin[xs], deps=[xin_free[xs]])
                    for half in range(2):
                        bk = (i % 2) * 2 + half
                        hm = None
                        for c in range(NCH):
                            lhs = attn_o[:, c, g * 512 + i * 128: g * 512 + (i + 1) * 128] if c < 4 else conv_o[:, c - 4, g * 512 + i * 128: g * 512 + (i + 1) * 128]
                            hm = PE.op(lambda e, lhs=lhs, c=c, bk=bk, half=half: e.matmul(B[bk][:, :], lhsT=lhs, rhs=WS[wo[c][0]][:, half * 512:(half + 1) * 512],
                                                                                         start=(c == 0), stop=(c == NCH - 1)),
                                       [wo[c][1], bank_free[bk], h_attn_o] if True else [], signal=(c == NCH - 1))
                        ha = DVE.op(lambda e, i=i, bk=bk, half=half, xs=xs: e.tensor_tensor(x1[:, i, half * 512:(half + 1) * 512], B[bk][:, :],
                                                                                         xin[xs][:, half * 512:(half + 1) * 512], op=ALU.add),
                                    [hm, h_xl, x1_free if (i == 0 and half == 0) else None])
                        bank_free[bk] = ha
                    xin_free[xs] = ha
                    if i == 3:
                        for c in range(NCH):
                            done_piece(g, c, hm)
                    h_x1.append(ha)
                h_pT = None
                for i in range(4):
                    pb = npl % 2; npl += 1
                    h_pl = SP.dma(pld[pb][:, :], p_own[g * 512 + i * 128: g * 512 + (i + 1) * 128, :], ds_pl[pb], deps=[pld_free[pb]])
                    h_pc = DVE.op(lambda e, pb=pb: e.tensor_copy(pbf[pb][:, :], pld[pb][:, :]), [h_pl, pbf_free[pb]])
                    pld_free[pb] = h_pc
                    hp = None
                    for c in range(2):
                        hp = PE.op(lambda e, c=c, pb=pb: e.transpose(B[5][:, :].bitcast(BF16)[:, c * 128:(c + 1) * 128], pbf[pb][:, c * 128:(c + 1) * 128], identb[:, :]),
                                   [h_pc, bank_free[5]] if c == 0 else [], signal=(c == 1))
                    pbf_free[pb] = hp
                    h_pT = DVE.op(lambda e, i=i: e.tensor_copy(pT[:, :, i * 128:(i + 1) * 128],
                                                               B[5][:, :].bitcast(BF16)[:, 0:256].rearrange("p (c t) -> p c t", c=2)), [hp, pT_free if i == 0 else None])
                    bank_free[5] = h_pT
                h_hT = rms_group(gFn, h_x1, hT_free)
                hid_w = None
                for f in range(NF):
                    ig, iu = NCH + 2 * f, NCH + 2 * f + 1
                    sg_, hgp = use_piece(g, ig)
                    su_, hup = use_piece(g, iu)
                    bg = (f % 2) * 2; bu = bg + 1
                    hm1 = hm2 = None
                    for c in range(NCH):
                        hm1 = PE.op(lambda e, c=c, sg_=sg_, bg=bg: e.matmul(B[bg][:, :], lhsT=WS[sg_][:, c * 128:(c + 1) * 128], rhs=hT[:, c, :],
                                                                           start=(c == 0), stop=(c == NCH - 1)),
                                    [hgp, h_hT, bank_free[bg]] if c == 0 else [], signal=(c == NCH - 1))
                    for c in range(NCH):
                        hm2 = PE.op(lambda e, c=c, su_=su_, bu=bu: e.matmul(B[bu][:, :], lhsT=WS[su_][:, c * 128:(c + 1) * 128], rhs=hT[:, c, :],
                                                                           start=(c == 0), stop=(c == NCH - 1)),
                                    [hup, bank_free[bu]] if c == 0 else [], signal=(c == NCH - 1))
                    done_piece(g, ig, hm1)
                    done_piece(g, iu, hm2)
                    tb = f % 2
                    a1 = ACT.op(lambda e, tb=tb, bg=bg: e.activation(thb[tb][:, :], B[bg][:, :], AF.Tanh, scale=0.5), [hm1])
                    a2 = DVE.op(lambda e, tb=tb, bg=bg: e.scalar_tensor_tensor(t2b[tb][:, :], thb[tb][:, :], 1.0, B[bg][:, :], op0=ALU.add, op1=ALU.mult), [a1])
                    bank_free[bg] = a2
                    a3 = DVE.op(lambda e, tb=tb, bu=bu, f=f: e.scalar_tensor_tensor(hid[:, f, :], t2b[tb][:, :], 0.5, B[bu][:, :], op0=ALU.mult, op1=ALU.mult),
                                [a2, hm2, hid_free if f == 0 else None])
                    bank_free[bu] = a3
                    hid_w = a3
                hT_free = hm2
                hm = None
                for f in range(NF):
                    ip = NCH + 2 * NF + f
                    sd_, hdp = use_piece(g, ip)
                    for i in range(4):
                        for half in range(2):
                            bk = i * 2 + half
                            hm = PE.op(lambda e, f=f, i=i, half=half, bk=bk, sd_=sd_: e.matmul(B[bk][:, :], lhsT=hid[:, f, i * 128:(i + 1) * 128],
                                                                                              rhs=WS[sd_][:, half * 512:(half + 1) * 512],
                                                                                              start=(f == 0), stop=(f == NF - 1)),
                                       [hdp, hid_w, bank_free[bk]] if f == 0 else ([hdp] if (i == 0 and half == 0) else []),
                                       signal=(i == 3 and half == 1))
                    done_piece(g, ip, hm)
                hid_free = hm
                h_x2 = []
                for i in range(4):
                    for half in range(2):
                        bk = i * 2 + half
                        ha = DVE.op(lambda e, i=i, half=half, bk=bk: e.tensor_tensor(x1[:, i, half * 512:(half + 1) * 512], B[bk][:, :],
                                                                                   x1[:, i, half * 512:(half + 1) * 512], op=ALU.add), [hm])
                        bank_free[bk] = ha
                    h_x2.append(ha)
                h_hT = rms_group(gPn, h_x2, hT_free)
                base = NCH + 3 * NF
                pgp = [use_piece(g, base + c) for c in range(NCH)]
                ppp = [use_piece(g, base + NCH + c) for c in range(2)]
                hm = None
                hh2 = None
                for i in range(4):
                    for half in range(2):
                        n_ = i * 2 + half
                        bkA = (n_ % 4) * 2
                        bkB = bkA + 1
                        hmA = None
                        for c in range(NCH):
                            hmA = PE.op(lambda e, c=c, i=i, half=half, bkA=bkA: e.matmul(B[bkA][:, :], lhsT=hT[:, c, i * 128:(i + 1) * 128],
                                                                                       rhs=WS[pgp[c][0]][:, half * 512:(half + 1) * 512],
                                                                                       start=(c == 0), stop=(c == NCH - 1)),
                                        [pgp[c][1], h_hT, bank_free[bkA]], signal=(c == NCH - 1))
                        hmB = None
                        for c in range(2):
                            hmB = PE.op(lambda e, c=c, i=i, half=half, bkB=bkB: e.matmul(B[bkB][:, :], lhsT=pT[:, c, i * 128:(i + 1) * 128],
                                                                                       rhs=WS[ppp[c][0]][:, half * 512:(half + 1) * 512],
                                                                                       start=(c == 0), stop=(c == 1)),
                                        [ppp[c][1], h_pT, bank_free[bkB]], signal=(c == 1))
                        hm = hmB
                        a1 = ACT.op(lambda e, i=i, half=half, bkA=bkA: e.activation(sgt[:, i, half * 512:(half + 1) * 512], B[bkA][:, :], AF.Tanh, scale=0.5),
                                    [hmA, sgt_free if (i == 0 and half == 0) else None])
                        bank_free[bkA] = a1
                        b0 = DVE.op(lambda e, i=i, half=half: e.tensor_scalar(sgt[:, i, half * 512:(half + 1) * 512], sgt[:, i, half * 512:(half + 1) * 512],
                                                                             0.5, 0.5, op0=ALU.mult, op1=ALU.add), [a1])
                        b1 = DVE.op(lambda e, i=i, half=half, bkB=bkB: e.tensor_tensor(sgt[:, i, half * 512:(half + 1) * 512], sgt[:, i, half * 512:(half + 1) * 512],
                                                                                     B[bkB][:, :], op=ALU.mult), [b0, hmB])
                        bank_free[bkB] = b1
                        hh2 = DVE.op(lambda e, i=i, half=half: e.tensor_tensor(sgt[:, i, half * 512:(half + 1) * 512], sgt[:, i, half * 512:(half + 1) * 512],
                                                                             x1[:, i, half * 512:(half + 1) * 512], op=ALU.add), [b1])
                    h_out_last = SP.dma(out_own[g * 512 + i * 128: g * 512 + (i + 1) * 128, :], sgt[:, i, :], ds_out, deps=[hh2])
                for c in range(NCH + 2):
                    done_piece(g, base + c, hm)
                hT_free = hm
                pT_free = hm
                x1_free = hh2
                sgt_free = h_out_last
            for E_ in (SP, POOL, ACT, DVE, PE):
                E_.wait([h_out_last])
            print("SBUF remaining (C):", nc.sbuf_bytes_remaining)

            flush()
    return nc


def host_consts(NG, core):
    NT = NG * 512
    S = NCORES * NT
    t0 = core * NT
    bf = ml_dtypes.bfloat16
    identf = np.eye(128, dtype=np.float32)
    bones = np.zeros((128, 128), np.float32)
    bones[:64, :64] = 1.0
    bones[64:, 64:] = 1.0
    kpos = (np.arange(S) + t0) % S
    kaug = np.zeros((NH, NG, 4, S), np.float32)
    qaug = np.zeros((NH, 4, 512), np.float32)
    bdiag = np.zeros((NH, 128, 896), np.float32)
    qi = np.arange(512)
    for h in range(NH):
        s8 = 8.0 * SLOPES[h]
        qaug[h, 0] = 1.0
        qaug[h, 1] = 1.0
        qaug[h, 2] = -s8 * 128.0 * (qi // 128)
        qaug[h, 3] = -s8 * (qi % 128)
        ki = np.arange(128)[:, None]
        xx = np.arange(896)[None, :]
        bdiag[h] = -s8 * np.abs(xx - 384 - ki)
        for g in range(NG):
            q0 = t0 + g * 512
            sign = np.where(kpos < q0, 1.0, np.where(kpos >= q0 + 512, -1.0, 0.0))
            rel = kpos - q0
            blk = np.floor_divide(rel, 128)
            rr = rel - blk * 128
            kaug[h, g, 0] = sign * s8 * 128.0 * blk
            kaug[h, g, 1] = sign * s8 * rr
            kaug[h, g, 2] = sign
            kaug[h, g, 3] = sign
    for a in (kaug, qaug):
        assert np.array_equal(a.astype(bf).astype(np.float32), a)
    return dict(c_identb=identf.astype(bf), c_identf=identf, c_bones=bones.astype(bf),
                c_kaug=kaug.astype(bf), c_qaug=qaug.astype(bf), c_bdiag=bdiag)


_PROG_CACHE = {}


def run(inputs, NG):
    NT = NG * 512
    S = NCORES * NT
    x = np.ascontiguousarray(np.asarray(inputs["x"], np.float32).reshape(S, D))
    p = np.ascontiguousarray(np.asarray(inputs["p"], np.float32).reshape(S, PLE))
    if NG not in _PROG_CACHE:
        _PROG_CACHE[NG] = build_program(NG)
    nc = _PROG_CACHE[NG]
    sq = lambda k: np.ascontiguousarray(np.asarray(inputs[k], np.float32)[0])
    shared = {k: sq(k) for k in ("w_in", "w_out", "w_gate", "w_up", "w_down", "w_ple_gate", "w_ple_proj",
                                 "lambda_q1", "lambda_k1", "lambda_q2", "lambda_k2", "conv_w")}
    fm = lambda k: np.ascontiguousarray(sq(k).reshape(-1, 128).T)
    for k in ("attn_norm", "ffn_norm", "ple_norm", "conv_b", "conv_ln_g", "conv_ln_b"):
        shared[k] = fm(k)
    for k in ("q_norm", "k_norm"):
        shared[k] = np.ascontiguousarray(np.concatenate([sq(k), sq(k)]).reshape(128, 1))
    shared["head_norm"] = np.ascontiguousarray(sq("head_norm").reshape(128, 1))
    xpad = np.concatenate([np.zeros((16, D), np.float32), x, np.zeros((16, D), np.float32)], axis=0)
    in_maps = []
    for c in range(NCORES):
        t0 = c * NT
        m = dict(shared)
        m["xr"] = np.ascontiguousarray(np.roll(x, -t0, axis=0))
        m["xhalo"] = np.ascontiguousarray(np.concatenate([xpad[t0:t0 + 16], xpad[t0 + NT + 16:t0 + NT + 32]], axis=0))
        m["p_own"] = np.ascontiguousarray(p[t0:t0 + NT])
        m.update(host_consts(NG, c))
        in_maps.append(m)
    res = run_bass_kernel_spmd(nc, in_maps, core_ids=list(range(NCORES)))
    out = np.concatenate([np.asarray(r["out_own"], np.float32) for r in res.results], axis=0)
    return out.reshape(1, S, D)


def kernel(**inputs):
    return run(inputs, 4)
```
